# Optimizing a Trainium2 kernel written in Bass

```python
import math
import jax, jax.numpy as jnp
from jax import lax
import numpy as np

D_MODEL = 1024
BATCH = 16
SEQ = 4096
DEPTH = 1
DEC_BATCH = 128
DEC_SEQ = 8
PAST_LEN = 8192
PAGE_SIZE = 128

D_MIX = D_MODEL
ATT_HEADS = 8
HEAD_DIM = 64
ATT_WIDTH = ATT_HEADS * HEAD_DIM
KV_HEADS = 2
HPG = ATT_HEADS // KV_HEADS
KV_W = KV_HEADS * HEAD_DIM
GATE_W = 3 * ATT_HEADS
CMP_BLOCK = 32
CMP_STRIDE = 16
CMP_HIDDEN = 128
SEL_BLOCK = 64
N_SEL = 16
WINDOW = 512
SEL_QB = 64
WIN_QB = 128
SSM_WIDTH = D_MIX - ATT_WIDTH
SSM_HEADDIM = 64
SSM_HEADS = SSM_WIDTH // SSM_HEADDIM
SSM_GROUPS = 2
D_STATE = 128
CONV_W = 4
CONV_DIM = SSM_WIDTH + 2 * SSM_GROUPS * D_STATE
SSD_CHUNK = 128
D_FF = 2816
EPS = 1e-6
NEG = -1e30
FORCE = 1e9
SPLITS = (ATT_WIDTH, KV_W, KV_W, KV_W, KV_W, KV_W, KV_W, GATE_W, SSM_WIDTH, CONV_DIM, SSM_HEADS)
IN_W = sum(SPLITS)

kernel_name = 'nsa_mamba2_macaron_hybrid_step'


def rmsnorm(x, w):
    xf = x.astype(jnp.float32)
    y = xf * lax.rsqrt(jnp.mean(xf * xf, axis=-1, keepdims=True) + EPS)
    return (y * w.astype(jnp.float32)).astype(x.dtype)


def ffn_half(x, nw, wg, wu, wd):
    h = rmsnorm(x, nw)
    return x + 0.5 * ((jax.nn.silu(h @ wg) * (h @ wu)) @ wd)


def alibi_slopes():
    s = 2.0 ** (-8.0 * np.arange(1, ATT_HEADS + 1) / ATT_HEADS)
    return jnp.asarray(s, dtype=jnp.float32).reshape(KV_HEADS, HPG)


def masked_softmax(s, mask):
    s = jnp.where(mask, s.astype(jnp.float32), NEG)
    e = jnp.where(mask, jnp.exp(s - jnp.max(s, axis=-1, keepdims=True)), 0.0)
    return e / jnp.maximum(jnp.sum(e, axis=-1, keepdims=True), 1e-30)


def split_proj(u):
    cuts = [int(c) for c in np.cumsum(SPLITS)[:-1]]
    return jnp.split(u, cuts, axis=-1)


def compress(rows, pe, w1, w2):
    b, T = rows.shape[:2]
    nc = T // CMP_STRIDE
    ch = rows[:, :nc * CMP_STRIDE].reshape(b, nc, CMP_STRIDE, KV_HEADS, HEAD_DIM)
    blk = jnp.concatenate([ch[:, :-1], ch[:, 1:]], axis=2) + pe[None, None, :, None, :]
    blk = blk.transpose(0, 1, 3, 2, 4).reshape(b, nc - 1, KV_HEADS, CMP_BLOCK * HEAD_DIM)
    return jax.nn.silu(blk @ w1) @ w2


def cmp_attention(q, q_pos, kc, vc, slopes):
    b, Tq = q.shape[:2]
    nb = kc.shape[1]
    qg = q.reshape(b, Tq, KV_HEADS, HPG, HEAD_DIM)
    s = jnp.einsum('btghd,bngd->bghtn', qg, kc).astype(jnp.float32) / math.sqrt(HEAD_DIM)
    end = jnp.arange(nb) * CMP_STRIDE + CMP_BLOCK - 1
    dist = (q_pos[:, None] - end[None, :]).astype(jnp.float32)
    s = s - slopes[None, :, :, None, None] * dist
    p = masked_softmax(s, dist >= 0)
    o = jnp.einsum('bghtn,bngd->btghd', p.astype(vc.dtype), vc)
    return o.reshape(b, Tq, ATT_HEADS, HEAD_DIM), p


def select_blocks(p, q_pos, n_sel_blocks):
    nb = p.shape[-1]
    i = jnp.arange(nb)[:, None]
    j = jnp.arange(n_sel_blocks)[None, :]
    ratio = SEL_BLOCK // CMP_STRIDE
    overlap = ((i // ratio) == j).astype(jnp.float32) + (((i + 1) // ratio) == j).astype(jnp.float32)
    imp = jnp.einsum('bghtn,nj->bgtj', p, overlap)
    own = (q_pos // SEL_BLOCK)[:, None]
    forced = (j == own) | (j == 0)
    valid = j * SEL_BLOCK <= q_pos[:, None]
    imp = jnp.where(forced, FORCE, jnp.where(valid, imp, -1.0))
    _, idx = lax.top_k(imp, min(N_SEL, n_sel_blocks))
    return idx


def sel_chunk(q, q_pos, idx, kb, vb, slopes):
    b, Q = q.shape[:2]
    k = idx.shape[-1]
    bi = jnp.arange(b)[:, None, None, None]
    gi = jnp.arange(KV_HEADS)[None, :, None, None]
    kg = kb[bi, gi, idx]
    vg = vb[bi, gi, idx]
    qg = q.reshape(b, Q, KV_HEADS, HPG, HEAD_DIM)
    s = jnp.einsum('bqghd,bgqksd->bghqks', qg, kg).astype(jnp.float32) / math.sqrt(HEAD_DIM)
    pos = idx[..., None] * SEL_BLOCK + jnp.arange(SEL_BLOCK)
    dist = (q_pos[None, None, :, None, None] - pos).astype(jnp.float32)[:, :, None]
    s = s - slopes[None, :, :, None, None, None] * dist
    s = s.reshape(b, KV_HEADS, HPG, Q, k * SEL_BLOCK)
    mask = (dist >= 0).reshape(b, KV_HEADS, 1, Q, k * SEL_BLOCK)
    p = masked_softmax(s, mask)
    o = jnp.einsum('bghqn,bgqnd->bqghd', p.astype(vg.dtype), vg.reshape(b, KV_HEADS, Q, k * SEL_BLOCK, HEAD_DIM))
    return o.reshape(b, Q, ATT_HEADS, HEAD_DIM)


def sel_attention(q, q_pos, idx, k_rows, v_rows, slopes, qb):
    b, T = q.shape[:2]
    Tk = k_rows.shape[1]
    nsel = -(-Tk // SEL_BLOCK)
    pad = nsel * SEL_BLOCK - Tk

    def blocks(r):
        r = jnp.pad(r, ((0, 0), (0, pad), (0, 0), (0, 0)))
        return r.reshape(b, nsel, SEL_BLOCK, KV_HEADS, HEAD_DIM).transpose(0, 3, 1, 2, 4)

    kb, vb = blocks(k_rows), blocks(v_rows)
    nq = T // qb
    k = idx.shape[-1]
    qs = q.reshape(b, nq, qb, ATT_HEADS, HEAD_DIM).transpose(1, 0, 2, 3, 4)
    ps = q_pos.reshape(nq, qb)
    ids = idx.reshape(b, KV_HEADS, nq, qb, k).transpose(2, 0, 1, 3, 4)
    o = lax.map(lambda a: sel_chunk(a[0], a[1], a[2], kb, vb, slopes), (qs, ps, ids))
    return o.transpose(1, 0, 2, 3, 4).reshape(b, T, ATT_HEADS, HEAD_DIM)


def window_chunk(q, q_pos, k, v, k_pos, slopes):
    b, Q = q.shape[:2]
    qg = q.reshape(b, Q, KV_HEADS, HPG, HEAD_DIM)
    s = jnp.einsum('bqghd,bkgd->bghqk', qg, k).astype(jnp.float32) / math.sqrt(HEAD_DIM)
    dist = q_pos[:, None] - k_pos[None, :]
    mask = (dist >= 0) & (dist < WINDOW) & (k_pos >= 0)[None, :]
    s = s - slopes[None, :, :, None, None] * dist.astype(jnp.float32)
    p = masked_softmax(s, mask)
    o = jnp.einsum('bghqk,bkgd->bqghd', p.astype(v.dtype), v)
    return o.reshape(b, Q, ATT_HEADS, HEAD_DIM)


def window_attention_prompt(q, k, v, slopes):
    b, T = q.shape[:2]
    nq = T // WIN_QB
    kp = jnp.pad(k, ((0, 0), (WINDOW, 0), (0, 0), (0, 0)))
    vp = jnp.pad(v, ((0, 0), (WINDOW, 0), (0, 0), (0, 0)))

    def body(i):
        qs = lax.dynamic_slice_in_dim(q, i * WIN_QB, WIN_QB, axis=1)
        ks = lax.dynamic_slice_in_dim(kp, i * WIN_QB, WINDOW + WIN_QB, axis=1)
        vs = lax.dynamic_slice_in_dim(vp, i * WIN_QB, WINDOW + WIN_QB, axis=1)
        qpos = i * WIN_QB + jnp.arange(WIN_QB)
        kpos = i * WIN_QB - WINDOW + jnp.arange(WINDOW + WIN_QB)
        return window_chunk(qs, qpos, ks, vs, kpos, slopes)

    o = lax.map(body, jnp.arange(nq))
    return o.transpose(1, 0, 2, 3, 4).reshape(b, T, ATT_HEADS, HEAD_DIM)


def causal_conv(xbc, conv_state, w, bias):
    T = xbc.shape[1]
    xp = jnp.concatenate([conv_state.astype(xbc.dtype), xbc], axis=1)
    out = bias
    for tap in range(CONV_W):
        out = out + xp[:, tap:tap + T] * w[tap]
    return jax.nn.silu(out), xp[:, xp.shape[1] - (CONV_W - 1):]


def ssd(x, dt, A, Bm, Cm, h0):
    b, T = x.shape[:2]
    L = SSD_CHUNK if T % SSD_CHUNK == 0 else T
    nc = T // L
    rep = SSM_HEADS // SSM_GROUPS
    f32 = jnp.float32
    xc = x.astype(f32).reshape(b, nc, L, SSM_HEADS, SSM_HEADDIM)
    dtc = dt.reshape(b, nc, L, SSM_HEADS)
    Bh = jnp.repeat(Bm.astype(f32), rep, axis=2).reshape(b, nc, L, SSM_HEADS, D_STATE)
    Ch = jnp.repeat(Cm.astype(f32), rep, axis=2).reshape(b, nc, L, SSM_HEADS, D_STATE)
    acs = jnp.cumsum(dtc * A, axis=2)
    causal = jnp.tril(jnp.ones((L, L), dtype=bool))[:, :, None]
    seg = acs[:, :, :, None, :] - acs[:, :, None, :, :]
    decay = jnp.where(causal, jnp.exp(jnp.where(causal, seg, 0.0)), 0.0)
    xdt = xc * dtc[..., None]
    cb = jnp.einsum('bclhn,bcshn->bclsh', Ch, Bh)
    y_diag = jnp.einsum('bclsh,bcshp->bclhp', cb * decay, xdt)
    decay_end = jnp.exp(acs[:, :, -1:, :] - acs)
    states = jnp.einsum('bclhn,bclh,bclhp->bchpn', Bh, decay_end, xdt)
    chunk_decay = jnp.exp(acs[:, :, -1, :])

    def step(h, inp):
        st, cd = inp
        return h * cd[:, :, None, None] + st, h

    hT, h_prev = lax.scan(step, h0.astype(f32), (states.transpose(1, 0, 2, 3, 4), chunk_decay.transpose(1, 0, 2)))
    h_prev = h_prev.transpose(1, 0, 2, 3, 4)
    y_off = jnp.einsum('bclhn,bchpn,bclh->bclhp', Ch, h_prev, jnp.exp(acs))
    return (y_diag + y_off).reshape(b, T, SSM_HEADS, SSM_HEADDIM), hT


def mamba_mixer(z, xbc, dtr, conv0, h0, prm):
    b, T = z.shape[:2]
    f32 = jnp.float32
    xbc, conv_new = causal_conv(xbc, conv0, prm['conv_w'], prm['conv_b'])
    xs, bm, cm = jnp.split(xbc, [SSM_WIDTH, SSM_WIDTH + SSM_GROUPS * D_STATE], axis=-1)
    xs = xs.reshape(b, T, SSM_HEADS, SSM_HEADDIM)
    bm = bm.reshape(b, T, SSM_GROUPS, D_STATE)
    cm = cm.reshape(b, T, SSM_GROUPS, D_STATE)
    dt = jax.nn.softplus(dtr.astype(f32) + prm['dt_bias'].astype(f32))
    A = -jnp.exp(prm['a_log'].astype(f32))
    y, h_new = ssd(xs, dt, A, bm, cm, h0)
    y = y + prm['d_skip'].astype(f32)[:, None] * xs.astype(f32)
    y = y.reshape(b, T, SSM_WIDTH) * jax.nn.silu(z.astype(f32))
    yg = y.reshape(b, T, SSM_GROUPS, SSM_WIDTH // SSM_GROUPS)
    yg = yg * lax.rsqrt(jnp.mean(yg * yg, axis=-1, keepdims=True) + EPS)
    y = yg.reshape(b, T, SSM_WIDTH) * prm['ssm_norm'].astype(f32)
    return y.astype(z.dtype), conv_new, h_new


def nsa_mixer(q, q_pos, full_cmp, full_sel, o_win, gates, prm, slopes, sel_qb):
    b, T = q.shape[:2]
    kc = compress(full_cmp[:, :, 0], prm['cmp_pe_k'], prm['cmp_w1_k'], prm['cmp_w2_k'])
    vc = compress(full_cmp[:, :, 1], prm['cmp_pe_v'], prm['cmp_w1_v'], prm['cmp_w2_v'])
    o_cmp, p_cmp = cmp_attention(q, q_pos, kc, vc, slopes)
    n_sel_blocks = -(-full_sel.shape[1] // SEL_BLOCK)
    idx = select_blocks(p_cmp, q_pos, n_sel_blocks)
    o_sel = sel_attention(q, q_pos, idx, full_sel[:, :, 0], full_sel[:, :, 1], slopes, sel_qb)
    g = jax.nn.sigmoid(gates.astype(jnp.float32)).reshape(b, T, 3, ATT_HEADS, 1).astype(q.dtype)
    o = g[:, :, 0] * o_cmp + g[:, :, 1] * o_sel + g[:, :, 2] * o_win
    return o.reshape(b, T, ATT_WIDTH)


def layer(x, q_pos, past, prm, is_prompt):
    x = ffn_half(x, prm['norm_ffn1'], prm['ffn1_wg'], prm['ffn1_wu'], prm['ffn1_wd'])
    h = rmsnorm(x, prm['norm_mix'])
    q, kc, vc, ks, vs, kw, vw, gates, z, xbc, dtr = split_proj(h @ prm['w_in'])
    b, T = x.shape[:2]
    q = q.reshape(b, T, ATT_HEADS, HEAD_DIM)

    def kv(a, c):
        return jnp.stack([a.reshape(b, T, KV_HEADS, HEAD_DIM), c.reshape(b, T, KV_HEADS, HEAD_DIM)], axis=2)

    new_cmp, new_sel, new_win_rows = kv(kc, vc), kv(ks, vs), kv(kw, vw)
    slopes = alibi_slopes()
    if is_prompt:
        full_cmp, full_sel, win_all = new_cmp, new_sel, new_win_rows
        o_win = window_attention_prompt(q, new_win_rows[:, :, 0], new_win_rows[:, :, 1], slopes)
        conv0 = jnp.zeros((b, CONV_W - 1, CONV_DIM), xbc.dtype)
        h0 = jnp.zeros((b, SSM_HEADS, SSM_HEADDIM, D_STATE), jnp.float32)
        sel_qb = min(SEL_QB, T)
    else:
        past_cmp, past_sel, past_win, conv0, h0 = past
        full_cmp = jnp.concatenate([past_cmp, new_cmp], axis=1)
        full_sel = jnp.concatenate([past_sel, new_sel], axis=1)
        win_all = jnp.concatenate([past_win, new_win_rows], axis=1)
        k_pos = q_pos[0] - past_win.shape[1] + jnp.arange(win_all.shape[1])
        o_win = window_chunk(q, q_pos, win_all[:, :, 0], win_all[:, :, 1], k_pos, slopes)
        sel_qb = T
    new_win = win_all[:, win_all.shape[1] - min(WINDOW, win_all.shape[1]):]
    o_att = nsa_mixer(q, q_pos, full_cmp, full_sel, o_win, gates, prm, slopes, sel_qb)
    y_ssm, conv_new, h_new = mamba_mixer(z, xbc, dtr, conv0, h0, prm)
    x = x + jnp.concatenate([o_att, y_ssm], axis=-1) @ prm['w_out']
    x = ffn_half(x, prm['norm_ffn2'], prm['ffn2_wg'], prm['ffn2_wu'], prm['ffn2_wd'])
    return x, (new_cmp, new_sel, new_win, conv_new, h_new)


def setup_inputs(seed: int = 0) -> dict:
    key = jax.random.key(seed)
    ks = jax.random.split(key, 40)
    n_pages = PAST_LEN // PAGE_SIZE
    n_used = DEC_BATCH * n_pages
    n_phys = (5 * n_used + 3) // 4
    win_len = min(WINDOW, PAST_LEN)

    def nrm(k, shape, scale):
        return scale * jax.random.normal(k, shape, jnp.float32)

    def gain(k, n):
        return 1.0 + nrm(k, (DEPTH, n), 0.02)

    dt0 = jnp.exp(jax.random.uniform(ks[30], (DEPTH, SSM_HEADS), jnp.float32, math.log(1e-3), math.log(1e-1)))
    return {
        'x_prompt': nrm(ks[0], (BATCH, SEQ, D_MODEL), 1.0),
        'x_sample': nrm(ks[1], (DEC_BATCH, DEC_SEQ, D_MODEL), 1.0),
        'cache_cmp': nrm(ks[2], (DEPTH, n_phys, PAGE_SIZE, 2, KV_HEADS, HEAD_DIM), 1.0),
        'cache_sel': nrm(ks[3], (DEPTH, n_phys, PAGE_SIZE, 2, KV_HEADS, HEAD_DIM), 1.0),
        'cache_win': nrm(ks[4], (DEPTH, DEC_BATCH, win_len, 2, KV_HEADS, HEAD_DIM), 1.0),
        'state_conv': nrm(ks[5], (DEPTH, DEC_BATCH, CONV_W - 1, CONV_DIM), 1.0),
        'state_ssm': nrm(ks[6], (DEPTH, DEC_BATCH, SSM_HEADS, SSM_HEADDIM, D_STATE), 0.1),
        'page_table': jax.random.permutation(ks[7], n_phys)[:n_used].reshape(DEC_BATCH, n_pages).astype(jnp.int32),
        'norm_ffn1': gain(ks[8], D_MODEL),
        'ffn1_wg': nrm(ks[9], (DEPTH, D_MODEL, D_FF), D_MODEL ** -0.5),
        'ffn1_wu': nrm(ks[10], (DEPTH, D_MODEL, D_FF), D_MODEL ** -0.5),
        'ffn1_wd': nrm(ks[11], (DEPTH, D_FF, D_MODEL), D_FF ** -0.5),
        'norm_mix': gain(ks[12], D_MODEL),
        'w_in': nrm(ks[13], (DEPTH, D_MODEL, IN_W), D_MODEL ** -0.5),
        'cmp_pe_k': nrm(ks[14], (DEPTH, CMP_BLOCK, HEAD_DIM), 0.1),
        'cmp_w1_k': nrm(ks[15], (DEPTH, CMP_BLOCK * HEAD_DIM, CMP_HIDDEN), (CMP_BLOCK * HEAD_DIM) ** -0.5),
        'cmp_w2_k': nrm(ks[16], (DEPTH, CMP_HIDDEN, HEAD_DIM), CMP_HIDDEN ** -0.5),
        'cmp_pe_v': nrm(ks[17], (DEPTH, CMP_BLOCK, HEAD_DIM), 0.1),
        'cmp_w1_v': nrm(ks[18], (DEPTH, CMP_BLOCK * HEAD_DIM, CMP_HIDDEN), (CMP_BLOCK * HEAD_DIM) ** -0.5),
        'cmp_w2_v': nrm(ks[19], (DEPTH, CMP_HIDDEN, HEAD_DIM), CMP_HIDDEN ** -0.5),
        'conv_w': nrm(ks[20], (DEPTH, CONV_W, CONV_DIM), CONV_W ** -0.5),
        'conv_b': nrm(ks[21], (DEPTH, CONV_DIM), 0.01),
        'dt_bias': dt0 + jnp.log(-jnp.expm1(-dt0)),
        'a_log': jnp.log(jax.random.uniform(ks[22], (DEPTH, SSM_HEADS), jnp.float32, 1.0, 16.0)),
        'd_skip': 1.0 + nrm(ks[23], (DEPTH, SSM_HEADS), 0.1),
        'ssm_norm': gain(ks[24], SSM_WIDTH),
        'w_out': nrm(ks[25], (DEPTH, D_MIX, D_MODEL), D_MIX ** -0.5),
        'norm_ffn2': gain(ks[26], D_MODEL),
        'ffn2_wg': nrm(ks[27], (DEPTH, D_MODEL, D_FF), D_MODEL ** -0.5),
        'ffn2_wu': nrm(ks[28], (DEPTH, D_MODEL, D_FF), D_MODEL ** -0.5),
        'ffn2_wd': nrm(ks[29], (DEPTH, D_FF, D_MODEL), D_FF ** -0.5),
        'norm_final': 1.0 + nrm(ks[31], (D_MODEL,), 0.02),
    }


def reference(x_prompt, x_sample, cache_cmp, cache_sel, cache_win, state_conv, state_ssm, page_table,
              norm_ffn1, ffn1_wg, ffn1_wu, ffn1_wd, norm_mix, w_in,
              cmp_pe_k, cmp_w1_k, cmp_w2_k, cmp_pe_v, cmp_w1_v, cmp_w2_v,
              conv_w, conv_b, dt_bias, a_log, d_skip, ssm_norm, w_out,
              norm_ffn2, ffn2_wg, ffn2_wu, ffn2_wd, norm_final):
    weights = dict(norm_ffn1=norm_ffn1, ffn1_wg=ffn1_wg, ffn1_wu=ffn1_wu, ffn1_wd=ffn1_wd,
                   norm_mix=norm_mix, w_in=w_in,
                   cmp_pe_k=cmp_pe_k, cmp_w1_k=cmp_w1_k, cmp_w2_k=cmp_w2_k,
                   cmp_pe_v=cmp_pe_v, cmp_w1_v=cmp_w1_v, cmp_w2_v=cmp_w2_v,
                   conv_w=conv_w, conv_b=conv_b, dt_bias=dt_bias, a_log=a_log, d_skip=d_skip,
                   ssm_norm=ssm_norm, w_out=w_out,
                   norm_ffn2=norm_ffn2, ffn2_wg=ffn2_wg, ffn2_wu=ffn2_wu, ffn2_wd=ffn2_wd)
    n_seq_s, n_pages = page_table.shape
    past_len = n_pages * PAGE_SIZE
    pos_p = jnp.arange(x_prompt.shape[1], dtype=jnp.int32)
    pos_s = past_len + jnp.arange(x_sample.shape[1], dtype=jnp.int32)
    xp, xs = x_prompt, x_sample
    st_p, st_s = [], []
    for l in range(DEPTH):
        prm = {name: w[l] for name, w in weights.items()}
        xp, sp = layer(xp, pos_p, None, prm, True)
        past_cmp = cache_cmp[l][page_table].reshape((n_seq_s, past_len) + cache_cmp.shape[3:])
        past_sel = cache_sel[l][page_table].reshape((n_seq_s, past_len) + cache_sel.shape[3:])
        past = (past_cmp, past_sel, cache_win[l], state_conv[l], state_ssm[l])
        xs, ss = layer(xs, pos_s, past, prm, False)
        st_p.append(sp)
        st_s.append(ss)

    def stacked(lst, i):
        return jnp.stack([s[i] for s in lst], axis=0)

    y_prompt = rmsnorm(xp, norm_final)
    y_sample = rmsnorm(xs, norm_final)
    return (y_prompt, y_sample,
            stacked(st_p, 0), stacked(st_p, 1), stacked(st_p, 2), stacked(st_p, 3), stacked(st_p, 4),
            stacked(st_s, 0), stacked(st_s, 1), stacked(st_s, 2), stacked(st_s, 3), stacked(st_s, 4))
```

```python
import numpy as np
import ml_dtypes
from contextlib import ExitStack
import concourse.bass as bass
import concourse.mybir as mybir
from concourse.bass_utils import run_bass_kernel_spmd


F32 = mybir.dt.float32
BF16 = mybir.dt.bfloat16
I32 = mybir.dt.int32
U32 = mybir.dt.uint32
AF = mybir.ActivationFunctionType
ALU = mybir.AluOpType
AX = mybir.AxisListType

DMA_K = 6
COMPUTE = ('pe', 'act', 'dve', 'pool')


class Op:
    __slots__ = ('stream', 'fn', 'is_dma', 'deps', 'signal', 'sem', 'val', 'waits', 'clock', 'idx')


class Prog:
    def __init__(self, nc):
        self.nc = nc
        self.ops = []
        self.streams = {s: [] for s in ('pe', 'act', 'dve', 'pool', 'sp')}
        self.acc = {}
        self.dma_count = {s: 0 for s in self.streams}
        self.pending = {s: set() for s in self.streams}
        self.recent_dma = {s: [] for s in self.streams}

    @staticmethod
    def region(ap):
        t = ap.tensor
        name = t.name
        tn = type(t).__name__
        ext = 0
        aps = list(ap.ap)
        if tn.startswith('DRam'):
            for s, c in aps:
                ext += (c - 1) * abs(s)
            return (name, 0, 1, ap.offset, ap.offset + ext + 1)
        shape = list(t.shape)
        pstride = 1
        for d in shape[1:]:
            pstride *= d
        p0 = ap.offset // pstride
        f0 = ap.offset % pstride
        npart = aps[0][1]
        if tn.startswith('PSum'):
            return (name, 0, 128, 0, 1 << 30)
        for s, c in aps[1:]:
            ext += (c - 1) * abs(s)
        return (name, p0, p0 + npart, f0, f0 + ext + 1)

    def _deps_for(self, idx, stream, is_dma, reads, writes):
        deps = set()
        for is_w, aps in ((False, reads), (True, writes)):
            for ap in aps:
                if ap is None:
                    continue
                name, p0, p1, f0, f1 = self.region(ap)
                if type(ap.tensor).__name__.startswith('PSum'):
                    is_w = True
                lst = self.acc.setdefault(name, [])
                keep = []
                for a in lst:
                    oi, ow, q0, q1, g0, g1 = a
                    ov = not (q1 <= p0 or p1 <= q0 or g1 <= f0 or f1 <= g0)
                    if ov and (is_w or ow) and oi != idx:
                        o = self.ops[oi]
                        same = (o.stream == stream) and not o.is_dma and not is_dma
                        if same and stream == 'pe':
                            pass
                        elif same and stream != 'pool' and (not ow) and is_w:
                            pass
                        else:
                            deps.add(oi)
                    covered = is_w and ov and p0 <= q0 and q1 <= p1 and f0 <= g0 and g1 <= f1
                    if not covered:
                        keep.append(a)
                keep.append([idx, is_w, p0, p1, f0, f1])
                self.acc[name] = keep
        return deps

    def add(self, stream, fn, reads=(), writes=(), is_dma=False):
        op = Op()
        op.idx = len(self.ops)
        op.stream = stream
        op.fn = fn
        op.is_dma = is_dma
        op.signal = is_dma
        op.sem = None
        op.val = 0
        op.waits = []
        self.ops.append(op)
        op.deps = self._deps_for(op.idx, stream, is_dma, reads, writes)
        if self.pending[stream]:
            op.deps |= self.pending[stream]
            self.pending[stream] = set()
        if is_dma:
            self.recent_dma[stream] = (self.recent_dma[stream] + [op.idx])[-DMA_K:]
        for d in op.deps:
            self.ops[d].signal = True
        self.streams[stream].append(op)
        return op

    def fence(self):
        F = set()
        for st, lst in self.streams.items():
            for o in reversed(lst):
                if not o.is_dma:
                    F.add(o.idx)
                    break
            F |= set(self.recent_dma[st])
        for st in self.streams:
            self.pending[st] = set(F)
        self.acc = {}

    def dma(self, out, in_, stream='sp', **kw):
        return self.add(stream, lambda e: e.dma_start(out=out, in_=in_, **kw), [in_], [out], is_dma=True)

    def mm(self, out, lhsT, rhs, start=True, stop=True, **kw):
        return self.add('pe', lambda e: e.matmul(out, lhsT, rhs, start=start, stop=stop, **kw),
                        [lhsT, rhs], [out])

    def tr(self, out, in_, ident):
        return self.add('pe', lambda e: e.transpose(out, in_, ident), [in_, ident], [out])

    def act(self, out, in_, func, bias=None, scale=None, accum_out=None, eng='act'):
        rd = [in_]
        kw = {}
        if bias is not None:
            kw['bias'] = bias
            if not isinstance(bias, (int, float)):
                rd.append(bias)
        if scale is not None:
            kw['scale'] = scale
            if not isinstance(scale, (int, float)):
                rd.append(scale)
        wr = [out]
        if accum_out is not None:
            kw['accum_out'] = accum_out
            wr.append(accum_out)
        return self.add('act', lambda e: e.activation(out, in_, func, **kw), rd, wr)

    def tt(self, out, in0, in1, op, eng='dve'):
        return self.add(eng, lambda e: e.tensor_tensor(out, in0, in1, op), [in0, in1], [out])

    def ts(self, out, in0, s1, s2=None, op0=ALU.mult, op1=None, eng='dve', accum_out=None):
        rd = [in0]
        if not isinstance(s1, (int, float)):
            rd.append(s1)
        if s2 is not None and not isinstance(s2, (int, float)):
            rd.append(s2)
        kw = {}
        if op1 is not None:
            kw['op1'] = op1
        wr = [out]
        if accum_out is not None:
            kw['accum_out'] = accum_out
            wr.append(accum_out)
        return self.add(eng, lambda e: e.tensor_scalar(out, in0, s1, s2, op0, **kw), rd, wr)

    def stt(self, out, in0, scalar, in1, op0, op1):
        rd = [in0, in1]
        if not isinstance(scalar, (int, float)):
            rd.append(scalar)
        return self.add('dve', lambda e: e.scalar_tensor_tensor(out, in0, scalar, in1, op0, op1), rd, [out])

    def copy(self, out, in_, eng='dve'):
        if eng == 'act':
            return self.add('act', lambda e: e.copy(out, in_), [in_], [out])
        return self.add(eng, lambda e: e.tensor_copy(out, in_), [in_], [out])

    def memset(self, ap, val, eng='dve'):
        return self.add(eng, lambda e: e.memset(ap, val), [], [ap])

    def finalize(self, stack):
        nc = self.nc
        sems = {}
        for s in COMPUTE:
            sems[s] = stack.enter_context(nc.semaphore('s_' + s))
        dsem = {}
        for s in ('sp', 'pool', 'act'):
            dsem[s] = [stack.enter_context(nc.semaphore('d_%s%d' % (s, i))) for i in range(DMA_K)]
        cnt = {s: 0 for s in COMPUTE}
        dcnt = {}
        dlast = {}
        for op in self.ops:
            if op.is_dma:
                n = dcnt.get(op.stream, 0)
                dcnt[op.stream] = n + 1
                k = n % DMA_K
                op.sem = ('d', op.stream, k)
                op.val = 16 * (n // DMA_K + 1)
                prev = dlast.get((op.stream, k))
                if prev is not None:
                    op.deps.add(prev)
                dlast[(op.stream, k)] = op.idx
            elif op.signal:
                cnt[op.stream] += 1
                op.sem = ('c', op.stream)
                op.val = cnt[op.stream]
        known = {s: {} for s in self.streams}
        for op in self.ops:
            kn = known[op.stream]
            for d in sorted(op.deps):
                p = self.ops[d]
                if kn.get(p.sem, 0) >= p.val:
                    continue
                op.waits.append((p.sem, p.val))
                for k2, v2 in p.clock.items():
                    if kn.get(k2, 0) < v2:
                        kn[k2] = v2
            clk = dict(kn)
            if op.sem is not None:
                clk[op.sem] = op.val
                if not op.is_dma:
                    pass
            op.clock = clk
        self.total_waits = sum(len(o.waits) for o in self.ops)

        def semobj(key):
            if key[0] == 'c':
                return sems[key[1]]
            return dsem[key[1]][key[2]]

        block = stack.enter_context(nc.Block())

        def emit_stream(eng, name):
            for op in self.streams[name]:
                for key, v in op.waits:
                    eng.wait_ge(semobj(key), v)
                inst = op.fn(eng)
                if op.sem is not None:
                    inst.then_inc(semobj(op.sem), 16 if op.is_dma else 1)
            n = dcnt.get(name, 0)
            for k in range(min(n, DMA_K)):
                last = (n - 1 - k) // DMA_K * DMA_K + k
                tot = 16 * (last // DMA_K + 1)
                eng.wait_ge(dsem[name][k], tot)

        @block.sync
        def _(e):
            emit_stream(e, 'sp')

        @block.tensor
        def _(e):
            emit_stream(e, 'pe')

        @block.scalar
        def _(e):
            emit_stream(e, 'act')

        @block.vector
        def _(e):
            emit_stream(e, 'dve')

        @block.gpsimd
        def _(e):
            emit_stream(e, 'pool')


NPBF = ml_dtypes.bfloat16
D = 1024
DFF = 2816
NF = DFF // 128
EPS = 1e-6
Q0, KC, VC, KS, VS, KW, VW, GT, Z0, XB, DTO = 0, 512, 640, 768, 896, 1024, 1152, 1280, 1304, 1816, 2840
INW = 2848
NEGM = -30000.0
SLOPES = [2.0 ** (-(h + 1)) for h in range(8)]
NQS = 8
TQ = 256
NSUB = TQ // 128
NCM = 2048 // TQ + 1


class Cfg:
    def __init__(self, nseq_p=2, seq=4096, nseq_s=16, past=8192, n_phys=10240,
                 stages=('ffn1', 'mixp', 'mixs', 'ffn2'), dbg=False):
        self.nseq_p = nseq_p
        self.seq = seq
        self.nseq_s = nseq_s
        self.past = past
        self.n_phys = n_phys
        self.stages = stages
        self.dbg = dbg
        self.ntok_p = nseq_p * seq
        self.ntok_s = nseq_s * NQS
        self.ntok = self.ntok_p + self.ntok_s
        self.npages = past // 128
        self.nkt_s = self.npages + 1
        self.ncmp_s = past // 16 - 1
        self.nsel_s = past // 64 + 1


def k_rows(pos, with_e, cmp_n=None):
    n = len(pos) if pos is not None else len(cmp_n)
    r = np.zeros((64, n), np.float32)
    if cmp_n is None:
        if with_e:
            jj = (pos // 64) % 32
            r[jj, np.arange(n)] = 1.0
        r[32] = pos % 128
        r[33] = pos // 128
        r[34] = 1.0
        r[35] = 1.0
    else:
        r[34] = 1.0
        r[35] = 1.0
        r[36] = cmp_n % 128
        r[37] = cmp_n // 128
        r[38] = 1.0
    return r.astype(NPBF)


def q_rows(qpos, slope):
    n = len(qpos)
    r = np.zeros((32, n), np.float32)
    r[0] = slope
    r[1] = 128 * slope
    r[2] = -slope * 64 * (qpos // 64)
    r[3] = -slope * (qpos % 64)
    r[4] = 16 * slope
    r[5] = 2048 * slope
    r[6] = 31 * slope
    return r


def host_consts(cfg):
    c = {}
    S = cfg.seq
    c['ident'] = np.eye(128, dtype=np.float32)
    c['tri'] = np.triu(np.ones((128, 128), np.float32))
    c['ones'] = np.ones((128, 128), np.float32)
    qp = np.arange(S)
    c['alq'] = np.stack([q_rows(qp, SLOPES[h]) for h in range(8)], axis=1).astype(NPBF)
    c['kc_sel'] = k_rows(qp, True)
    c['kc_win'] = k_rows(qp, False)
    c['kc_cmp'] = k_rows(None, False, cmp_n=np.arange(256))
    ki = np.arange(128)[:, None]
    qi = np.arange(TQ)[None, :]
    c['caus'] = np.stack([np.where(128 * v + ki > qi, NEGM, 0.0) for v in range(NSUB)], axis=1).astype(NPBF)
    c['far'] = np.stack([np.where(qi - ki >= 512 - 128 * u, NEGM, 0.0) for u in range(1, 5)], axis=1).astype(NPBF)
    c['cmpm'] = np.stack([np.where(16 * ki + 31 > qi + TQ * r, NEGM, 0.0) for r in range(NCM)], axis=1).astype(NPBF)
    nb = S // 16 - 1
    nselp = S // 64
    ov = np.zeros((256, 64), np.float32)
    for i in range(min(nb, 256)):
        for j in (i // 4, (i + 1) // 4):
            if j < 64:
                ov[i, j] += 1.0
    c['ovl'] = ov.reshape(2, 128, 64).transpose(1, 0, 2).copy().astype(NPBF)
    j = np.arange(64)[None, :]
    q = qp[:, None]
    forced = (j == q // 64) | (j == 0)
    valid = (j * 64 <= q) & (j < nselp)
    VAL = np.where(forced, 0.0, np.where(valid, 1.0, 0.0))
    ADD = np.where(forced, 1e9, np.where(valid, 0.0, -1.0))
    c['va'] = np.concatenate([VAL, ADD], axis=1).astype(np.float32)
    PAST = cfg.past
    qs = PAST + np.arange(NQS)
    aq = np.zeros((32, 2, 4, NQS), np.float32)
    for g in range(2):
        for hh in range(4):
            aq[:, g, hh, :] = q_rows(qs, SLOPES[g * 4 + hh])
    c['alq_s'] = aq.reshape(32, 2, 32).astype(NPBF)
    nk = cfg.nkt_s * 128
    c['kc_sel_s'] = k_rows(np.arange(nk), True)
    c['kc_win_s'] = k_rows(PAST - 512 + np.arange(640), False)
    c['kc_cmp_s'] = k_rows(None, False, cmp_n=np.arange(((cfg.ncmp_s + 127) // 128) * 128))
    ki = np.arange(128)[:, None]
    qi8 = np.tile(np.arange(NQS), 4)[None, :]
    c['caus8'] = np.where((ki > qi8) | (ki >= NQS), NEGM, 0.0).astype(NPBF)
    c['far8'] = np.where(ki <= qi8, NEGM, 0.0).astype(NPBF)
    nlast = cfg.ncmp_s - ((cfg.ncmp_s + 127) // 128 - 1) * 128
    c['cmpm_s'] = np.where((ki >= nlast) & (qi8 >= 0), NEGM, 0.0).astype(NPBF)
    nch = (cfg.ncmp_s + 127) // 128
    nsl = cfg.nsel_s
    ovs = np.zeros((nch * 128, nsl), np.float32)
    for i in range(cfg.ncmp_s):
        for jx in (i // 4, (i + 1) // 4):
            if jx < nsl:
                ovs[i, jx] += 1.0
    c['ovl_s'] = ovs.reshape(nch, 128, nsl).transpose(1, 0, 2).copy().astype(NPBF)
    js = np.arange(nsl)[None, :]
    forced_s = (js == (qs[:, None] // 64)) | (js == 0)
    vas = np.zeros((NQS, 2, nsl), np.float32)
    vas[:, 0] = np.where(forced_s, 0.0, 1.0)
    vas[:, 1] = np.where(forced_s, 1e9, 0.0)
    c['va_s'] = vas
    return c


CONST_DT = {'alq': BF16, 'kc_sel': BF16, 'kc_win': BF16, 'kc_cmp': BF16, 'caus': BF16, 'far': BF16, 'cmpm': BF16,
            'ovl': BF16, 'alq_s': BF16, 'kc_sel_s': BF16, 'kc_win_s': BF16, 'kc_cmp_s': BF16, 'caus8': BF16,
            'far8': BF16, 'ovl_s': BF16, 'cmpm_s': BF16}


def build(cfg):
    nc = bass.Bass("TRN2", target_bir_lowering=False)
    P = Prog(nc)
    top = ExitStack()
    NT, NTP, NS = cfg.ntok, cfg.ntok_p, cfg.ntok_s
    S = cfg.seq
    PAST = cfg.past

    def din(name, shape, dt=F32):
        return nc.dram_tensor(name, list(shape), dt, kind="ExternalInput").ap()

    def dout(name, shape, dt=F32):
        return nc.dram_tensor(name, list(shape), dt, kind="ExternalOutput").ap()

    def dscr(name, shape, dt=F32):
        return nc.dram_tensor(name, list(shape), dt, kind="Internal").ap()

    x_in = din("x_in", [NT, D])
    hc = host_consts(cfg)
    C = {k: din("c_" + k, v.shape, CONST_DT.get(k, F32)) for k, v in hc.items()}
    w = {}
    for nm, shp in (("ffn1_wg", [D, DFF]), ("ffn1_wu", [D, DFF]), ("ffn1_wd", [DFF, D]),
                    ("ffn2_wg", [D, DFF]), ("ffn2_wu", [D, DFF]), ("ffn2_wd", [DFF, D]),
                    ("norm_ffn1", [D]), ("norm_ffn2", [D]), ("norm_final", [D]), ("norm_mix", [D]),
                    ("w_in", [D, INW]), ("w_out", [D, D]),
                    ("cmp_pe_k", [32, 64]), ("cmp_w1_k", [2048, 128]), ("cmp_w2_k", [128, 64]),
                    ("cmp_pe_v", [32, 64]), ("cmp_w1_v", [2048, 128]), ("cmp_w2_v", [128, 64]),
                    ("conv_w", [4, 1024]), ("conv_b", [1024]), ("dt_bias", [8]), ("a_log", [8]),
                    ("d_skip", [8]), ("ssm_norm", [512])):
        w[nm] = din(nm, shp)
    cache_cmp = din("cache_cmp", [cfg.n_phys * 128, 256])
    cache_sel = din("cache_sel", [cfg.n_phys * 128, 256])
    cache_win = din("cache_win", [max(cfg.nseq_s, 1), 512, 256])
    state_conv = din("state_conv", [max(cfg.nseq_s, 1) * 3, 1024])
    state_ssm = din("state_ssm", [max(cfg.nseq_s, 1), 512, 128])
    page_table = din("page_table", [max(cfg.nseq_s, 1), cfg.npages], I32)
    iota_p = din("iota_p", [128, cfg.npages], F32)

    y_out = dout("y_out", [NT, D])
    o_cmp_p = dout("o_cmp_p", [max(NTP, 1), 256])
    o_sel_p = dout("o_sel_p", [max(NTP, 1), 256])
    o_win_p = dout("o_win_p", [max(cfg.nseq_p, 1), 512, 256])
    o_conv_p = dout("o_conv_p", [max(cfg.nseq_p, 1), 3, 1024])
    o_ssm_p = dout("o_ssm_p", [max(cfg.nseq_p, 1), 512, 128])
    o_cmp_s = dout("o_cmp_s", [max(NS, 1), 256])
    o_sel_s = dout("o_sel_s", [max(NS, 1), 256])
    o_win_s = dout("o_win_s", [max(cfg.nseq_s, 1), 512, 256])
    o_conv_s = dout("o_conv_s", [max(cfg.nseq_s, 1), 3, 1024])
    o_ssm_s = dout("o_ssm_s", [max(cfg.nseq_s, 1), 512, 128])
    x1_d = dscr("x1_scr", [NT, D])
    x2_d = dscr("x2_scr", [NT, D])
    dbg = {}
    if cfg.dbg:
        dbg['att'] = dout("dbg_att", [NT, 512])
        dbg['ssm'] = dout("dbg_ssm", [NT, 512])
        dbg['x2'] = dout("dbg_x2", [NT, D])

    PSF = [top.enter_context(nc.psum_tensor("psf%d" % i, [128, 512], F32)) for i in range(7)]
    PSB = top.enter_context(nc.psum_tensor("psb", [128, 1024], BF16))

    uniq = [0]

    class Scope:
        def __init__(self):
            self.st = ExitStack()

        def sb(self, name, shape, dt=F32):
            uniq[0] += 1
            return self.st.enter_context(nc.sbuf_tensor("%s_u%d" % (name, uniq[0]), list(shape), dt))

        def close(self):
            P.fence()
            self.st.close()

    def rms_rstd(sc_stat, X, L, col, junk):
        st = sc_stat
        width = X.shape[-1]
        P.act(junk[0:L, :width], X, AF.Square, accum_out=st[0:L, col:col + 1])
        P.ts(st[0:L, col + 1:col + 2], st[0:L, col:col + 1], 1.0 / width, EPS, op0=ALU.mult, op1=ALU.add)
        P.act(st[0:L, col + 2:col + 3], st[0:L, col + 1:col + 2], AF.Sqrt)
        o, i = st[0:L, col + 3:col + 4], st[0:L, col + 2:col + 3]
        P.add('dve', lambda e, o=o, i=i: e.reciprocal(o, i), [i], [o])
        return o

    def ffn_phase(pfx, src_d, dst_d, final):
        sc = Scope()
        identf = sc.sb("identf", [128, 128], F32)
        identb = sc.sb("identb", [128, 128], BF16)
        P.dma(identf[:], C['ident'])
        P.copy(identb[:], identf[:])
        WG = sc.sb("WG", [128, 8, DFF], BF16)
        WU = sc.sb("WU", [128, 8, DFF], BF16)
        WD = sc.sb("WD", [128, NF, D], BF16)
        stg = [sc.sb("stg%d" % i, [128, 1408], F32) for i in range(2)]
        xin = [sc.sb("xin%d" % i, [128, D], F32) for i in range(2)]
        xres = [sc.sb("xres%d" % i, [128, D], F32) for i in range(2)]
        hb = sc.sb("hb", [128, D], BF16)
        hT = sc.sb("hT", [128, 8, 512], BF16)
        AT = sc.sb("AT", [128, NF, 512], BF16)
        sil = [sc.sb("sil%d" % i, [128, 512], BF16) for i in range(2)]
        nwT = sc.sb("nwT", [128, 8], F32)
        nfin = sc.sb("nfin", [128, D], F32)
        stat = sc.sb("stat", [128, 32], F32)
        junk = sc.sb("junk", [128, D], BF16)
        psT = PSB
        psG, psU, psO = PSF[0:2], PSF[2:4], PSF[4:6]
        cnt = [0]

        def load_w(dst3, src2, nchunk, width):
            for c in range(nchunk):
                for c0 in range(0, width, 1408):
                    cw = min(1408, width - c0)
                    s = stg[cnt[0] % 2]
                    eng = ('act', 'dve', 'pool')[cnt[0] % 3]
                    cnt[0] += 1
                    P.dma(s[:, :cw], src2[c * 128:(c + 1) * 128, c0:c0 + cw])
                    P.copy(dst3[:, c, c0:c0 + cw], s[:, :cw], eng=eng)

        load_w(WG, w[pfx + "_wg"], 8, DFF)
        load_w(WU, w[pfx + "_wu"], 8, DFF)
        load_w(WD, w[pfx + "_wd"], NF, D)
        nrm = w["norm_ffn1" if pfx == "ffn1" else "norm_ffn2"]
        P.add('sp', lambda e: e.dma_start(out=nwT[:], in_=nrm.rearrange("(c p) -> p c", p=128),
                                          allow_slow_non_contiguous=True), [nrm], [nwT[:]], is_dma=True)
        if final:
            P.dma(nfin[:], w["norm_final"].partition_broadcast(128))
        tiles = []
        t0 = 0
        while t0 < NT:
            n = min(512, NT - t0)
            tiles.append((t0, n))
            t0 += n
        scn = [0]

        def subs_of(ti):
            t0, n = tiles[ti]
            return [(s0, min(128, n - s0)) for s0 in range(0, n, 128)]

        def prologue_sub(ti, si):
            t0, n = tiles[ti]
            s0, L = subs_of(ti)[si]
            X = xin[scn[0] % 2]
            scn[0] += 1
            P.dma(X[0:L, :], src_d[t0 + s0:t0 + s0 + L, :])
            r = rms_rstd(stat, X[0:L, :], L, 4 * si, junk)
            P.ts(hb[0:L, :], X[0:L, :], r, None, op0=ALU.mult)
            for k in range(8):
                P.tr(psT[:, k * 128:k * 128 + L], hb[0:L, k * 128:(k + 1) * 128], identb[0:L, 0:L])
            for k in range(8):
                if k % 2 == 0:
                    P.act(hT[:, k, s0:s0 + L], psT[:, k * 128:k * 128 + L], AF.Copy, scale=nwT[:, k:k + 1])
                else:
                    P.ts(hT[:, k, s0:s0 + L], psT[:, k * 128:k * 128 + L], nwT[:, k:k + 1], None, op0=ALU.mult)

        for si in range(len(subs_of(0))):
            prologue_sub(0, si)
        rc = 0
        for ti, (t0, n) in enumerate(tiles):
            subs = subs_of(ti)
            for f in range(NF):
                g = psG[f % 2]
                u = psU[f % 2]
                for k in range(8):
                    P.mm(g[:, :n], WG[:, k, f * 128:(f + 1) * 128], hT[:, k, :n], start=(k == 0), stop=(k == 7))
                for k in range(8):
                    P.mm(u[:, :n], WU[:, k, f * 128:(f + 1) * 128], hT[:, k, :n], start=(k == 0), stop=(k == 7))
                sl = sil[f % 2]
                P.act(sl[:, :n], g[:, :n], AF.Silu)
                P.tt(AT[:, f, :n], sl[:, :n], u[:, :n], ALU.mult)
            nxt = subs_of(ti + 1) if ti + 1 < len(tiles) else []
            for si, (s0, L) in enumerate(subs):
                X = xres[rc % 2]
                rc += 1
                P.dma(X[0:L, :], src_d[t0 + s0:t0 + s0 + L, :])
                for h2 in range(2):
                    o = psO[(si * 2 + h2) % 2]
                    for f in range(NF):
                        P.mm(o[0:L, :], AT[:, f, s0:s0 + L], WD[:, f, h2 * 512:(h2 + 1) * 512],
                             start=(f == 0), stop=(f == NF - 1))
                    P.stt(X[0:L, h2 * 512:(h2 + 1) * 512], o[0:L, :], 0.5, X[0:L, h2 * 512:(h2 + 1) * 512],
                          ALU.mult, ALU.add)
                if final:
                    r = rms_rstd(stat, X[0:L, :], L, 16 + 4 * si, junk)
                    P.stt(X[0:L, :], X[0:L, :], r, nfin[0:L, :], ALU.mult, ALU.mult)
                P.dma(dst_d[t0 + s0:t0 + s0 + L, :], X[0:L, :])
                if si < len(nxt):
                    prologue_sub(ti + 1, si)
            for si in range(len(subs), len(nxt)):
                prologue_sub(ti + 1, si)
        sc.close()

    def mixer_phase():
        mc = Scope()
        identf = mc.sb("identf", [128, 128], F32)
        identb = mc.sb("identb", [128, 128], BF16)
        tri = mc.sb("tri", [128, 128], F32)
        onesf = mc.sb("onesf", [128, 128], F32)
        P.dma(identf[:], C['ident'])
        P.copy(identb[:], identf[:])
        P.dma(tri[:], C['tri'])
        P.dma(onesf[:], C['ones'])
        Win = mc.sb("Win", [128, 8, INW], BF16)
        Wout = mc.sb("Wout", [128, 8, D], BF16)
        W1 = [mc.sb("W1_%d" % i, [128, 32, 128], BF16) for i in range(2)]
        W2 = [mc.sb("W2_%d" % i, [128, 64], BF16) for i in range(2)]
        peT = [mc.sb("peT_%d" % i, [128, 32], BF16) for i in range(2)]
        b1 = [mc.sb("b1_%d" % i, [128, 1], F32) for i in range(2)]
        nwT = mc.sb("nwT", [128, 8], F32)
        cw = mc.sb("cw", [128, 8, 4], F32)
        cb = mc.sb("cb", [128, 8], F32)
        dtb = mc.sb("dtb", [128, 8], F32)
        Aneg = mc.sb("Aneg", [128, 8], F32)
        dsk = mc.sb("dsk", [128, 8], F32)
        snw = mc.sb("snw", [128, 4], F32)
        stat = mc.sb("stat", [128, 32], F32)
        junk = mc.sb("junk", [128, D], BF16)
        tmpsc = Scope()
        stg = [tmpsc.sb("mstg%d" % i, [128, 1424], F32) for i in range(2)]
        cnt = [0]

        def slow_dma(out, in_):
            P.add('sp', lambda e: e.dma_start(out=out, in_=in_, allow_slow_non_contiguous=True), [in_], [out],
                  is_dma=True)

        def load_w(dst3, src2, nchunk, width, piece=1424):
            for c in range(nchunk):
                for c0 in range(0, width, piece):
                    cwid = min(piece, width - c0)
                    s = stg[cnt[0] % 2]
                    eng = ('act', 'dve', 'pool')[cnt[0] % 3]
                    cnt[0] += 1
                    P.dma(s[:, :cwid], src2[c * 128:(c + 1) * 128, c0:c0 + cwid])
                    P.copy(dst3[:, c, c0:c0 + cwid], s[:, :cwid], eng=eng)

        load_w(Win, w['w_in'], 8, INW)
        load_w(Wout, w['w_out'], 8, D)
        for i, kv in enumerate(('k', 'v')):
            w1 = w['cmp_w1_' + kv].rearrange("(r d) h -> d r h", d=64)
            for half in range(2):
                for r0 in range(0, 32, 8):
                    s = stg[cnt[0] % 2]
                    cnt[0] += 1
                    sv = s[half * 64:(half + 1) * 64, 0:1024].rearrange("p (r h) -> p r h", h=128)
                    P.dma(sv, w1[:, r0:r0 + 8, :])
                    P.copy(W1[i][half * 64:(half + 1) * 64, r0:r0 + 8, :], sv)
            s = stg[cnt[0] % 2]
            cnt[0] += 1
            P.dma(s[:, 0:64], w['cmp_w2_' + kv])
            P.copy(W2[i][:], s[:, 0:64])
            s = stg[cnt[0] % 2]
            cnt[0] += 1
            for half in range(2):
                slow_dma(s[half * 64:(half + 1) * 64, 0:32], w['cmp_pe_' + kv].rearrange("r d -> d r"))
            P.copy(peT[i][:], s[:, 0:32])
            ps = PSF[0]
            for r in range(32):
                P.mm(ps[:, 0:1], W1[i][0:64, r, :], peT[i][0:64, r:r + 1], start=(r == 0), stop=(r == 31))
            P.copy(b1[i][:], ps[:, 0:1])
        slow_dma(nwT[:], w['norm_mix'].rearrange("(c p) -> p c", p=128))
        for tap in range(4):
            slow_dma(cw[:, :, tap], w['conv_w'][tap, :].rearrange("(c p) -> p c", p=128))
        slow_dma(cb[:], w['conv_b'].rearrange("(c p) -> p c", p=128))
        slow_dma(snw[:], w['ssm_norm'].rearrange("(c p) -> p c", p=128))
        P.dma(dtb[:], w['dt_bias'].partition_broadcast(128))
        P.dma(dsk[:], w['d_skip'].partition_broadcast(128))
        P.dma(Aneg[:], w['a_log'].partition_broadcast(128))
        P.act(Aneg[:], Aneg[:], AF.Exp)
        P.ts(Aneg[:], Aneg[:], -1.0, None, op0=ALU.mult)

        tmpsc.close()
        K = dict(identf=identf, identb=identb, tri=tri, onesf=onesf, Win=Win, Wout=Wout, W1=W1, W2=W2, b1=b1,
                 nwT=nwT, cw=cw, cb=cb, dtb=dtb, Aneg=Aneg, dsk=dsk, snw=snw, stat=stat, junk=junk)
        if 'mixp' in cfg.stages:
            for sq in range(cfg.nseq_p):
                mixer_prompt(K, sq)
        if 'mixs' in cfg.stages and cfg.nseq_s > 0:
            mixer_sample(K)
        mc.close()

    def ssd_chunk(K, T, L, BT, CT, xtm, Btm, dtv, av, zs, hTf, hTb, ymix_out):
        tri, onesf = K['tri'], K['onesf']
        psC, psCB, psE, psY, psYo, psSt = PSF[0], PSF[1], PSF[2:4], PSF[4], PSF[5], PSF[6]
        acsc, expacs, dend, cd = T['acsc'], T['expacs'], T['dend'], T['cd']
        aTri, CBm, segc, Eb, MTb, xdt, xdtd, Ys, Yt, tmp2 = (T[k] for k in
                                                             ('aTri', 'CBm', 'segc', 'Eb', 'MTb', 'xdt', 'xdtd', 'Ys',
                                                              'Yt', 'tmp2'))
        P.mm(psC[0:L, 0:8], tri[0:L, 0:L], av, start=True, stop=True)
        P.copy(acsc[0:L, :], psC[0:L, 0:8])
        P.act(expacs[0:L, :], acsc[0:L, :], AF.Exp)
        P.tt(aTri[0:L, :, 0:L], tri[0:L, 0:L].unsqueeze(1).broadcast_to([L, 8, L]),
             av.unsqueeze(2).broadcast_to([L, 8, L]), ALU.mult)
        for g in range(2):
            P.mm(psCB[0:L, g * 128:g * 128 + L], BT[g], CT[g], start=True, stop=True)
            P.tt(CBm[0:L, g, 0:L], psCB[0:L, g * 128:g * 128 + L], tri[0:L, 0:L], ALU.mult)
        P.tt(xdt[0:L, :].rearrange("p (h d) -> p h d", h=8), xtm.rearrange("p (h d) -> p h d", h=8),
             dtv.unsqueeze(2).broadcast_to([L, 8, 64]), ALU.mult)
        for h in range(8):
            g = h // 4
            pe = psE[h % 2]
            P.mm(pe[0:L, 0:L], onesf[0:L, 0:L], aTri[0:L, h, 0:L], start=True, stop=True)
            sg = segc[h % 2]
            P.ts(sg[0:L, 0:L], pe[0:L, 0:L], acsc[0:L, h:h + 1], 0.0, op0=ALU.subtract, op1=ALU.min)
            E = Eb[h % 2]
            P.act(E[0:L, 0:L], sg[0:L, 0:L], AF.Exp)
            P.copy(dend[0:L, h:h + 1], E[0:L, L - 1:L], eng='pool')
            M = MTb[h % 2]
            P.tt(M[0:L, 0:L], E[0:L, 0:L], CBm[0:L, g, 0:L], ALU.mult)
            P.mm(psY[0:L, h * 64:(h + 1) * 64], M[0:L, 0:L], xdt[0:L, h * 64:(h + 1) * 64],
                 start=(h == 0), stop=True, skip_group_check=True)
        for g in range(2):
            P.mm(psYo[0:L, g * 256:(g + 1) * 256], CT[g], hTb[:, g * 256:(g + 1) * 256],
                 start=(g == 0), stop=True, skip_group_check=True)
        P.tt(Ys[0:L, :].rearrange("p (h d) -> p h d", h=8), psYo[0:L, :].rearrange("p (h d) -> p h d", h=8),
             expacs[0:L, :].unsqueeze(2).broadcast_to([L, 8, 64]), ALU.mult)
        P.tt(Yt[0:L, :], psY[0:L, :], Ys[0:L, :], ALU.add)
        P.tt(tmp2[0:L, :].rearrange("p (h d) -> p h d", h=8), xtm.rearrange("p (h d) -> p h d", h=8),
             K['dsk'][0:L, :].unsqueeze(2).broadcast_to([L, 8, 64]), ALU.mult)
        P.tt(Yt[0:L, :], Yt[0:L, :], tmp2[0:L, :], ALU.add)
        P.tt(Yt[0:L, :], Yt[0:L, :], zs, ALU.mult)
        st = K['stat']
        for g in range(2):
            r = rms_rstd(st, Yt[0:L, g * 256:(g + 1) * 256], L, 16 + 4 * g, K['junk'])
            P.ts(ymix_out[:, g * 256:(g + 1) * 256], Yt[0:L, g * 256:(g + 1) * 256], r, None, op0=ALU.mult)
        P.tt(xdtd[0:L, :].rearrange("p (h d) -> p h d", h=8), xdt[0:L, :].rearrange("p (h d) -> p h d", h=8),
             dend[0:L, :].unsqueeze(2).broadcast_to([L, 8, 64]), ALU.mult)
        for g in range(2):
            P.mm(psSt[:, g * 256:(g + 1) * 256], Btm[g], xdtd[0:L, g * 256:(g + 1) * 256],
                 start=(g == 0), stop=True, skip_group_check=True)
        P.mm(psC[:, 8:16], onesf[0:L, :], av, start=True, stop=True)
        P.act(T['cdall'][:, :], psC[:, 8:16], AF.Exp)
        P.tt(hTf[:, :].rearrange("p (h d) -> p h d", h=8), hTf[:, :].rearrange("p (h d) -> p h d", h=8),
             T['cdall'][:, :].unsqueeze(2).broadcast_to([128, 8, 64]), ALU.mult)
        P.tt(hTf[:, :], hTf[:, :], psSt[:, :], ALU.add)
        P.copy(hTb[:, :], hTf[:, :], eng='pool')

    def ssd_tmp(sc):
        T = {}
        for nm, shp, dt in (('acsc', [128, 8], F32), ('expacs', [128, 8], F32), ('dend', [128, 8], F32),
                            ('cd', [128, 8], F32), ('cdall', [128, 8], F32), ('aTri', [128, 8, 128], F32),
                            ('CBm', [128, 2, 128], F32), ('xdt', [128, 512], BF16), ('xdtd', [128, 512], BF16),
                            ('Ys', [128, 512], F32), ('Yt', [128, 512], F32), ('tmp2', [128, 512], F32)):
            T[nm] = sc.sb("ssd_" + nm, shp, dt)
        T['segc'] = [sc.sb("ssd_segc%d" % i, [128, 128], F32) for i in range(2)]
        T['Eb'] = [sc.sb("ssd_Eb%d" % i, [128, 128], F32) for i in range(2)]
        T['MTb'] = [sc.sb("ssd_MT%d" % i, [128, 128], BF16) for i in range(2)]
        return T

    def mixer_prompt(K, sq):
        sc = Scope()
        identf, identb, Win, Wout = K['identf'], K['identb'], K['Win'], K['Wout']
        stat, junk = K['stat'], K['junk']
        base = sq * S
        ntile = S // TQ
        far_nz = [bool(np.any(hc['far'][:, u, :].astype(np.float32) != 0)) for u in range(4)]
        nsel_t = S // 128
        caus = sc.sb("caus", [128, NSUB, TQ], BF16)
        far = sc.sb("far", [128, 4, TQ], BF16)
        cmpm = sc.sb("cmpm", [128, NCM, TQ], BF16)
        ovl = sc.sb("ovl", [128, 2, 64], BF16)
        P.dma(caus[:], C['caus'])
        P.dma(far[:], C['far'])
        P.dma(cmpm[:], C['cmpm'])
        P.dma(ovl[:], C['ovl'])
        KTs = [sc.sb("KTs%d" % g, [128, S], BF16) for g in range(2)]
        KTw = sc.sb("KTw", [128, 2, 1024], BF16)
        kcT = sc.sb("kcT", [128, 2, 256], BF16)
        Vs = sc.sb("Vs", [128, nsel_t, 2, 65], BF16)
        Vw = sc.sb("Vw", [128, 8, 2, 65], BF16)
        vca = sc.sb("vca", [128, 2, 2, 65], BF16)
        Hh = [sc.sb("Hh%d" % i, [128, 2, 256], BF16) for i in range(2)]
        raw = [sc.sb("raw%d" % i, [128, 16 + TQ], BF16) for i in range(2)]
        QTa = sc.sb("QTa", [128, 2, 8, TQ], BF16)
        hT = sc.sb("hT", [128, 8, TQ], BF16)
        hb = sc.sb("hb", [128, D], BF16)
        xin = sc.sb("xio", [128, D], F32)
        xres = xin
        kvst = [sc.sb("kvst", [128, 768], F32)] * 2
        gate = sc.sb("gate", [128, NSUB, 24], F32)
        zs = sc.sb("zs", [128, NSUB, 512], BF16)
        dtv = sc.sb("dtv", [128, NSUB, 8], F32)
        av = sc.sb("av", [128, NSUB, 8], F32)
        att = sc.sb("att", [128, NSUB, 512], F32)
        attb = sc.sb("attb", [128, 512], BF16)
        ymix = sc.sb("ymix", [128, NSUB, 512], BF16)
        PT = [sc.sb("PT%d" % i, [128, 2 * TQ], BF16) for i in range(3)]
        rz = sc.sb("rz", [128, 8], F32)
        coef = sc.sb("coef", [128, 8], F32)
        imp = sc.sb("imp", [128, NSUB, 2, 64], F32)
        impf = sc.sb("impf", [128, 64], F32)
        impw = sc.sb("impw", [128, 64], F32)
        m8 = sc.sb("m8", [128, 16], F32)
        msk = sc.sb("msk", [128, 64], F32)
        SELB = sc.sb("SELB", [128, NSUB, 2, 2, 96], F32)
        va = sc.sb("va", [128, NSUB, 128], F32)
        xp = [sc.sb("xp%d" % i, [128, 3 + TQ], F32) for i in range(2)]
        accf = sc.sb("accf", [128, TQ], F32)
        hist = sc.sb("hist", [128, 8, 3], F32)
        convo = sc.sb("convo", [128, 8, TQ], BF16)
        xtm1 = sc.sb("xtm", [128, 512], BF16)
        Btm1 = sc.sb("Btm", [128, 2, 128], BF16)
        hTf = sc.sb("hTf", [128, 512], F32)
        hTb = sc.sb("hTb", [128, 512], BF16)
        ostg = xin
        T = ssd_tmp(sc)
        psM = PSF[0:2]
        psS = PSF[2:4]
        psO = PSF[4:6]
        psI = PSF[6]
        psT = PSB
        mcnt = [0]

        def nextM():
            mcnt[0] += 1
            return psM[mcnt[0] % 2]

        for g in range(2):
            P.dma(KTs[g][64:128, :], C['kc_sel'])
            P.memset(KTs[g][0:64, :], 0.0, eng='pool')
        P.memset(KTw[:], 0.0, eng='pool')
        P.memset(kcT[0:64, :, :], 0.0)
        for g in range(2):
            P.dma(kcT[64:128, g, :], C['kc_cmp'])
        P.memset(Vs[:], 0.0, eng='pool')
        P.memset(Vs[:, :, :, 64:65], 1.0, eng='pool')
        P.memset(Vw[:], 0.0, eng='pool')
        P.memset(Vw[:, :, :, 64:65], 1.0, eng='pool')
        P.memset(vca[:], 0.0, eng='pool')
        P.memset(vca[:, :, :, 64:65], 1.0, eng='pool')
        for i in range(2):
            P.memset(Hh[i][:], 0.0)
            P.memset(raw[i][:], 0.0)
        P.memset(QTa[:], 0.0, eng='pool')
        P.memset(SELB[:], 0.0)
        P.memset(hist[:], 0.0)
        P.memset(hTf[:], 0.0)
        P.memset(hTb[:], 0.0)

        for ti in range(ntile):
            q0 = ti * TQ
            useG1 = q0 >= 2048
            for s in range(NSUB):
                r0 = base + q0 + s * 128
                P.dma(xin[:], x1_d[r0:r0 + 128, :])
                r = rms_rstd(stat, xin[:], 128, 4 * s, junk)
                P.ts(hb[:], xin[:], r, None, op0=ALU.mult)
                for k in range(8):
                    P.tr(psT[:, k * 128:(k + 1) * 128], hb[:, k * 128:(k + 1) * 128], identb[:])
                for k in range(8):
                    if k % 2 == 0:
                        P.act(hT[:, k, s * 128:(s + 1) * 128], psT[:, k * 128:(k + 1) * 128], AF.Copy,
                              scale=K['nwT'][:, k:k + 1])
                    else:
                        P.ts(hT[:, k, s * 128:(s + 1) * 128], psT[:, k * 128:(k + 1) * 128], K['nwT'][:, k:k + 1],
                             None, op0=ALU.mult)

            def proj_fm(col0, ncol):
                ps = nextM()
                for k in range(8):
                    P.mm(ps[0:ncol, 0:TQ], Win[:, k, col0:col0 + ncol], hT[:, k, :], start=(k == 0), stop=(k == 7))
                return ps

            for h in range(8):
                ps = proj_fm(Q0 + h * 64, 64)
                P.act(QTa[0:64, 0, h, :], ps[0:64, 0:TQ], AF.Copy, scale=0.125)
                if useG1:
                    P.ts(QTa[0:64, 1, h, :], ps[0:64, 0:TQ], 0.125, None, op0=ALU.mult)
            wc = q0 % 1024
            for g in range(2):
                ps = proj_fm(KS + g * 64, 64)
                P.copy(KTs[g][0:64, q0:q0 + TQ], ps[0:64, 0:TQ], eng='act')
                ps = proj_fm(KW + g * 64, 64)
                P.copy(KTw[0:64, g, wc:wc + TQ], ps[0:64, 0:TQ])
                P.dma(KTw[64:128, g, wc:wc + TQ], C['kc_win'][:, q0:q0 + TQ])
            for i, c0 in enumerate((KC, VC)):
                ps = proj_fm(c0, 128)
                P.copy(raw[i][:, 16:16 + TQ], ps[:, 0:TQ], eng=('act' if i == 0 else 'dve'))
            for c in range(8):
                ps = proj_fm(XB + c * 128, 128)
                X = xp[c % 2]
                P.copy(X[:, 0:3], hist[:, c, :], eng='pool')
                P.copy(X[:, 3:3 + TQ], ps[:, 0:TQ], eng='act')
                P.ts(accf[:], X[:, 0:TQ], K['cw'][:, c, 0:1], K['cb'][:, c:c + 1], op0=ALU.mult, op1=ALU.add)
                for tap in range(1, 4):
                    P.stt(accf[:], X[:, tap:tap + TQ], K['cw'][:, c, tap:tap + 1], accf[:], ALU.mult, ALU.add)
                P.act(convo[:, c, :], accf[:], AF.Silu)
                P.copy(hist[:, c, :], X[:, TQ:TQ + 3], eng='pool')
            for s in range(NSUB):
                kv = kvst[s % 2]
                tok0 = base + q0 + s * 128
                ps = nextM()
                for k in range(8):
                    P.mm(ps[:, :], hT[:, k, s * 128:(s + 1) * 128], Win[:, k, 512:1024], start=(k == 0), stop=(k == 7))
                P.copy(kv[:, 0:512], ps[:, :], eng='act')
                ps = nextM()
                for k in range(8):
                    P.mm(ps[:, 0:280], hT[:, k, s * 128:(s + 1) * 128], Win[:, k, 1024:1304], start=(k == 0),
                         stop=(k == 7))
                P.copy(kv[:, 512:768], ps[:, 0:256])
                P.act(gate[:, s, :], ps[:, 256:280], AF.Sigmoid)
                P.dma(o_cmp_p[tok0:tok0 + 128, :], kv[:, 0:256])
                P.dma(o_sel_p[tok0:tok0 + 128, :], kv[:, 256:512])
                tl = q0 + s * 128
                if tl >= S - 512:
                    P.dma(o_win_p[sq, tl - (S - 512):tl - (S - 512) + 128, :], kv[:, 512:768])
                P.copy(Vs[:, tl // 128, :, 0:64], kv[:, 384:512].rearrange("p (g d) -> p g d", g=2), eng='pool')
                P.copy(Vw[:, (tl // 128) % 8, :, 0:64], kv[:, 640:768].rearrange("p (g d) -> p g d", g=2), eng='pool')
                ps = nextM()
                for k in range(8):
                    P.mm(ps[:, :], hT[:, k, s * 128:(s + 1) * 128], Win[:, k, Z0:Z0 + 512], start=(k == 0), stop=(k == 7))
                P.act(zs[:, s, :], ps[:, :], AF.Silu)
                ps = nextM()
                for k in range(8):
                    P.mm(ps[:, 0:8], hT[:, k, s * 128:(s + 1) * 128], Win[:, k, DTO:DTO + 8], start=(k == 0),
                         stop=(k == 7))
                P.tt(dtv[:, s, :], ps[:, 0:8], K['dtb'][:], ALU.add)
                P.act(dtv[:, s, :], dtv[:, s, :], AF.Exp)
                P.act(dtv[:, s, :], dtv[:, s, :], AF.Ln, bias=1.0)
                P.tt(av[:, s, :], dtv[:, s, :], K['Aneg'][:], ALU.mult)
            jlo = max(0, q0 // 16 - 1)
            jhi = (q0 + TQ) // 16 - 1
            nb = jhi - jlo
            cst = 16 * jlo - q0 + 16
            for i in range(2):
                for g in range(2):
                    ps = nextM()
                    for r in range(32):
                        P.mm(ps[:, 0:nb], K['W1'][i][g * 64:(g + 1) * 64, r, :],
                             raw[i][g * 64:(g + 1) * 64, cst + r:cst + r + 16 * (nb - 1) + 1:16],
                             start=(r == 0), stop=(r == 31))
                    P.act(Hh[i][:, g, jlo:jhi], ps[:, 0:nb], AF.Silu, bias=K['b1'][i][:, 0:1])
                P.copy(raw[i][:, 0:16], raw[i][:, TQ:TQ + 16], eng='pool')
            for g in range(2):
                ps = nextM()
                P.mm(ps[0:64, 0:nb], K['W2'][0][:, :], Hh[0][:, g, jlo:jhi], start=True, stop=True)
                P.copy(kcT[0:64, g, jlo:jhi], ps[0:64, 0:nb])
                for c in range(jlo // 128, (jhi - 1) // 128 + 1):
                    ps = nextM()
                    P.mm(ps[:, 0:64], Hh[1][:, g, c * 128:(c + 1) * 128], K['W2'][1][:, :], start=True, stop=True)
                    P.copy(vca[:, c, g, 0:64], ps[:, 0:64], eng='act')
            P.dma(QTa[96:128, 0, :, :], C['alq'][:, :, q0:q0 + TQ])
            if useG1:
                P.dma(QTa[96:128, 1, :, :], C['alq'][:, :, q0:q0 + TQ])
            P.dma(va[:], C['va'][q0:q0 + TQ, :].rearrange("(s p) c -> p s c", p=128))

            def attend(hp, branch, tiles, first_branch):
                po = psO[hp % 2]
                n_t = len(tiles)
                for idx in range(n_t + 1):
                    if idx < n_t:
                        kt, G, mk, vv, ov = tiles[idx]
                        sb_ = psS[idx % 2]
                        P.mm(sb_[:, 0:2 * TQ], kt, QTa[:, G, 2 * hp:2 * hp + 2, :], start=True, stop=(mk is None))
                        if mk is not None:
                            P.mm(sb_[:, 0:2 * TQ], identb[:], mk.unsqueeze(1).broadcast_to([128, 2, TQ]),
                                 start=False, stop=True)
                        P.act(PT[idx % 3][:], sb_[:, 0:2 * TQ], AF.Exp)
                    if idx >= 1:
                        j = idx - 1
                        kt, G, mk, vv, ov = tiles[j]
                        pt = PT[j % 3]
                        for hl in range(2):
                            for s in range(NSUB):
                                P.mm(po[:, hl * 65 * NSUB + s * 65:hl * 65 * NSUB + (s + 1) * 65],
                                     pt[:, hl * TQ + s * 128:hl * TQ + (s + 1) * 128], vv,
                                     start=(j == 0 and hl == 0 and s == 0), stop=(j == n_t - 1), skip_group_check=True)
                        if ov is not None:
                            for hl in range(2):
                                for s in range(NSUB):
                                    P.mm(psI[:, hl * 64 * NSUB + s * 64:hl * 64 * NSUB + (s + 1) * 64],
                                         pt[:, hl * TQ + s * 128:hl * TQ + (s + 1) * 128], ov,
                                         start=(j == 0 and hl == 0 and s == 0), stop=(j == n_t - 1),
                                         skip_group_check=True)
                for hl in range(2):
                    h = 2 * hp + hl
                    o0 = hl * 65 * NSUB
                    zc = po[:, o0 + 64:o0 + 64 + 65 * (NSUB - 1) + 1:65]
                    rzh = rz[:, hl * NSUB:(hl + 1) * NSUB]
                    cfh = coef[:, hl * NSUB:(hl + 1) * NSUB]
                    P.ts(rzh, zc, 1e-30, None, op0=ALU.max)
                    P.add('dve', lambda e, rzh=rzh: e.reciprocal(rzh, rzh), [rzh], [rzh])
                    P.tt(cfh, rzh, gate[:, :, branch * 8 + h], ALU.mult)
                    for s in range(NSUB):
                        dst = att[:, s, h * 64:(h + 1) * 64]
                        src = po[:, o0 + s * 65:o0 + s * 65 + 64]
                        if first_branch:
                            P.ts(dst, src, coef[:, hl * NSUB + s:hl * NSUB + s + 1], None, op0=ALU.mult)
                        else:
                            P.stt(dst, src, coef[:, hl * NSUB + s:hl * NSUB + s + 1], dst, ALU.mult, ALU.add)

            for hp in range(4):
                g = hp // 2
                tiles = []
                for c in range(2):
                    if c == 1 and q0 < 2048:
                        continue
                    rel = (q0 - 2048 * c) // TQ
                    mk = cmpm[:, rel, :] if rel < NCM else None
                    tiles.append((kcT[:, g, c * 128:(c + 1) * 128], 0, mk, vca[:, c, g, :], ovl[:, c, :]))
                attend(hp, 0, tiles, True)
                for hl in range(2):
                    for s in range(NSUB):
                        dst = imp[:, s, g, :]
                        src = psI[:, hl * 64 * NSUB + s * 64:hl * 64 * NSUB + (s + 1) * 64]
                        rzc = rz[:, hl * NSUB + s:hl * NSUB + s + 1]
                        if hp % 2 == 0 and hl == 0:
                            P.ts(dst, src, rzc, None, op0=ALU.mult)
                        else:
                            P.stt(dst, src, rzc, dst, ALU.mult, ALU.add)
            for s in range(NSUB):
                for g in range(2):
                    P.tt(impw[:], imp[:, s, g, :], va[:, s, 0:64], ALU.mult)
                    P.tt(impf[:], impw[:], va[:, s, 64:128], ALU.add)
                    P.add('dve', lambda e: e.max(m8[:, 0:8], impf[:]), [impf[:]], [m8[:, 0:8]])
                    P.add('dve', lambda e: e.match_replace(impw[:], m8[:, 0:8], impf[:], -1e30),
                          [m8[:, 0:8], impf[:]], [impw[:]])
                    P.add('dve', lambda e: e.max(m8[:, 8:16], impw[:]), [impw[:]], [m8[:, 8:16]])
                    P.ts(msk[:], impf[:], m8[:, 15:16], None, op0=ALU.is_ge)
                    P.ts(SELB[:, s, g, :, 64:96], msk[:].rearrange("p (a b) -> p a b", a=2), -1.0, -NEGM,
                         op0=ALU.add, op1=ALU.mult)
            for hp in range(4):
                g = hp // 2
                tiles = []
                for i in range(4 + NSUB):
                    k0 = q0 - 512 + 128 * i
                    if k0 < 0:
                        continue
                    mk = far[:, (4 - i) - 1, :] if i < 4 else caus[:, i - 4, :]
                    if i < 4 and not far_nz[(4 - i) - 1]:
                        mk = None
                    col = k0 % 1024
                    tiles.append((KTw[:, g, col:col + 128], 0, mk, Vw[:, (k0 // 128) % 8, g, :], None))
                attend(hp, 2, tiles, False)
            for s in range(NSUB):
                for g in range(2):
                    for G in range(2 if useG1 else 1):
                        pst = nextM()
                        P.tr(pst[0:96, 0:128], SELB[:, s, g, G, :], identf[:])
                        P.copy(QTa[64:96, G, g * 4:(g + 1) * 4, s * 128:(s + 1) * 128],
                               pst[64:96, 0:128].unsqueeze(1).broadcast_to([32, 4, 128]))
            for hp in range(4):
                g = hp // 2
                tiles = []
                for t in range(q0 // 128 + NSUB):
                    mk = caus[:, (t * 128 - q0) // 128, :] if t * 128 >= q0 else None
                    tiles.append((KTs[g][:, t * 128:(t + 1) * 128], t // 16, mk, Vs[:, t, g, :], None))
                attend(hp, 1, tiles, False)
            for s in range(NSUB):
                for c in range(4):
                    P.tr(psT[:, c * 128:(c + 1) * 128], convo[:, c, s * 128:(s + 1) * 128], identb[:])
                for g in range(2):
                    P.tr(psT[:, 512 + g * 128:512 + (g + 1) * 128], convo[:, 4 + g, s * 128:(s + 1) * 128], identb[:])
                P.copy(xtm1[:], psT[:, 0:512], eng='act')
                P.copy(Btm1[:], psT[:, 512:768].rearrange("p (g n) -> p g n", g=2))
                ssd_chunk(K, T, 128,
                          [convo[:, 4 + g, s * 128:(s + 1) * 128] for g in range(2)],
                          [convo[:, 6 + g, s * 128:(s + 1) * 128] for g in range(2)],
                          xtm1[:], [Btm1[:, g, :] for g in range(2)], dtv[:, s, :], av[:, s, :], zs[:, s, :],
                          hTf, hTb, ymix[:, s, :])
            mixT = hT
            for s in range(NSUB):
                P.copy(attb[:], att[:, s, :], eng='pool')
                for c in range(4):
                    P.tr(psT[:, c * 128:(c + 1) * 128], attb[:, c * 128:(c + 1) * 128], identb[:])
                for c in range(4):
                    P.tr(psT[:, 512 + c * 128:512 + (c + 1) * 128], ymix[:, s, c * 128:(c + 1) * 128], identb[:])
                for c in range(4):
                    P.copy(mixT[:, c, s * 128:(s + 1) * 128], psT[:, c * 128:(c + 1) * 128], eng='act')
                for c in range(4):
                    P.ts(mixT[:, 4 + c, s * 128:(s + 1) * 128], psT[:, 512 + c * 128:512 + (c + 1) * 128],
                         K['snw'][:, c:c + 1], None, op0=ALU.mult)
            for s in range(NSUB):
                r0 = base + q0 + s * 128
                P.dma(xres[:], x1_d[r0:r0 + 128, :])
                for h2 in range(2):
                    ps = nextM()
                    for k in range(8):
                        P.mm(ps[:, :], mixT[:, k, s * 128:(s + 1) * 128], Wout[:, k, h2 * 512:(h2 + 1) * 512],
                             start=(k == 0), stop=(k == 7))
                    P.tt(xres[:, h2 * 512:(h2 + 1) * 512], ps[:, :], xres[:, h2 * 512:(h2 + 1) * 512], ALU.add)
                P.dma(x2_d[r0:r0 + 128, :], xres[:])
                if cfg.dbg:
                    P.dma(dbg['att'][r0:r0 + 128, :], att[:, s, :])
                    P.dma(dbg['x2'][r0:r0 + 128, :], xres[:])
        pst = PSF[0]
        for c in range(8):
            P.tr(pst[0:3, c * 128:(c + 1) * 128] if False else PSF[c % 2][0:3, 0:128], hist[:, c, :], identf[:])
            P.copy(ostg[0:3, c * 128:(c + 1) * 128], PSF[c % 2][0:3, 0:128])
        P.dma(o_conv_p[sq, :, :], ostg[0:3, :])
        for c in range(4):
            ps = PSF[c % 2]
            P.tr(ps[:, 0:128], hTf[:, c * 128:(c + 1) * 128], identf[:])
            P.copy(ostg[:, c * 128:(c + 1) * 128], ps[:, 0:128])
            P.dma(o_ssm_p[sq, c * 128:(c + 1) * 128, :], ostg[:, c * 128:(c + 1) * 128])
        sc.close()

    def mixer_sample(K):
        sc = Scope()
        identf, identb, Win, Wout = K['identf'], K['identb'], K['Win'], K['Wout']
        stat, junk = K['stat'], K['junk']
        NB = cfg.nseq_s
        NPG = cfg.npages
        NKT = cfg.nkt_s
        NCH = (cfg.ncmp_s + 127) // 128
        NCB = cfg.ncmp_s
        NSL = cfg.nsel_s
        NG = (NSL + 31) // 32
        caus8 = sc.sb("caus8", [128, 32], BF16)
        far8 = sc.sb("far8", [128, 32], BF16)
        ovls = sc.sb("ovls", [128, NCH, NSL], BF16)
        vas = sc.sb("vas", [NQS, 2, NSL], F32)
        iot = sc.sb("iot", [128, NPG], F32)
        pidxf = sc.sb("pidxf", [128, NPG], F32)
        P.dma(caus8[:], C['caus8'])
        P.dma(far8[:], C['far8'])
        cmpms = sc.sb("cmpms", [128, 32], BF16)
        P.dma(cmpms[:], C['cmpm_s'])
        P.dma(ovls[:], C['ovl_s'])
        P.dma(vas[:], C['va_s'])
        P.dma(iot[:], iota_p)
        hTs = sc.sb("hTs", [128, 8, NS], BF16)
        hb = sc.sb("hb_s", [128, D], BF16)
        xin = sc.sb("xin_s", [128, D], F32)
        qTs = sc.sb("qTs", [128, 8, NS], BF16)
        ksTs = sc.sb("ksTs", [128, 2, NS], BF16)
        kwTs = sc.sb("kwTs", [128, 2, NS], BF16)
        xbcT = sc.sb("xbcT", [128, 8, NS], F32)
        histT = sc.sb("histT", [128, 8, NB * 3], F32)
        xps = sc.sb("xps", [128, 8, 11], F32)
        accs = sc.sb("accs", [128, NQS], F32)
        convs = sc.sb("convs", [128, 8, NQS], BF16)
        ub = sc.sb("ub", [NQS, 1312], F32)
        pidx = sc.sb("pidx", [128, NPG], I32)
        pidxu = sc.sb("pidxu", [128, NPG], U32)
        NPGB = 6
        PG = [sc.sb("PG%d" % i, [128, 256], F32) for i in range(NPGB)]
        Hs = [sc.sb("Hs%d" % i, [128, 2, NCH * 128], BF16) for i in range(2)]
        kcTs = sc.sb("kcTs", [128, 2, NCH * 128], BF16)
        vcs = sc.sb("vcs", [128, NCH, 2, 65], BF16)
        KTs = sc.sb("KTss", [128, 2, NKT * 128], BF16)
        Vs = sc.sb("Vss", [128, NKT, 2, 65], BF16)
        KTw = sc.sb("KTws", [128, 2, 640], BF16)
        Vw = sc.sb("Vws", [128, 5, 2, 65], BF16)
        QTa = sc.sb("QTas", [128, NG, 2, 32], BF16)
        PT = [sc.sb("PTs%d" % i, [128, 32], BF16) for i in range(3)]
        gate = sc.sb("gate_s", [NQS, 24], F32)
        zs = sc.sb("zs_s", [NQS, 512], BF16)
        dtv = sc.sb("dtv_s", [NQS, 8], F32)
        av = sc.sb("av_s", [NQS, 8], F32)
        att = sc.sb("att_s", [NQS, 512], F32)
        attb = sc.sb("attb_s", [NQS, 512], BF16)
        ymix = sc.sb("ymix_s", [NQS, 512], BF16)
        rz = sc.sb("rz_s", [NQS, 8], F32)
        coef = sc.sb("coef_s", [NQS, 8], F32)
        imp = sc.sb("imp_s", [NQS, 2, NSL], F32)
        NSLP = NG * 32
        impf = sc.sb("impf_s", [NQS, NSL], F32)
        impw = sc.sb("impw_s", [NQS, NSL], F32)
        m8 = sc.sb("m8_s", [NQS, 16], F32)
        msk = sc.sb("msk_s", [NQS, NSLP], F32)
        SELB = sc.sb("SELB_s", [NQS, NG, 96], F32)
        xtm = sc.sb("xtm_s", [NQS, 512], BF16)
        Btm = sc.sb("Btm_s", [NQS, 2, 128], BF16)
        hTf = sc.sb("hTf_s", [128, 512], F32)
        hTb = sc.sb("hTb_s", [128, 512], BF16)
        mixT = sc.sb("mixT_s", [128, 8, NQS], BF16)
        xres = sc.sb("xres_s", [NQS, D], F32)
        ostg = xin
        T = ssd_tmp(sc)
        psM = PSF[0:2]
        psS = PSF[2:4]
        psO = PSF[4]
        psI = PSF[5:7]
        psT = PSB
        mcnt = [0]

        def nextM():
            mcnt[0] += 1
            return psM[mcnt[0] % 2]

        for t0 in range(0, NS, 128):
            L = min(128, NS - t0)
            P.dma(xin[0:L, :], x1_d[NTP + t0:NTP + t0 + L, :])
            r = rms_rstd(stat, xin[0:L, :], L, 0, junk)
            P.ts(hb[0:L, :], xin[0:L, :], r, None, op0=ALU.mult)
            for k in range(8):
                P.tr(psT[:, k * 128:k * 128 + L], hb[0:L, k * 128:(k + 1) * 128], identb[0:L, 0:L])
            for k in range(8):
                P.ts(hTs[:, k, t0:t0 + L], psT[:, k * 128:k * 128 + L], K['nwT'][:, k:k + 1], None, op0=ALU.mult)

        def proj_fm(col0, ncol):
            ps = nextM()
            for k in range(8):
                P.mm(ps[0:ncol, 0:NS], Win[:, k, col0:col0 + ncol], hTs[:, k, :], start=(k == 0), stop=(k == 7))
            return ps

        for h in range(8):
            ps = proj_fm(Q0 + h * 64, 64)
            P.act(qTs[0:64, h, :], ps[0:64, 0:NS], AF.Copy, scale=0.125)
        for g in range(2):
            ps = proj_fm(KS + g * 64, 64)
            P.copy(ksTs[0:64, g, :], ps[0:64, 0:NS])
            ps = proj_fm(KW + g * 64, 64)
            P.copy(kwTs[0:64, g, :], ps[0:64, 0:NS])
        for c in range(8):
            ps = proj_fm(XB + c * 128, 128)
            P.copy(xbcT[:, c, :], ps[:, 0:NS], eng='act')
        for r0 in range(0, NB * 3, 96):
            L = min(96, NB * 3 - r0)
            P.dma(xin[0:L, :], state_conv[r0:r0 + L, :])
            for c in range(8):
                ps = nextM()
                P.tr(ps[:, 0:L], xin[0:L, c * 128:(c + 1) * 128], identf[0:L, 0:L])
                P.copy(histT[:, c, r0:r0 + L], ps[:, 0:L])

        P.memset(KTs[0:64, :, :], 0.0, eng='pool')
        P.memset(KTw[0:64, :, :], 0.0, eng='pool')
        for g in range(2):
            P.dma(KTw[64:128, g, :], C['kc_win_s'])
            P.dma(kcTs[64:128, g, :], C['kc_cmp_s'])
        P.memset(kcTs[0:64, :, :], 0.0)
        P.memset(Vs[:], 0.0, eng='pool')
        P.memset(Vs[:, 0:NPG, :, 64:65], 1.0, eng='pool')
        P.memset(Vs[0:NQS, NPG, :, 64:65], 1.0, eng='pool')
        P.memset(Vw[:], 0.0, eng='pool')
        P.memset(Vw[:, 0:4, :, 64:65], 1.0, eng='pool')
        P.memset(Vw[0:NQS, 4, :, 64:65], 1.0, eng='pool')
        P.memset(vcs[:], 0.0, eng='pool')
        P.memset(vcs[:, :, :, 64:65], 1.0, eng='pool')
        for i in range(2):
            P.memset(Hs[i][:], 0.0)
        P.memset(QTa[:], 0.0)
        for G in range(NG):
            P.dma(QTa[96:128, G, :, :], C['alq_s'])
        P.memset(SELB[:], 0.0)
        P.memset(msk[:], 0.0)

        def gather_page(dst, cache, p):
            idx = pidxu[:, p:p + 1]
            P.add('pool', lambda e: e.indirect_dma_start(out=dst, out_offset=None, in_=cache,
                                                          in_offset=bass.IndirectOffsetOnAxis(ap=idx, axis=0)),
                  [cache, idx], [dst], is_dma=True)

        def attend_s(g, tiles, branch, first_branch, with_imp):
            n_t = len(tiles)
            for idx in range(n_t + 1):
                if idx < n_t:
                    kt, rq, mk, vv, ov = tiles[idx]
                    sb_ = psS[idx % 2]
                    P.mm(sb_[:, 0:32], kt, rq, start=True, stop=(mk is None))
                    if mk is not None:
                        P.mm(sb_[:, 0:32], identb[:], mk, start=False, stop=True)
                    P.act(PT[idx % 3][:], sb_[:, 0:32], AF.Exp)
                if idx >= 1:
                    j = idx - 1
                    kt, rq, mk, vv, ov = tiles[j]
                    pt = PT[j % 3]
                    for hh in range(4):
                        P.mm(psO[0:NQS, hh * 65:(hh + 1) * 65], pt[:, hh * NQS:(hh + 1) * NQS], vv,
                             start=(j == 0 and hh == 0), stop=(j == n_t - 1), skip_group_check=True)
                    if with_imp:
                        for hh in range(4):
                            P.mm(psI[hh // 2][0:NQS, (hh % 2) * NSL:(hh % 2 + 1) * NSL], pt[:, hh * NQS:(hh + 1) * NQS], ov,
                                 start=(j == 0 and hh % 2 == 0), stop=(j == n_t - 1), skip_group_check=True)
            zc = psO[0:NQS, 64:64 + 65 * 3 + 1:65]
            P.ts(rz[:, 0:4], zc, 1e-30, None, op0=ALU.max)
            P.add('dve', lambda e: e.reciprocal(rz[:, 0:4], rz[:, 0:4]), [rz[:, 0:4]], [rz[:, 0:4]])
            P.tt(coef[:, 0:4], rz[:, 0:4], gate[:, branch * 8 + g * 4:branch * 8 + g * 4 + 4], ALU.mult)
            for hh in range(4):
                h = g * 4 + hh
                dst = att[:, h * 64:(h + 1) * 64]
                if first_branch:
                    P.ts(dst, psO[0:NQS, hh * 65:hh * 65 + 64], coef[:, hh:hh + 1], None, op0=ALU.mult)
                else:
                    P.stt(dst, psO[0:NQS, hh * 65:hh * 65 + 64], coef[:, hh:hh + 1], dst, ALU.mult, ALU.add)
            if with_imp:
                for hh in range(4):
                    src = psI[hh // 2][0:NQS, (hh % 2) * NSL:(hh % 2 + 1) * NSL]
                    if hh == 0:
                        P.ts(imp[:, g, :], src, rz[:, hh:hh + 1], None, op0=ALU.mult)
                    else:
                        P.stt(imp[:, g, :], src, rz[:, hh:hh + 1], imp[:, g, :], ALU.mult, ALU.add)

        for b in range(NB):
            tb = b * NQS
            for gi, (c0, cwid, dstt, d0) in enumerate(((512, 512, ub, 0), (1024, 280, ub, 512), (Z0, 512, ub, 792),
                                                       (DTO, 8, ub, 1304), (XB, 512, xres, 0), (XB + 512, 512, xres, 512))):
                ps = nextM()
                for k in range(8):
                    P.mm(ps[0:NQS, 0:cwid], hTs[:, k, tb:tb + NQS], Win[:, k, c0:c0 + cwid], start=(k == 0), stop=(k == 7))
                P.copy(dstt[:, d0:d0 + cwid], ps[0:NQS, 0:cwid], eng=('act' if gi % 2 else 'dve'))
            P.dma(o_cmp_s[tb:tb + NQS, :], ub[:, 0:256])
            P.dma(o_sel_s[tb:tb + NQS, :], ub[:, 256:512])
            P.dma(o_win_s[b, 512 - NQS:512, :], ub[:, 512:768])
            P.dma(o_win_s[b, 0:512 - NQS, :], cache_win[b, NQS:512, :])
            P.dma(o_conv_s[b, :, :], xres[NQS - 3:NQS, :])
            P.act(gate[:], ub[:, 768:792], AF.Sigmoid)
            P.act(zs[:], ub[:, 792:1304], AF.Silu)
            P.tt(dtv[:], ub[:, 1304:1312], K['dtb'][0:NQS, :], ALU.add)
            P.copy(xps[:, :, 0:3], histT[:, :, b * 3:(b + 1) * 3], eng='pool')
            P.copy(xps[:, :, 3:11], xbcT[:, :, tb:tb + NQS], eng='pool')
            for c in range(8):
                P.ts(accs[:], xps[:, c, 0:NQS], K['cw'][:, c, 0:1], K['cb'][:, c:c + 1], op0=ALU.mult, op1=ALU.add)
                for tap in range(1, 4):
                    P.stt(accs[:], xps[:, c, tap:tap + NQS], K['cw'][:, c, tap:tap + 1], accs[:], ALU.mult, ALU.add)
                P.act(convs[:, c, :], accs[:], AF.Silu)
            P.act(dtv[:], dtv[:], AF.Exp)
            P.act(dtv[:], dtv[:], AF.Ln, bias=1.0)
            P.tt(av[:], dtv[:], K['Aneg'][0:NQS, :], ALU.mult)
            P.dma(pidx[:], page_table[b, :].partition_broadcast(128))
            P.ts(pidxf[:], pidx[:], 128.0, None, op0=ALU.mult)
            P.tt(pidxu[:], pidxf[:], iot[:], ALU.add)
            for g in range(2):
                P.copy(QTa[0:64, :, g, :].rearrange("p a (h t) -> p a h t", h=4),
                       qTs[0:64, g * 4:(g + 1) * 4, tb:tb + NQS].unsqueeze(1).broadcast_to([64, NG, 4, NQS]))
            for p in range(NPG):
                pg = PG[p % NPGB]
                gather_page(pg[:], cache_cmp, p)
                for i in range(2):
                    ps = nextM()
                    P.tr(ps[:, 0:128], pg[:, i * 128:(i + 1) * 128], identf[:])
                    P.copy(KTs[:, i, 0:NPG * 128].rearrange("p (r c) -> p r c", r=16)[:, :, 8 * p:8 * p + 8],
                           ps[:, 0:128].rearrange("p (c r) -> p r c", r=16), eng=('act' if i == 0 else 'dve'))
            for i in range(2):
                for g in range(2):
                    for j0 in range(0, NCB, 512):
                        nb = min(512, NCB - j0)
                        ps = nextM()
                        for r in range(32):
                            st_ = (r % 16) * (NPG * 8) + j0 + r // 16
                            P.mm(ps[:, 0:nb], K['W1'][i][g * 64:(g + 1) * 64, r, :],
                                 KTs[g * 64:(g + 1) * 64, i, st_:st_ + nb],
                                 start=(r == 0), stop=(r == 31))
                        P.act(Hs[i][:, g, j0:j0 + nb], ps[:, 0:nb], AF.Silu, bias=K['b1'][i][:, 0:1])
            for g in range(2):
                for j0 in range(0, NCB, 512):
                    nb = min(512, NCB - j0)
                    ps = nextM()
                    P.mm(ps[0:64, 0:nb], K['W2'][0][:, :], Hs[0][:, g, j0:j0 + nb], start=True, stop=True)
                    P.copy(kcTs[0:64, g, j0:j0 + nb], ps[0:64, 0:nb])
                for c in range(NCH):
                    ps = nextM()
                    P.mm(ps[:, 0:64], Hs[1][:, g, c * 128:(c + 1) * 128], K['W2'][1][:, :], start=True, stop=True)
                    P.copy(vcs[:, c, g, 0:64], ps[:, 0:64], eng='act')
            for g in range(2):
                tiles = [(kcTs[:, g, c * 128:(c + 1) * 128], QTa[:, 0, g, :], (cmpms[:] if c == NCH - 1 else None),
                          vcs[:, c, g, :], ovls[:, c, :]) for c in range(NCH)]
                attend_s(g, tiles, 0, True, True)
            for g in range(2):
                P.tt(impw[:], imp[:, g, :], vas[:, 0, :], ALU.mult)
                P.tt(impf[:], impw[:], vas[:, 1, :], ALU.add)
                P.add('dve', lambda e: e.max(m8[:, 0:8], impf[:]), [impf[:]], [m8[:, 0:8]])
                P.add('dve', lambda e: e.match_replace(impw[:], m8[:, 0:8], impf[:], -1e30),
                      [m8[:, 0:8], impf[:]], [impw[:]])
                P.add('dve', lambda e: e.max(m8[:, 8:16], impw[:]), [impw[:]], [m8[:, 8:16]])
                P.ts(msk[:, 0:NSL], impf[:], m8[:, 15:16], None, op0=ALU.is_ge)
                P.ts(SELB[:, :, 64:96], msk[:].rearrange("p (a b) -> p a b", a=NG), -1.0, -NEGM,
                     op0=ALU.add, op1=ALU.mult)
                for G in range(NG):
                    pst = nextM()
                    P.tr(pst[0:96, 0:NQS], SELB[:, G, :], identf[0:NQS, 0:NQS])
                    P.copy(QTa[64:96, G, g, :].rearrange("p (h t) -> p h t", h=4),
                           pst[64:96, 0:NQS].unsqueeze(1).broadcast_to([32, 4, NQS]))
            for g in range(2):
                P.dma(KTs[64:128, g, :], C['kc_sel_s'])
            for p in range(NPG):
                pg = PG[p % NPGB]
                gather_page(pg[:], cache_sel, p)
                for g in range(2):
                    ps = nextM()
                    P.tr(ps[0:64, 0:128], pg[:, g * 64:(g + 1) * 64], identf[:])
                    P.copy(KTs[0:64, g, p * 128:(p + 1) * 128], ps[0:64, 0:128], eng=('act' if g == 0 else 'dve'))
                P.copy(Vs[:, p, :, 0:64], pg[:, 128:256].rearrange("p (g d) -> p g d", g=2), eng='act')
            for g in range(2):
                P.copy(KTs[0:64, g, NPG * 128:NPG * 128 + NQS], ksTs[0:64, g, tb:tb + NQS])
            P.copy(Vs[0:NQS, NPG, :, 0:64], ub[:, 384:512].rearrange("p (g d) -> p g d", g=2))
            for g in range(2):
                tiles = []
                for t in range(NKT):
                    mk = caus8[:] if t == NPG else None
                    tiles.append((KTs[:, g, t * 128:(t + 1) * 128], QTa[:, t // 16, g, :], mk, Vs[:, t, g, :], None))
                attend_s(g, tiles, 1, False, False)
            for t in range(4):
                pg = PG[t % NPGB]
                P.dma(pg[:], cache_win[b, t * 128:(t + 1) * 128, :])
                for g in range(2):
                    ps = nextM()
                    P.tr(ps[0:64, 0:128], pg[:, g * 64:(g + 1) * 64], identf[:])
                    P.copy(KTw[0:64, g, t * 128:(t + 1) * 128], ps[0:64, 0:128], eng=('act' if g == 0 else 'dve'))
                P.copy(Vw[:, t, :, 0:64], pg[:, 128:256].rearrange("p (g d) -> p g d", g=2), eng='act')
            for g in range(2):
                P.copy(KTw[0:64, g, 512:512 + NQS], kwTs[0:64, g, tb:tb + NQS])
            P.copy(Vw[0:NQS, 4, :, 0:64], ub[:, 640:768].rearrange("p (g d) -> p g d", g=2))
            for g in range(2):
                tiles = []
                for t in range(5):
                    mk = far8[:] if t == 0 else (caus8[:] if t == 4 else None)
                    tiles.append((KTw[:, g, t * 128:(t + 1) * 128], QTa[:, 0, g, :], mk, Vw[:, t, g, :], None))
                attend_s(g, tiles, 2, False, False)
            for c in range(4):
                ps = nextM()
                P.dma(ostg[:, 0:128], state_ssm[b, c * 128:(c + 1) * 128, :])
                P.tr(ps[:, 0:128], ostg[:, 0:128], identf[:])
                P.copy(hTf[:, c * 128:(c + 1) * 128], ps[:, 0:128])
            P.copy(hTb[:], hTf[:], eng='pool')
            for c in range(4):
                P.tr(psT[0:NQS, c * 128:(c + 1) * 128], convs[:, c, :], identb[:])
            for g in range(2):
                P.tr(psT[0:NQS, 512 + g * 128:512 + (g + 1) * 128], convs[:, 4 + g, :], identb[:])
            P.copy(xtm[:], psT[0:NQS, 0:512], eng='act')
            P.copy(Btm[:], psT[0:NQS, 512:768].rearrange("p (g n) -> p g n", g=2))
            ssd_chunk(K, T, NQS,
                      [convs[:, 4 + g, :] for g in range(2)],
                      [convs[:, 6 + g, :] for g in range(2)],
                      xtm[:], [Btm[:, g, :] for g in range(2)], dtv[:], av[:], zs[:], hTf, hTb, ymix[:])
            for c in range(4):
                ps = nextM()
                P.tr(ps[:, 0:128], hTf[:, c * 128:(c + 1) * 128], identf[:])
                P.copy(ostg[:, c * 128:(c + 1) * 128], ps[:, 0:128])
                P.dma(o_ssm_s[b, c * 128:(c + 1) * 128, :], ostg[:, c * 128:(c + 1) * 128])
            P.copy(attb[:], att[:], eng='pool')
            for c in range(4):
                P.tr(psT[:, c * 128:c * 128 + NQS], attb[:, c * 128:(c + 1) * 128], identb[0:NQS, 0:NQS])
            for c in range(4):
                P.tr(psT[:, 512 + c * 128:512 + c * 128 + NQS], ymix[:, c * 128:(c + 1) * 128], identb[0:NQS, 0:NQS])
            for c in range(4):
                P.copy(mixT[:, c, :], psT[:, c * 128:c * 128 + NQS], eng='act')
                P.ts(mixT[:, 4 + c, :], psT[:, 512 + c * 128:512 + c * 128 + NQS], K['snw'][:, c:c + 1], None,
                     op0=ALU.mult)
            P.dma(xres[:], x1_d[NTP + tb:NTP + tb + NQS, :])
            for h2 in range(2):
                ps = nextM()
                for k in range(8):
                    P.mm(ps[0:NQS, :], mixT[:, k, :], Wout[:, k, h2 * 512:(h2 + 1) * 512], start=(k == 0), stop=(k == 7))
                P.tt(xres[:, h2 * 512:(h2 + 1) * 512], ps[0:NQS, :], xres[:, h2 * 512:(h2 + 1) * 512], ALU.add)
            P.dma(x2_d[NTP + tb:NTP + tb + NQS, :], xres[:])
            if cfg.dbg:
                P.dma(dbg['att'][NTP + tb:NTP + tb + NQS, :], att[:])
                P.dma(dbg['x2'][NTP + tb:NTP + tb + NQS, :], xres[:])
        sc.close()

    cur = x_in
    if 'ffn1' in cfg.stages:
        ffn_phase("ffn1", x_in, x1_d, False)
    if 'mixp' in cfg.stages or 'mixs' in cfg.stages:
        mixer_phase()
    if 'mixs' not in cfg.stages and NS > 0:
        P.dma(x2_d[NTP:NT, :], x1_d[NTP:NT, :])
    if 'mixp' not in cfg.stages and NTP > 0:
        P.dma(x2_d[0:NTP, :], x1_d[0:NTP, :])
    if 'ffn2' in cfg.stages:
        ffn_phase("ffn2", x2_d, y_out, True)
    P.finalize(top)
    top.close()
    return nc, P, hc


WNAMES = ("ffn1_wg", "ffn1_wu", "ffn1_wd", "ffn2_wg", "ffn2_wu", "ffn2_wd", "norm_ffn1", "norm_ffn2", "norm_mix",
          "w_in", "w_out", "cmp_pe_k", "cmp_w1_k", "cmp_w2_k", "cmp_pe_v", "cmp_w1_v", "cmp_w2_v", "conv_w",
          "conv_b", "dt_bias", "a_log", "d_skip", "ssm_norm")


def make_in_maps(cfg, hc, inp, ncores):
    nsp, nss = cfg.nseq_p, cfg.nseq_s
    base = {}
    for n in WNAMES:
        base[n] = np.ascontiguousarray(inp[n][0], dtype=np.float32)
    base['norm_final'] = np.ascontiguousarray(inp['norm_final'], dtype=np.float32)
    for k, v in hc.items():
        base['c_' + k] = v
    base['cache_cmp'] = np.ascontiguousarray(inp['cache_cmp'][0]).reshape(-1, 256)
    base['cache_sel'] = np.ascontiguousarray(inp['cache_sel'][0]).reshape(-1, 256)
    base['iota_p'] = np.ascontiguousarray(np.tile(np.arange(128, dtype=np.float32)[:, None], (1, cfg.npages)))
    maps = []
    for c in range(ncores):
        m = dict(base)
        xp = inp['x_prompt'][c * nsp:(c + 1) * nsp].reshape(-1, D)
        xs = inp['x_sample'][c * nss:(c + 1) * nss].reshape(-1, D)
        m['x_in'] = np.ascontiguousarray(np.concatenate([xp, xs], axis=0))
        m['cache_win'] = np.ascontiguousarray(inp['cache_win'][0, c * nss:(c + 1) * nss]).reshape(nss, 512, 256)
        m['state_conv'] = np.ascontiguousarray(inp['state_conv'][0, c * nss:(c + 1) * nss]).reshape(nss * 3, 1024)
        m['state_ssm'] = np.ascontiguousarray(inp['state_ssm'][0, c * nss:(c + 1) * nss]).reshape(nss, 512, 128)
        m['page_table'] = np.ascontiguousarray(inp['page_table'][c * nss:(c + 1) * nss]).astype(np.int32)
        maps.append(m)
    return maps


def assemble(cfg, results, ncores):
    nsp, nss, S = cfg.nseq_p, cfg.nseq_s, cfg.seq
    cat = lambda name: [np.asarray(r[name]) for r in results]
    y = cat('y_out')
    y_p = np.concatenate([a[:cfg.ntok_p].reshape(nsp, S, D) for a in y], 0)
    y_s = np.concatenate([a[cfg.ntok_p:].reshape(nss, NQS, D) for a in y], 0)
    ncp = np.concatenate([a.reshape(nsp, S, 2, 2, 64) for a in cat('o_cmp_p')], 0)[None]
    nsl = np.concatenate([a.reshape(nsp, S, 2, 2, 64) for a in cat('o_sel_p')], 0)[None]
    nwp = np.concatenate([a.reshape(nsp, 512, 2, 2, 64) for a in cat('o_win_p')], 0)[None]
    ncv = np.concatenate([a.reshape(nsp, 3, 1024) for a in cat('o_conv_p')], 0)[None]
    nsm = np.concatenate([a.reshape(nsp, 8, 64, 128) for a in cat('o_ssm_p')], 0)[None]
    scp = np.concatenate([a.reshape(nss, NQS, 2, 2, 64) for a in cat('o_cmp_s')], 0)[None]
    ssl = np.concatenate([a.reshape(nss, NQS, 2, 2, 64) for a in cat('o_sel_s')], 0)[None]
    swp = np.concatenate([a.reshape(nss, 512, 2, 2, 64) for a in cat('o_win_s')], 0)[None]
    scv = np.concatenate([a.reshape(nss, 3, 1024) for a in cat('o_conv_s')], 0)[None]
    ssm = np.concatenate([a.reshape(nss, 8, 64, 128) for a in cat('o_ssm_s')], 0)[None]
    return (y_p, y_s, ncp, nsl, nwp, ncv, nsm, scp, ssl, swp, scv, ssm)


def kernel(**inputs):
    inputs = {k: np.asarray(v) for k, v in inputs.items()}
    B, S = inputs['x_prompt'].shape[0], inputs['x_prompt'].shape[1]
    NBS = inputs['x_sample'].shape[0]
    npages = inputs['page_table'].shape[1]
    ncores = 8
    cfg = Cfg(nseq_p=B // ncores, seq=S, nseq_s=NBS // ncores, past=npages * 128,
              n_phys=inputs['cache_cmp'].shape[1])
    nc, P, hc = build(cfg)
    maps = make_in_maps(cfg, hc, inputs, ncores)
    res = run_bass_kernel_spmd(nc, maps, core_ids=list(range(ncores)))
    outs = assemble(cfg, res.results, ncores)
    return tuple(np.ascontiguousarray(o, dtype=np.float32) for o in outs)
```

```python
import numpy as np
import ml_dtypes
from contextlib import ExitStack
import concourse.bass as bass
import concourse.mybir as mybir
from concourse.bass_utils import run_bass_kernel_spmd


F32 = mybir.dt.float32
BF16 = mybir.dt.bfloat16
I32 = mybir.dt.int32
U32 = mybir.dt.uint32
AF = mybir.ActivationFunctionType
ALU = mybir.AluOpType
AX = mybir.AxisListType

DMA_K = 6
COMPUTE = ('pe', 'act', 'dve', 'pool')


class Op:
    __slots__ = ('stream', 'fn', 'is_dma', 'deps', 'signal', 'sem', 'val', 'waits', 'clock', 'idx', 'tag')


class Prog:
    def __init__(self, nc):
        self.nc = nc
        self.ops = []
        self.streams = {s: [] for s in ('pe', 'act', 'dve', 'pool', 'sp')}
        self.acc = {}
        self.dma_count = {s: 0 for s in self.streams}
        self.pending = {s: set() for s in self.streams}
        self.recent_dma = {s: [] for s in self.streams}

    @staticmethod
    def region(ap):
        t = ap.tensor
        name = t.name
        tn = type(t).__name__
        ext = 0
        aps = list(ap.ap)
        if tn.startswith('DRam'):
            for s, c in aps:
                ext += (c - 1) * abs(s)
            return (name, 0, 1, ap.offset, ap.offset + ext + 1)
        shape = list(t.shape)
        pstride = 1
        for d in shape[1:]:
            pstride *= d
        p0 = ap.offset // pstride
        f0 = ap.offset % pstride
        npart = aps[0][1]
        if tn.startswith('PSum'):
            return (name, 0, 128, 0, 1 << 30)
        for s, c in aps[1:]:
            ext += (c - 1) * abs(s)
        return (name, p0, p0 + npart, f0, f0 + ext + 1)

    def _deps_for(self, idx, stream, is_dma, reads, writes):
        deps = set()
        for is_w, aps in ((False, reads), (True, writes)):
            for ap in aps:
                if ap is None:
                    continue
                name, p0, p1, f0, f1 = self.region(ap)
                if type(ap.tensor).__name__.startswith('PSum'):
                    is_w = True
                lst = self.acc.setdefault(name, [])
                keep = []
                for a in lst:
                    oi, ow, q0, q1, g0, g1 = a
                    ov = not (q1 <= p0 or p1 <= q0 or g1 <= f0 or f1 <= g0)
                    if ov and (is_w or ow) and oi != idx:
                        o = self.ops[oi]
                        same = (o.stream == stream) and not o.is_dma and not is_dma
                        if same and stream == 'pe':
                            pass
                        elif same and stream != 'pool' and (not ow) and is_w:
                            pass
                        else:
                            deps.add(oi)
                    covered = is_w and ov and p0 <= q0 and q1 <= p1 and f0 <= g0 and g1 <= f1
                    if not covered:
                        keep.append(a)
                keep.append([idx, is_w, p0, p1, f0, f1])
                self.acc[name] = keep
        return deps

    def add(self, stream, fn, reads=(), writes=(), is_dma=False):
        op = Op()
        op.idx = len(self.ops)
        op.stream = stream
        op.fn = fn
        op.is_dma = is_dma
        op.signal = is_dma
        op.sem = None
        op.val = 0
        op.waits = []
        op.tag = getattr(self, 'tag', '')
        self.ops.append(op)
        op.deps = self._deps_for(op.idx, stream, is_dma, reads, writes)
        if self.pending[stream]:
            op.deps |= self.pending[stream]
            self.pending[stream] = set()
        if is_dma:
            self.recent_dma[stream] = (self.recent_dma[stream] + [op.idx])[-DMA_K:]
        for d in op.deps:
            self.ops[d].signal = True
        self.streams[stream].append(op)
        return op

    def fence(self):
        F = set()
        for st, lst in self.streams.items():
            for o in reversed(lst):
                if not o.is_dma:
                    F.add(o.idx)
                    break
            F |= set(self.recent_dma[st])
        for st in self.streams:
            self.pending[st] = set(F)
        self.acc = {}

    def dma(self, out, in_, stream='sp', **kw):
        return self.add(stream, lambda e: e.dma_start(out=out, in_=in_, **kw), [in_], [out], is_dma=True)

    def mm(self, out, lhsT, rhs, start=True, stop=True, **kw):
        return self.add('pe', lambda e: e.matmul(out, lhsT, rhs, start=start, stop=stop, **kw),
                        [lhsT, rhs], [out])

    def tr(self, out, in_, ident):
        return self.add('pe', lambda e: e.transpose(out, in_, ident), [in_, ident], [out])

    def act(self, out, in_, func, bias=None, scale=None, accum_out=None, eng='act'):
        rd = [in_]
        kw = {}
        if bias is not None:
            kw['bias'] = bias
            if not isinstance(bias, (int, float)):
                rd.append(bias)
        if scale is not None:
            kw['scale'] = scale
            if not isinstance(scale, (int, float)):
                rd.append(scale)
        wr = [out]
        if accum_out is not None:
            kw['accum_out'] = accum_out
            wr.append(accum_out)
        return self.add('act', lambda e: e.activation(out, in_, func, **kw), rd, wr)

    def tt(self, out, in0, in1, op, eng='dve'):
        return self.add(eng, lambda e: e.tensor_tensor(out, in0, in1, op), [in0, in1], [out])

    def ts(self, out, in0, s1, s2=None, op0=ALU.mult, op1=None, eng='dve', accum_out=None):
        rd = [in0]
        if not isinstance(s1, (int, float)):
            rd.append(s1)
        if s2 is not None and not isinstance(s2, (int, float)):
            rd.append(s2)
        kw = {}
        if op1 is not None:
            kw['op1'] = op1
        wr = [out]
        if accum_out is not None:
            kw['accum_out'] = accum_out
            wr.append(accum_out)
        return self.add(eng, lambda e: e.tensor_scalar(out, in0, s1, s2, op0, **kw), rd, wr)

    def stt(self, out, in0, scalar, in1, op0, op1):
        rd = [in0, in1]
        if not isinstance(scalar, (int, float)):
            rd.append(scalar)
        return self.add('dve', lambda e: e.scalar_tensor_tensor(out, in0, scalar, in1, op0, op1), rd, [out])

    def copy(self, out, in_, eng='dve'):
        if eng == 'act':
            return self.add('act', lambda e: e.copy(out, in_), [in_], [out])
        return self.add(eng, lambda e: e.tensor_copy(out, in_), [in_], [out])

    def memset(self, ap, val, eng='dve'):
        return self.add(eng, lambda e: e.memset(ap, val), [], [ap])

    def finalize(self, stack):
        nc = self.nc
        sems = {}
        for s in COMPUTE:
            sems[s] = stack.enter_context(nc.semaphore('s_' + s))
        dsem = {}
        for s in ('sp', 'pool', 'act'):
            dsem[s] = [stack.enter_context(nc.semaphore('d_%s%d' % (s, i))) for i in range(DMA_K)]
        cnt = {s: 0 for s in COMPUTE}
        dcnt = {}
        dlast = {}
        for op in self.ops:
            if op.is_dma:
                n = dcnt.get(op.stream, 0)
                dcnt[op.stream] = n + 1
                k = n % DMA_K
                op.sem = ('d', op.stream, k)
                op.val = 16 * (n // DMA_K + 1)
                prev = dlast.get((op.stream, k))
                if prev is not None:
                    op.deps.add(prev)
                dlast[(op.stream, k)] = op.idx
            elif op.signal:
                cnt[op.stream] += 1
                op.sem = ('c', op.stream)
                op.val = cnt[op.stream]
        known = {s: {} for s in self.streams}
        for op in self.ops:
            kn = known[op.stream]
            for d in sorted(op.deps):
                p = self.ops[d]
                if kn.get(p.sem, 0) >= p.val:
                    continue
                op.waits.append((p.sem, p.val))
                for k2, v2 in p.clock.items():
                    if kn.get(k2, 0) < v2:
                        kn[k2] = v2
            clk = dict(kn)
            if op.sem is not None:
                clk[op.sem] = op.val
                if not op.is_dma:
                    pass
            op.clock = clk
        self.total_waits = sum(len(o.waits) for o in self.ops)

        def semobj(key):
            if key[0] == 'c':
                return sems[key[1]]
            return dsem[key[1]][key[2]]

        block = stack.enter_context(nc.Block())

        def emit_stream(eng, name):
            for op in self.streams[name]:
                for key, v in op.waits:
                    eng.wait_ge(semobj(key), v)
                inst = op.fn(eng)
                if op.sem is not None:
                    inst.then_inc(semobj(op.sem), 16 if op.is_dma else 1)
            n = dcnt.get(name, 0)
            for k in range(min(n, DMA_K)):
                last = (n - 1 - k) // DMA_K * DMA_K + k
                tot = 16 * (last // DMA_K + 1)
                eng.wait_ge(dsem[name][k], tot)

        @block.sync
        def _(e):
            emit_stream(e, 'sp')

        @block.tensor
        def _(e):
            emit_stream(e, 'pe')

        @block.scalar
        def _(e):
            emit_stream(e, 'act')

        @block.vector
        def _(e):
            emit_stream(e, 'dve')

        @block.gpsimd
        def _(e):
            emit_stream(e, 'pool')


NPBF = ml_dtypes.bfloat16
D = 1024
DFF = 2816
NF = DFF // 128
EPS = 1e-6
Q0, KC, VC, KS, VS, KW, VW, GT, Z0, XB, DTO = 0, 512, 640, 768, 896, 1024, 1152, 1280, 1304, 1816, 2840
INW = 2848
NEGM = -30000.0
SLOPES = [2.0 ** (-(h + 1)) for h in range(8)]
NQS = 8
TQ = 256
NSUB = TQ // 128
NCM = 2048 // TQ + 1


class Cfg:
    def __init__(self, nseq_p=2, seq=4096, nseq_s=16, past=8192, n_phys=10240,
                 stages=('ffn1', 'mixp', 'mixs', 'ffn2'), dbg=False):
        self.nseq_p = nseq_p
        self.seq = seq
        self.nseq_s = nseq_s
        self.past = past
        self.n_phys = n_phys
        self.stages = stages
        self.dbg = dbg
        self.ntok_p = nseq_p * seq
        self.ntok_s = nseq_s * NQS
        self.ntok = self.ntok_p + self.ntok_s
        self.npages = past // 128
        self.nkt_s = self.npages + 1
        self.ncmp_s = past // 16 - 1
        self.nsel_s = past // 64 + 1


def k_rows(pos, with_e, cmp_n=None):
    n = len(pos) if pos is not None else len(cmp_n)
    r = np.zeros((64, n), np.float32)
    if cmp_n is None:
        if with_e:
            jj = (pos // 64) % 32
            r[jj, np.arange(n)] = 1.0
        r[32] = pos % 128
        r[33] = pos // 128
        r[34] = 1.0
        r[35] = 1.0
    else:
        r[34] = 1.0
        r[35] = 1.0
        r[36] = cmp_n % 128
        r[37] = cmp_n // 128
        r[38] = 1.0
    return r.astype(NPBF)


def q_rows(qpos, slope):
    n = len(qpos)
    r = np.zeros((32, n), np.float32)
    r[0] = slope
    r[1] = 128 * slope
    r[2] = -slope * 64 * (qpos // 64)
    r[3] = -slope * (qpos % 64)
    r[4] = 16 * slope
    r[5] = 2048 * slope
    r[6] = 31 * slope
    return r


def host_consts(cfg):
    c = {}
    S = cfg.seq
    c['ident'] = np.eye(128, dtype=np.float32)
    c['tri'] = np.triu(np.ones((128, 128), np.float32))
    c['ones'] = np.ones((128, 128), np.float32)
    qp = np.arange(S)
    c['alq'] = np.stack([q_rows(qp, SLOPES[h]) for h in range(8)], axis=1).astype(NPBF)
    c['kc_sel'] = k_rows(qp, True)
    c['kc_win'] = k_rows(qp, False)
    c['kc_cmp'] = k_rows(None, False, cmp_n=np.arange(256))
    ki = np.arange(128)[:, None]
    qi = np.arange(TQ)[None, :]
    c['caus'] = np.stack([np.where(128 * v + ki > qi, NEGM, 0.0) for v in range(NSUB)], axis=1).astype(NPBF)
    c['far'] = np.stack([np.where(qi - ki >= 512 - 128 * u, NEGM, 0.0) for u in range(1, 5)], axis=1).astype(NPBF)
    c['cmpm'] = np.stack([np.where(16 * ki + 31 > qi + TQ * r, NEGM, 0.0) for r in range(NCM)], axis=1).astype(NPBF)
    nb = S // 16 - 1
    nselp = S // 64
    ov = np.zeros((256, 64), np.float32)
    for i in range(min(nb, 256)):
        for j in (i // 4, (i + 1) // 4):
            if j < 64:
                ov[i, j] += 1.0
    c['ovl'] = ov.reshape(2, 128, 64).transpose(1, 0, 2).copy().astype(NPBF)
    j = np.arange(64)[None, :]
    q = qp[:, None]
    forced = (j == q // 64) | (j == 0)
    valid = (j * 64 <= q) & (j < nselp)
    VAL = np.where(forced, 0.0, np.where(valid, 1.0, 0.0))
    ADD = np.where(forced, 1e9, np.where(valid, 0.0, -1.0))
    c['va'] = np.concatenate([VAL, ADD], axis=1).astype(np.float32)
    PAST = cfg.past
    qs = PAST + np.arange(NQS)
    aq = np.zeros((32, 2, 4, NQS), np.float32)
    for g in range(2):
        for hh in range(4):
            aq[:, g, hh, :] = q_rows(qs, SLOPES[g * 4 + hh])
    c['alq_s'] = aq.reshape(32, 2, 32).astype(NPBF)
    nk = cfg.nkt_s * 128
    c['kc_sel_s'] = k_rows(np.arange(nk), True)
    c['kc_win_s'] = k_rows(PAST - 512 + np.arange(640), False)
    c['kc_cmp_s'] = k_rows(None, False, cmp_n=np.arange(((cfg.ncmp_s + 127) // 128) * 128))
    ki = np.arange(128)[:, None]
    qi8 = np.tile(np.arange(NQS), 4)[None, :]
    c['caus8'] = np.where((ki > qi8) | (ki >= NQS), NEGM, 0.0).astype(NPBF)
    c['far8'] = np.where(ki <= qi8, NEGM, 0.0).astype(NPBF)
    nlast = cfg.ncmp_s - ((cfg.ncmp_s + 127) // 128 - 1) * 128
    c['cmpm_s'] = np.where((ki >= nlast) & (qi8 >= 0), NEGM, 0.0).astype(NPBF)
    nch = (cfg.ncmp_s + 127) // 128
    nsl = cfg.nsel_s
    ovs = np.zeros((nch * 128, nsl), np.float32)
    for i in range(cfg.ncmp_s):
        for jx in (i // 4, (i + 1) // 4):
            if jx < nsl:
                ovs[i, jx] += 1.0
    c['ovl_s'] = ovs.reshape(nch, 128, nsl).transpose(1, 0, 2).copy().astype(NPBF)
    js = np.arange(nsl)[None, :]
    forced_s = (js == (qs[:, None] // 64)) | (js == 0)
    vas = np.zeros((NQS, 2, nsl), np.float32)
    vas[:, 0] = np.where(forced_s, 0.0, 1.0)
    vas[:, 1] = np.where(forced_s, 1e9, 0.0)
    c['va_s'] = vas
    return c


CONST_DT = {'alq': BF16, 'kc_sel': BF16, 'kc_win': BF16, 'kc_cmp': BF16, 'caus': BF16, 'far': BF16, 'cmpm': BF16,
            'ovl': BF16, 'alq_s': BF16, 'kc_sel_s': BF16, 'kc_win_s': BF16, 'kc_cmp_s': BF16, 'caus8': BF16,
            'far8': BF16, 'ovl_s': BF16, 'cmpm_s': BF16}


def build(cfg):
    nc = bass.Bass("TRN2", target_bir_lowering=False)
    P = Prog(nc)
    top = ExitStack()
    NT, NTP, NS = cfg.ntok, cfg.ntok_p, cfg.ntok_s
    S = cfg.seq
    PAST = cfg.past

    def din(name, shape, dt=F32):
        return nc.dram_tensor(name, list(shape), dt, kind="ExternalInput").ap()

    def dout(name, shape, dt=F32):
        return nc.dram_tensor(name, list(shape), dt, kind="ExternalOutput").ap()

    def dscr(name, shape, dt=F32):
        return nc.dram_tensor(name, list(shape), dt, kind="Internal").ap()

    x_in = din("x_in", [NT, D])
    hc = host_consts(cfg)
    C = {k: din("c_" + k, v.shape, CONST_DT.get(k, F32)) for k, v in hc.items()}
    w = {}
    for nm, shp in (("ffn1_wg", [D, DFF]), ("ffn1_wu", [D, DFF]), ("ffn1_wd", [DFF, D]),
                    ("ffn2_wg", [D, DFF]), ("ffn2_wu", [D, DFF]), ("ffn2_wd", [DFF, D]),
                    ("norm_ffn1", [D]), ("norm_ffn2", [D]), ("norm_final", [D]), ("norm_mix", [D]),
                    ("w_in", [D, INW]), ("w_out", [D, D]),
                    ("cmp_pe_k", [32, 64]), ("cmp_w1_k", [2048, 128]), ("cmp_w2_k", [128, 64]),
                    ("cmp_pe_v", [32, 64]), ("cmp_w1_v", [2048, 128]), ("cmp_w2_v", [128, 64]),
                    ("conv_w", [4, 1024]), ("conv_b", [1024]), ("dt_bias", [8]), ("a_log", [8]),
                    ("d_skip", [8]), ("ssm_norm", [512])):
        w[nm] = din(nm, shp)
    cache_cmp = din("cache_cmp", [cfg.n_phys * 128, 256])
    cache_sel = din("cache_sel", [cfg.n_phys * 128, 256])
    cache_win = din("cache_win", [max(cfg.nseq_s, 1), 512, 256])
    state_conv = din("state_conv", [max(cfg.nseq_s, 1) * 3, 1024])
    state_ssm = din("state_ssm", [max(cfg.nseq_s, 1), 512, 128])
    page_table = din("page_table", [max(cfg.nseq_s, 1), cfg.npages], I32)
    iota_p = din("iota_p", [128, cfg.npages], F32)

    y_out = dout("y_out", [NT, D])
    o_cmp_p = dout("o_cmp_p", [max(NTP, 1), 256])
    o_sel_p = dout("o_sel_p", [max(NTP, 1), 256])
    o_win_p = dout("o_win_p", [max(cfg.nseq_p, 1), 512, 256])
    o_conv_p = dout("o_conv_p", [max(cfg.nseq_p, 1), 3, 1024])
    o_ssm_p = dout("o_ssm_p", [max(cfg.nseq_p, 1), 512, 128])
    o_cmp_s = dout("o_cmp_s", [max(NS, 1), 256])
    o_sel_s = dout("o_sel_s", [max(NS, 1), 256])
    o_win_s = dout("o_win_s", [max(cfg.nseq_s, 1), 512, 256])
    o_conv_s = dout("o_conv_s", [max(cfg.nseq_s, 1), 3, 1024])
    o_ssm_s = dout("o_ssm_s", [max(cfg.nseq_s, 1), 512, 128])
    x1_d = dscr("x1_scr", [NT, D])
    x2_d = dscr("x2_scr", [NT, D])
    NSs = max(NS, 1)
    u_scr = dscr("u_scr", [NSs, 2336])
    q_scr = dscr("q_scr", [64, 8, NSs], BF16)
    ks_scr = dscr("ks_scr", [64, 2, NSs], BF16)
    kw_scr = dscr("kw_scr", [64, 2, NSs], BF16)
    xb_scr = dscr("xb_scr", [128, 8, NSs])
    hs_scr = dscr("hs_scr", [128, 8, max(cfg.nseq_s, 1) * 3])
    dbg = {}
    if cfg.dbg:
        dbg['att'] = dout("dbg_att", [NT, 512])
        dbg['ssm'] = dout("dbg_ssm", [NT, 512])
        dbg['x2'] = dout("dbg_x2", [NT, D])

    PSF = [top.enter_context(nc.psum_tensor("psf%d" % i, [128, 512], F32)) for i in range(7)]
    PSB = top.enter_context(nc.psum_tensor("psb", [128, 1024], BF16))

    uniq = [0]

    class Scope:
        def __init__(self):
            self.st = ExitStack()

        def sb(self, name, shape, dt=F32):
            uniq[0] += 1
            return self.st.enter_context(nc.sbuf_tensor("%s_u%d" % (name, uniq[0]), list(shape), dt))

        def close(self):
            P.fence()
            self.st.close()

    def rms_rstd(sc_stat, X, L, col, junk):
        st = sc_stat
        width = X.shape[-1]
        P.act(junk[0:L, :width], X, AF.Square, accum_out=st[0:L, col:col + 1])
        P.ts(st[0:L, col + 1:col + 2], st[0:L, col:col + 1], 1.0 / width, EPS, op0=ALU.mult, op1=ALU.add)
        P.act(st[0:L, col + 2:col + 3], st[0:L, col + 1:col + 2], AF.Sqrt)
        o, i = st[0:L, col + 3:col + 4], st[0:L, col + 2:col + 3]
        P.add('dve', lambda e, o=o, i=i: e.reciprocal(o, i), [i], [o])
        return o

    def ffn_phase(pfx, src_d, dst_d, final):
        P.tag = pfx
        sc = Scope()
        identf = sc.sb("identf", [128, 128], F32)
        identb = sc.sb("identb", [128, 128], BF16)
        P.dma(identf[:], C['ident'])
        P.copy(identb[:], identf[:])
        WG = sc.sb("WG", [128, 8, DFF], BF16)
        WU = sc.sb("WU", [128, 8, DFF], BF16)
        WD = sc.sb("WD", [128, NF, D], BF16)
        stg = [sc.sb("stg%d" % i, [128, 1408], F32) for i in range(2)]
        xin = [sc.sb("xin%d" % i, [128, D], F32) for i in range(2)]
        xres = [sc.sb("xres%d" % i, [128, D], F32) for i in range(2)]
        hb = sc.sb("hb", [128, D], BF16)
        hT = sc.sb("hT", [128, 8, 512], BF16)
        AT = sc.sb("AT", [128, NF, 512], BF16)
        sil = [sc.sb("sil%d" % i, [128, 512], BF16) for i in range(2)]
        nwT = sc.sb("nwT", [128, 8], F32)
        nfin = sc.sb("nfin", [128, D], F32)
        stat = sc.sb("stat", [128, 32], F32)
        junk = sc.sb("junk", [128, D], BF16)
        psT = PSB
        psG, psU, psO = PSF[0:2], PSF[2:4], PSF[4:6]
        cnt = [0]

        def load_w(dst3, src2, nchunk, width):
            for c in range(nchunk):
                for c0 in range(0, width, 1408):
                    cw = min(1408, width - c0)
                    s = stg[cnt[0] % 2]
                    eng = ('act', 'dve', 'pool')[cnt[0] % 3]
                    cnt[0] += 1
                    P.dma(s[:, :cw], src2[c * 128:(c + 1) * 128, c0:c0 + cw])
                    P.copy(dst3[:, c, c0:c0 + cw], s[:, :cw], eng=eng)

        load_w(WG, w[pfx + "_wg"], 8, DFF)
        load_w(WU, w[pfx + "_wu"], 8, DFF)
        load_w(WD, w[pfx + "_wd"], NF, D)
        nrm = w["norm_ffn1" if pfx == "ffn1" else "norm_ffn2"]
        P.add('sp', lambda e: e.dma_start(out=nwT[:], in_=nrm.rearrange("(c p) -> p c", p=128),
                                          allow_slow_non_contiguous=True), [nrm], [nwT[:]], is_dma=True)
        if final:
            P.dma(nfin[:], w["norm_final"].partition_broadcast(128))
        tiles = []
        t0 = 0
        while t0 < NT:
            n = min(512, NT - t0)
            tiles.append((t0, n))
            t0 += n
        scn = [0]

        def subs_of(ti):
            t0, n = tiles[ti]
            return [(s0, min(128, n - s0)) for s0 in range(0, n, 128)]

        def prologue_sub(ti, si):
            t0, n = tiles[ti]
            s0, L = subs_of(ti)[si]
            X = xin[scn[0] % 2]
            scn[0] += 1
            P.dma(X[0:L, :], src_d[t0 + s0:t0 + s0 + L, :])
            r = rms_rstd(stat, X[0:L, :], L, 4 * si, junk)
            P.ts(hb[0:L, :], X[0:L, :], r, None, op0=ALU.mult)
            for k in range(8):
                P.tr(psT[:, k * 128:k * 128 + L], hb[0:L, k * 128:(k + 1) * 128], identb[0:L, 0:L])
            for k in range(8):
                if k % 2 == 0:
                    P.act(hT[:, k, s0:s0 + L], psT[:, k * 128:k * 128 + L], AF.Copy, scale=nwT[:, k:k + 1])
                else:
                    P.ts(hT[:, k, s0:s0 + L], psT[:, k * 128:k * 128 + L], nwT[:, k:k + 1], None, op0=ALU.mult)

        for si in range(len(subs_of(0))):
            prologue_sub(0, si)
        rc = 0
        for ti, (t0, n) in enumerate(tiles):
            subs = subs_of(ti)
            for f in range(NF):
                g = psG[f % 2]
                u = psU[f % 2]
                for k in range(8):
                    P.mm(g[:, :n], WG[:, k, f * 128:(f + 1) * 128], hT[:, k, :n], start=(k == 0), stop=(k == 7))
                for k in range(8):
                    P.mm(u[:, :n], WU[:, k, f * 128:(f + 1) * 128], hT[:, k, :n], start=(k == 0), stop=(k == 7))
                sl = sil[f % 2]
                P.act(sl[:, :n], g[:, :n], AF.Silu)
                P.tt(AT[:, f, :n], sl[:, :n], u[:, :n], ALU.mult)
            nxt = subs_of(ti + 1) if ti + 1 < len(tiles) else []
            for si, (s0, L) in enumerate(subs):
                X = xres[rc % 2]
                rc += 1
                P.dma(X[0:L, :], src_d[t0 + s0:t0 + s0 + L, :])
                for h2 in range(2):
                    o = psO[(si * 2 + h2) % 2]
                    for f in range(NF):
                        P.mm(o[0:L, :], AT[:, f, s0:s0 + L], WD[:, f, h2 * 512:(h2 + 1) * 512],
                             start=(f == 0), stop=(f == NF - 1))
                    P.stt(X[0:L, h2 * 512:(h2 + 1) * 512], o[0:L, :], 0.5, X[0:L, h2 * 512:(h2 + 1) * 512],
                          ALU.mult, ALU.add)
                if final:
                    r = rms_rstd(stat, X[0:L, :], L, 16 + 4 * si, junk)
                    P.stt(X[0:L, :], X[0:L, :], r, nfin[0:L, :], ALU.mult, ALU.mult)
                P.dma(dst_d[t0 + s0:t0 + s0 + L, :], X[0:L, :])
                if si < len(nxt):
                    prologue_sub(ti + 1, si)
            for si in range(len(subs), len(nxt)):
                prologue_sub(ti + 1, si)
        sc.close()

    def mixer_phase():
        P.tag = 'mix_init'
        mc = Scope()
        identf = mc.sb("identf", [128, 128], F32)
        identb = mc.sb("identb", [128, 128], BF16)
        tri = mc.sb("tri", [128, 128], F32)
        onesf = mc.sb("onesf", [128, 128], F32)
        P.dma(identf[:], C['ident'])
        P.copy(identb[:], identf[:])
        P.dma(tri[:], C['tri'])
        P.dma(onesf[:], C['ones'])
        Wout = mc.sb("Wout", [128, 8, D], BF16)
        W1 = [mc.sb("W1_%d" % i, [128, 32, 128], BF16) for i in range(2)]
        W2 = [mc.sb("W2_%d" % i, [128, 64], BF16) for i in range(2)]
        peT = [mc.sb("peT_%d" % i, [128, 32], BF16) for i in range(2)]
        b1 = [mc.sb("b1_%d" % i, [128, 1], F32) for i in range(2)]
        nwT = mc.sb("nwT", [128, 8], F32)
        cw = mc.sb("cw", [128, 8, 4], F32)
        cb = mc.sb("cb", [128, 8], F32)
        dtb = mc.sb("dtb", [128, 8], F32)
        Aneg = mc.sb("Aneg", [128, 8], F32)
        dsk = mc.sb("dsk", [128, 8], F32)
        snw = mc.sb("snw", [128, 4], F32)
        stat = mc.sb("stat", [128, 32], F32)
        junk = mc.sb("junk", [128, D], BF16)
        wsc = Scope()
        Win = wsc.sb("Win", [128, 8, INW], BF16)
        tmpsc = Scope()
        stg = [tmpsc.sb("mstg%d" % i, [128, 1424], F32) for i in range(2)]
        cnt = [0]

        def slow_dma(out, in_):
            P.add('sp', lambda e: e.dma_start(out=out, in_=in_, allow_slow_non_contiguous=True), [in_], [out],
                  is_dma=True)

        def load_w(dst3, src2, nchunk, width, piece=1424):
            for c in range(nchunk):
                for c0 in range(0, width, piece):
                    cwid = min(piece, width - c0)
                    s = stg[cnt[0] % 2]
                    eng = ('act', 'dve', 'pool')[cnt[0] % 3]
                    cnt[0] += 1
                    P.dma(s[:, :cwid], src2[c * 128:(c + 1) * 128, c0:c0 + cwid])
                    P.copy(dst3[:, c, c0:c0 + cwid], s[:, :cwid], eng=eng)

        load_w(Win, w['w_in'], 8, INW)
        load_w(Wout, w['w_out'], 8, D)
        for i, kv in enumerate(('k', 'v')):
            w1 = w['cmp_w1_' + kv].rearrange("(r d) h -> d r h", d=64)
            for half in range(2):
                for r0 in range(0, 32, 8):
                    s = stg[cnt[0] % 2]
                    cnt[0] += 1
                    sv = s[half * 64:(half + 1) * 64, 0:1024].rearrange("p (r h) -> p r h", h=128)
                    P.dma(sv, w1[:, r0:r0 + 8, :])
                    P.copy(W1[i][half * 64:(half + 1) * 64, r0:r0 + 8, :], sv)
            s = stg[cnt[0] % 2]
            cnt[0] += 1
            P.dma(s[:, 0:64], w['cmp_w2_' + kv])
            P.copy(W2[i][:], s[:, 0:64])
            s = stg[cnt[0] % 2]
            cnt[0] += 1
            for half in range(2):
                slow_dma(s[half * 64:(half + 1) * 64, 0:32], w['cmp_pe_' + kv].rearrange("r d -> d r"))
            P.copy(peT[i][:], s[:, 0:32])
            ps = PSF[0]
            for r in range(32):
                P.mm(ps[:, 0:1], W1[i][0:64, r, :], peT[i][0:64, r:r + 1], start=(r == 0), stop=(r == 31))
            P.copy(b1[i][:], ps[:, 0:1])
        slow_dma(nwT[:], w['norm_mix'].rearrange("(c p) -> p c", p=128))
        for tap in range(4):
            slow_dma(cw[:, :, tap], w['conv_w'][tap, :].rearrange("(c p) -> p c", p=128))
        slow_dma(cb[:], w['conv_b'].rearrange("(c p) -> p c", p=128))
        slow_dma(snw[:], w['ssm_norm'].rearrange("(c p) -> p c", p=128))
        P.dma(dtb[:], w['dt_bias'].partition_broadcast(128))
        P.dma(dsk[:], w['d_skip'].partition_broadcast(128))
        P.dma(Aneg[:], w['a_log'].partition_broadcast(128))
        P.act(Aneg[:], Aneg[:], AF.Exp)
        P.ts(Aneg[:], Aneg[:], -1.0, None, op0=ALU.mult)

        tmpsc.close()
        K = dict(identf=identf, identb=identb, tri=tri, onesf=onesf, Win=Win, Wout=Wout, W1=W1, W2=W2, b1=b1,
                 nwT=nwT, cw=cw, cb=cb, dtb=dtb, Aneg=Aneg, dsk=dsk, snw=snw, stat=stat, junk=junk)
        if 'mixp' in cfg.stages:
            for sq in range(cfg.nseq_p):
                mixer_prompt(K, sq)
        if 'mixs' in cfg.stages and cfg.nseq_s > 0:
            sample_group(K)
        wsc.close()
        if 'mixs' in cfg.stages and cfg.nseq_s > 0:
            mixer_sample(K)
        mc.close()

    def ssd_chunk(K, T, L, BT, CT, xtm, Btm, dtv, av, zs, hTf, hTb, ymix_out):
        tri, onesf = K['tri'], K['onesf']
        psC, psCB, psE, psY, psYo, psSt = PSF[0], PSF[1], PSF[2:4], PSF[4], PSF[5], PSF[6]
        acsc, expacs, dend, cd = T['acsc'], T['expacs'], T['dend'], T['cd']
        aTri, CBm, segc, Eb, MTb, xdt, xdtd, Ys, Yt, tmp2 = (T[k] for k in
                                                             ('aTri', 'CBm', 'segc', 'Eb', 'MTb', 'xdt', 'xdtd', 'Ys',
                                                              'Yt', 'tmp2'))
        P.mm(psC[0:L, 0:8], tri[0:L, 0:L], av, start=True, stop=True)
        P.copy(acsc[0:L, :], psC[0:L, 0:8])
        P.act(expacs[0:L, :], acsc[0:L, :], AF.Exp)
        P.tt(aTri[0:L, :, 0:L], tri[0:L, 0:L].unsqueeze(1).broadcast_to([L, 8, L]),
             av.unsqueeze(2).broadcast_to([L, 8, L]), ALU.mult)
        for g in range(2):
            P.mm(psCB[0:L, g * 128:g * 128 + L], BT[g], CT[g], start=True, stop=True)
            P.tt(CBm[0:L, g, 0:L], psCB[0:L, g * 128:g * 128 + L], tri[0:L, 0:L], ALU.mult)
        P.tt(xdt[0:L, :].rearrange("p (h d) -> p h d", h=8), xtm.rearrange("p (h d) -> p h d", h=8),
             dtv.unsqueeze(2).broadcast_to([L, 8, 64]), ALU.mult)
        for h in range(8):
            g = h // 4
            pe = psE[h % 2]
            P.mm(pe[0:L, 0:L], onesf[0:L, 0:L], aTri[0:L, h, 0:L], start=True, stop=True)
            sg = segc[h % 2]
            P.ts(sg[0:L, 0:L], pe[0:L, 0:L], acsc[0:L, h:h + 1], 0.0, op0=ALU.subtract, op1=ALU.min)
            E = Eb[h % 2]
            P.act(E[0:L, 0:L], sg[0:L, 0:L], AF.Exp)
            P.copy(dend[0:L, h:h + 1], E[0:L, L - 1:L], eng='pool')
            M = MTb[h % 2]
            P.tt(M[0:L, 0:L], E[0:L, 0:L], CBm[0:L, g, 0:L], ALU.mult)
            P.mm(psY[0:L, h * 64:(h + 1) * 64], M[0:L, 0:L], xdt[0:L, h * 64:(h + 1) * 64],
                 start=(h == 0), stop=True, skip_group_check=True)
        for g in range(2):
            P.mm(psYo[0:L, g * 256:(g + 1) * 256], CT[g], hTb[:, g * 256:(g + 1) * 256],
                 start=(g == 0), stop=True, skip_group_check=True)
        P.tt(Ys[0:L, :].rearrange("p (h d) -> p h d", h=8), psYo[0:L, :].rearrange("p (h d) -> p h d", h=8),
             expacs[0:L, :].unsqueeze(2).broadcast_to([L, 8, 64]), ALU.mult)
        P.tt(Yt[0:L, :], psY[0:L, :], Ys[0:L, :], ALU.add)
        P.tt(tmp2[0:L, :].rearrange("p (h d) -> p h d", h=8), xtm.rearrange("p (h d) -> p h d", h=8),
             K['dsk'][0:L, :].unsqueeze(2).broadcast_to([L, 8, 64]), ALU.mult)
        P.tt(Yt[0:L, :], Yt[0:L, :], tmp2[0:L, :], ALU.add)
        P.tt(Yt[0:L, :], Yt[0:L, :], zs, ALU.mult)
        st = K['stat']
        for g in range(2):
            r = rms_rstd(st, Yt[0:L, g * 256:(g + 1) * 256], L, 16 + 4 * g, K['junk'])
            P.ts(ymix_out[:, g * 256:(g + 1) * 256], Yt[0:L, g * 256:(g + 1) * 256], r, None, op0=ALU.mult)
        P.tt(xdtd[0:L, :].rearrange("p (h d) -> p h d", h=8), xdt[0:L, :].rearrange("p (h d) -> p h d", h=8),
             dend[0:L, :].unsqueeze(2).broadcast_to([L, 8, 64]), ALU.mult)
        for g in range(2):
            P.mm(psSt[:, g * 256:(g + 1) * 256], Btm[g], xdtd[0:L, g * 256:(g + 1) * 256],
                 start=(g == 0), stop=True, skip_group_check=True)
        P.mm(psC[:, 8:16], onesf[0:L, :], av, start=True, stop=True)
        P.act(T['cdall'][:, :], psC[:, 8:16], AF.Exp)
        P.tt(hTf[:, :].rearrange("p (h d) -> p h d", h=8), hTf[:, :].rearrange("p (h d) -> p h d", h=8),
             T['cdall'][:, :].unsqueeze(2).broadcast_to([128, 8, 64]), ALU.mult)
        P.tt(hTf[:, :], hTf[:, :], psSt[:, :], ALU.add)
        P.copy(hTb[:, :], hTf[:, :], eng='pool')

    def ssd_tmp(sc):
        T = {}
        for nm, shp, dt in (('acsc', [128, 8], F32), ('expacs', [128, 8], F32), ('dend', [128, 8], F32),
                            ('cd', [128, 8], F32), ('cdall', [128, 8], F32), ('aTri', [128, 8, 128], F32),
                            ('CBm', [128, 2, 128], F32), ('xdt', [128, 512], BF16), ('xdtd', [128, 512], BF16),
                            ('Ys', [128, 512], F32), ('Yt', [128, 512], F32), ('tmp2', [128, 512], F32)):
            T[nm] = sc.sb("ssd_" + nm, shp, dt)
        T['segc'] = [sc.sb("ssd_segc%d" % i, [128, 128], F32) for i in range(2)]
        T['Eb'] = [sc.sb("ssd_Eb%d" % i, [128, 128], F32) for i in range(2)]
        T['MTb'] = [sc.sb("ssd_MT%d" % i, [128, 128], BF16) for i in range(2)]
        return T

    def mixer_prompt(K, sq):
        sc = Scope()
        identf, identb, Win, Wout = K['identf'], K['identb'], K['Win'], K['Wout']
        stat, junk = K['stat'], K['junk']
        base = sq * S
        ntile = S // TQ
        far_nz = [bool(np.any(hc['far'][:, u, :].astype(np.float32) != 0)) for u in range(4)]
        nsel_t = S // 128
        caus = sc.sb("caus", [128, NSUB, TQ], BF16)
        far = sc.sb("far", [128, 4, TQ], BF16)
        cmpm = sc.sb("cmpm", [128, NCM, TQ], BF16)
        ovl = sc.sb("ovl", [128, 2, 64], BF16)
        P.dma(caus[:], C['caus'])
        P.dma(far[:], C['far'])
        P.dma(cmpm[:], C['cmpm'])
        P.dma(ovl[:], C['ovl'])
        KTs = [sc.sb("KTs%d" % g, [128, S], BF16) for g in range(2)]
        KTw = sc.sb("KTw", [128, 2, 1024], BF16)
        kcT = sc.sb("kcT", [128, 2, 256], BF16)
        Vs = sc.sb("Vs", [128, nsel_t, 2, 65], BF16)
        Vw = sc.sb("Vw", [128, 8, 2, 65], BF16)
        vca = sc.sb("vca", [128, 2, 2, 65], BF16)
        Hh = [sc.sb("Hh%d" % i, [128, 2, 256], BF16) for i in range(2)]
        raw = [sc.sb("raw%d" % i, [128, 16 + TQ], BF16) for i in range(2)]
        QTa = sc.sb("QTa", [128, 2, 8, TQ], BF16)
        hT = sc.sb("hT", [128, 8, TQ], BF16)
        hb = sc.sb("hb", [128, D], BF16)
        xin = sc.sb("xio", [128, D], F32)
        xres = xin
        kvst = [sc.sb("kvst", [128, 768], F32)] * 2
        gate = sc.sb("gate", [128, NSUB, 24], F32)
        zs = sc.sb("zs", [128, NSUB, 512], BF16)
        dtv = sc.sb("dtv", [128, NSUB, 8], F32)
        av = sc.sb("av", [128, NSUB, 8], F32)
        att = sc.sb("att", [128, NSUB, 512], F32)
        attb = sc.sb("attb", [128, 512], BF16)
        ymix = sc.sb("ymix", [128, NSUB, 512], BF16)
        PT = [sc.sb("PT%d" % i, [128, 2 * TQ], BF16) for i in range(3)]
        rz = sc.sb("rz", [128, 8], F32)
        coef = sc.sb("coef", [128, 8], F32)
        imp = sc.sb("imp", [128, NSUB, 2, 64], F32)
        impf = sc.sb("impf", [128, 64], F32)
        impw = sc.sb("impw", [128, 64], F32)
        m8 = sc.sb("m8", [128, 16], F32)
        msk = sc.sb("msk", [128, 64], F32)
        SELB = sc.sb("SELB", [128, NSUB, 2, 2, 96], F32)
        va = sc.sb("va", [128, NSUB, 128], F32)
        xp = [sc.sb("xp%d" % i, [128, 3 + TQ], F32) for i in range(2)]
        accf = sc.sb("accf", [128, TQ], F32)
        hist = sc.sb("hist", [128, 8, 3], F32)
        convo = sc.sb("convo", [128, 8, TQ], BF16)
        xtm1 = sc.sb("xtm", [128, 512], BF16)
        Btm1 = sc.sb("Btm", [128, 2, 128], BF16)
        hTf = sc.sb("hTf", [128, 512], F32)
        hTb = sc.sb("hTb", [128, 512], BF16)
        ostg = xin
        T = ssd_tmp(sc)
        psM = PSF[0:2]
        psS = PSF[2:4]
        psO = PSF[4:6]
        psI = PSF[6]
        psT = PSB
        mcnt = [0]

        def nextM():
            mcnt[0] += 1
            return psM[mcnt[0] % 2]

        for g in range(2):
            P.dma(KTs[g][64:128, :], C['kc_sel'])
            P.memset(KTs[g][0:64, :], 0.0, eng='pool')
        P.memset(KTw[:], 0.0, eng='pool')
        P.memset(kcT[0:64, :, :], 0.0)
        for g in range(2):
            P.dma(kcT[64:128, g, :], C['kc_cmp'])
        P.memset(Vs[:], 0.0, eng='pool')
        P.memset(Vs[:, :, :, 64:65], 1.0, eng='pool')
        P.memset(Vw[:], 0.0, eng='pool')
        P.memset(Vw[:, :, :, 64:65], 1.0, eng='pool')
        P.memset(vca[:], 0.0, eng='pool')
        P.memset(vca[:, :, :, 64:65], 1.0, eng='pool')
        for i in range(2):
            P.memset(Hh[i][:], 0.0)
            P.memset(raw[i][:], 0.0)
        P.memset(QTa[:], 0.0, eng='pool')
        P.memset(SELB[:], 0.0)
        P.memset(hist[:], 0.0)
        P.memset(hTf[:], 0.0)
        P.memset(hTb[:], 0.0)

        for ti in range(ntile):
            q0 = ti * TQ
            useG1 = q0 >= 2048
            P.tag = 'pA'
            for s in range(NSUB):
                r0 = base + q0 + s * 128
                P.dma(xin[:], x1_d[r0:r0 + 128, :])
                r = rms_rstd(stat, xin[:], 128, 4 * s, junk)
                P.ts(hb[:], xin[:], r, None, op0=ALU.mult)
                for k in range(8):
                    P.tr(psT[:, k * 128:(k + 1) * 128], hb[:, k * 128:(k + 1) * 128], identb[:])
                for k in range(8):
                    if k % 2 == 0:
                        P.act(hT[:, k, s * 128:(s + 1) * 128], psT[:, k * 128:(k + 1) * 128], AF.Copy,
                              scale=K['nwT'][:, k:k + 1])
                    else:
                        P.ts(hT[:, k, s * 128:(s + 1) * 128], psT[:, k * 128:(k + 1) * 128], K['nwT'][:, k:k + 1],
                             None, op0=ALU.mult)

            def proj_fm(col0, ncol):
                ps = nextM()
                for k in range(8):
                    P.mm(ps[0:ncol, 0:TQ], Win[:, k, col0:col0 + ncol], hT[:, k, :], start=(k == 0), stop=(k == 7))
                return ps

            P.tag = 'pB'
            for h in range(8):
                ps = proj_fm(Q0 + h * 64, 64)
                P.act(QTa[0:64, 0, h, :], ps[0:64, 0:TQ], AF.Copy, scale=0.125)
                if useG1:
                    P.ts(QTa[0:64, 1, h, :], ps[0:64, 0:TQ], 0.125, None, op0=ALU.mult)
            wc = q0 % 1024
            for g in range(2):
                ps = proj_fm(KS + g * 64, 64)
                P.copy(KTs[g][0:64, q0:q0 + TQ], ps[0:64, 0:TQ], eng='act')
                ps = proj_fm(KW + g * 64, 64)
                P.copy(KTw[0:64, g, wc:wc + TQ], ps[0:64, 0:TQ])
                P.dma(KTw[64:128, g, wc:wc + TQ], C['kc_win'][:, q0:q0 + TQ])
            for i, c0 in enumerate((KC, VC)):
                ps = proj_fm(c0, 128)
                P.copy(raw[i][:, 16:16 + TQ], ps[:, 0:TQ], eng=('act' if i == 0 else 'dve'))
            for c in range(8):
                ps = proj_fm(XB + c * 128, 128)
                X = xp[c % 2]
                P.copy(X[:, 0:3], hist[:, c, :], eng='pool')
                P.copy(X[:, 3:3 + TQ], ps[:, 0:TQ], eng='act')
                P.ts(accf[:], X[:, 0:TQ], K['cw'][:, c, 0:1], K['cb'][:, c:c + 1], op0=ALU.mult, op1=ALU.add)
                for tap in range(1, 4):
                    P.stt(accf[:], X[:, tap:tap + TQ], K['cw'][:, c, tap:tap + 1], accf[:], ALU.mult, ALU.add)
                P.act(convo[:, c, :], accf[:], AF.Silu)
                P.copy(hist[:, c, :], X[:, TQ:TQ + 3], eng='pool')
            P.tag = 'pC'
            for s in range(NSUB):
                kv = kvst[s % 2]
                tok0 = base + q0 + s * 128
                ps = nextM()
                for k in range(8):
                    P.mm(ps[:, :], hT[:, k, s * 128:(s + 1) * 128], Win[:, k, 512:1024], start=(k == 0), stop=(k == 7))
                P.copy(kv[:, 0:512], ps[:, :], eng='act')
                ps = nextM()
                for k in range(8):
                    P.mm(ps[:, 0:280], hT[:, k, s * 128:(s + 1) * 128], Win[:, k, 1024:1304], start=(k == 0),
                         stop=(k == 7))
                P.copy(kv[:, 512:768], ps[:, 0:256])
                P.act(gate[:, s, :], ps[:, 256:280], AF.Sigmoid)
                P.dma(o_cmp_p[tok0:tok0 + 128, :], kv[:, 0:256])
                P.dma(o_sel_p[tok0:tok0 + 128, :], kv[:, 256:512])
                tl = q0 + s * 128
                if tl >= S - 512:
                    P.dma(o_win_p[sq, tl - (S - 512):tl - (S - 512) + 128, :], kv[:, 512:768])
                P.copy(Vs[:, tl // 128, :, 0:64], kv[:, 384:512].rearrange("p (g d) -> p g d", g=2), eng='pool')
                P.copy(Vw[:, (tl // 128) % 8, :, 0:64], kv[:, 640:768].rearrange("p (g d) -> p g d", g=2), eng='pool')
                ps = nextM()
                for k in range(8):
                    P.mm(ps[:, :], hT[:, k, s * 128:(s + 1) * 128], Win[:, k, Z0:Z0 + 512], start=(k == 0), stop=(k == 7))
                P.act(zs[:, s, :], ps[:, :], AF.Silu)
                ps = nextM()
                for k in range(8):
                    P.mm(ps[:, 0:8], hT[:, k, s * 128:(s + 1) * 128], Win[:, k, DTO:DTO + 8], start=(k == 0),
                         stop=(k == 7))
                P.tt(dtv[:, s, :], ps[:, 0:8], K['dtb'][:], ALU.add)
                P.act(dtv[:, s, :], dtv[:, s, :], AF.Exp)
                P.act(dtv[:, s, :], dtv[:, s, :], AF.Ln, bias=1.0)
                P.tt(av[:, s, :], dtv[:, s, :], K['Aneg'][:], ALU.mult)
            P.tag = 'pD'
            jlo = max(0, q0 // 16 - 1)
            jhi = (q0 + TQ) // 16 - 1
            nb = jhi - jlo
            cst = 16 * jlo - q0 + 16
            for i in range(2):
                for g in range(2):
                    ps = nextM()
                    for r in range(32):
                        P.mm(ps[:, 0:nb], K['W1'][i][g * 64:(g + 1) * 64, r, :],
                             raw[i][g * 64:(g + 1) * 64, cst + r:cst + r + 16 * (nb - 1) + 1:16],
                             start=(r == 0), stop=(r == 31))
                    P.act(Hh[i][:, g, jlo:jhi], ps[:, 0:nb], AF.Silu, bias=K['b1'][i][:, 0:1])
                P.copy(raw[i][:, 0:16], raw[i][:, TQ:TQ + 16], eng='pool')
            for g in range(2):
                ps = nextM()
                P.mm(ps[0:64, 0:nb], K['W2'][0][:, :], Hh[0][:, g, jlo:jhi], start=True, stop=True)
                P.copy(kcT[0:64, g, jlo:jhi], ps[0:64, 0:nb])
                for c in range(jlo // 128, (jhi - 1) // 128 + 1):
                    ps = nextM()
                    P.mm(ps[:, 0:64], Hh[1][:, g, c * 128:(c + 1) * 128], K['W2'][1][:, :], start=True, stop=True)
                    P.copy(vca[:, c, g, 0:64], ps[:, 0:64], eng='act')
            P.tag = 'pE'
            P.dma(QTa[96:128, 0, :, :], C['alq'][:, :, q0:q0 + TQ])
            if useG1:
                P.dma(QTa[96:128, 1, :, :], C['alq'][:, :, q0:q0 + TQ])
            P.dma(va[:], C['va'][q0:q0 + TQ, :].rearrange("(s p) c -> p s c", p=128))

            def attend(hp, branch, tiles, first_branch):
                po = psO[hp % 2]
                n_t = len(tiles)
                for idx in range(n_t + 1):
                    if idx < n_t:
                        kt, G, mk, vv, ov = tiles[idx]
                        sb_ = psS[idx % 2]
                        P.mm(sb_[:, 0:2 * TQ], kt, QTa[:, G, 2 * hp:2 * hp + 2, :], start=True, stop=(mk is None))
                        if mk is not None:
                            P.mm(sb_[:, 0:2 * TQ], identb[:], mk.unsqueeze(1).broadcast_to([128, 2, TQ]),
                                 start=False, stop=True)
                        P.act(PT[idx % 3][:], sb_[:, 0:2 * TQ], AF.Exp)
                    if idx >= 1:
                        j = idx - 1
                        kt, G, mk, vv, ov = tiles[j]
                        pt = PT[j % 3]
                        for hl in range(2):
                            for s in range(NSUB):
                                P.mm(po[:, hl * 65 * NSUB + s * 65:hl * 65 * NSUB + (s + 1) * 65],
                                     pt[:, hl * TQ + s * 128:hl * TQ + (s + 1) * 128], vv,
                                     start=(j == 0 and hl == 0 and s == 0), stop=(j == n_t - 1), skip_group_check=True)
                        if ov is not None:
                            for hl in range(2):
                                for s in range(NSUB):
                                    P.mm(psI[:, hl * 64 * NSUB + s * 64:hl * 64 * NSUB + (s + 1) * 64],
                                         pt[:, hl * TQ + s * 128:hl * TQ + (s + 1) * 128], ov,
                                         start=(j == 0 and hl == 0 and s == 0), stop=(j == n_t - 1),
                                         skip_group_check=True)
                for hl in range(2):
                    h = 2 * hp + hl
                    o0 = hl * 65 * NSUB
                    zc = po[:, o0 + 64:o0 + 64 + 65 * (NSUB - 1) + 1:65]
                    rzh = rz[:, hl * NSUB:(hl + 1) * NSUB]
                    cfh = coef[:, hl * NSUB:(hl + 1) * NSUB]
                    P.ts(rzh, zc, 1e-30, None, op0=ALU.max)
                    P.add('dve', lambda e, rzh=rzh: e.reciprocal(rzh, rzh), [rzh], [rzh])
                    P.tt(cfh, rzh, gate[:, :, branch * 8 + h], ALU.mult)
                    for s in range(NSUB):
                        dst = att[:, s, h * 64:(h + 1) * 64]
                        src = po[:, o0 + s * 65:o0 + s * 65 + 64]
                        if first_branch:
                            P.ts(dst, src, coef[:, hl * NSUB + s:hl * NSUB + s + 1], None, op0=ALU.mult)
                        else:
                            P.stt(dst, src, coef[:, hl * NSUB + s:hl * NSUB + s + 1], dst, ALU.mult, ALU.add)

            P.tag = 'pF'
            for hp in range(4):
                g = hp // 2
                tiles = []
                for c in range(2):
                    if c == 1 and q0 < 2048:
                        continue
                    rel = (q0 - 2048 * c) // TQ
                    mk = cmpm[:, rel, :] if rel < NCM else None
                    tiles.append((kcT[:, g, c * 128:(c + 1) * 128], 0, mk, vca[:, c, g, :], ovl[:, c, :]))
                attend(hp, 0, tiles, True)
                for hl in range(2):
                    for s in range(NSUB):
                        dst = imp[:, s, g, :]
                        src = psI[:, hl * 64 * NSUB + s * 64:hl * 64 * NSUB + (s + 1) * 64]
                        rzc = rz[:, hl * NSUB + s:hl * NSUB + s + 1]
                        if hp % 2 == 0 and hl == 0:
                            P.ts(dst, src, rzc, None, op0=ALU.mult)
                        else:
                            P.stt(dst, src, rzc, dst, ALU.mult, ALU.add)
            P.tag = 'pG1'
            for s in range(NSUB):
                for g in range(2):
                    P.tt(impw[:], imp[:, s, g, :], va[:, s, 0:64], ALU.mult)
                    P.tt(impf[:], impw[:], va[:, s, 64:128], ALU.add)
                    P.add('dve', lambda e: e.max(m8[:, 0:8], impf[:]), [impf[:]], [m8[:, 0:8]])
                    P.add('dve', lambda e: e.match_replace(impw[:], m8[:, 0:8], impf[:], -1e30),
                          [m8[:, 0:8], impf[:]], [impw[:]])
                    P.add('dve', lambda e: e.max(m8[:, 8:16], impw[:]), [impw[:]], [m8[:, 8:16]])
                    P.ts(msk[:], impf[:], m8[:, 15:16], None, op0=ALU.is_ge)
                    P.ts(SELB[:, s, g, :, 64:96], msk[:].rearrange("p (a b) -> p a b", a=2), -1.0, -NEGM,
                         op0=ALU.add, op1=ALU.mult)
            P.tag = 'pI'
            for hp in range(4):
                g = hp // 2
                tiles = []
                for i in range(4 + NSUB):
                    k0 = q0 - 512 + 128 * i
                    if k0 < 0:
                        continue
                    mk = far[:, (4 - i) - 1, :] if i < 4 else caus[:, i - 4, :]
                    if i < 4 and not far_nz[(4 - i) - 1]:
                        mk = None
                    col = k0 % 1024
                    tiles.append((KTw[:, g, col:col + 128], 0, mk, Vw[:, (k0 // 128) % 8, g, :], None))
                attend(hp, 2, tiles, False)
            P.tag = 'pG2'
            for s in range(NSUB):
                for g in range(2):
                    for G in range(2 if useG1 else 1):
                        pst = nextM()
                        P.tr(pst[0:96, 0:128], SELB[:, s, g, G, :], identf[:])
                        P.copy(QTa[64:96, G, g * 4:(g + 1) * 4, s * 128:(s + 1) * 128],
                               pst[64:96, 0:128].unsqueeze(1).broadcast_to([32, 4, 128]))
            P.tag = 'pH'
            for hp in range(4):
                g = hp // 2
                tiles = []
                for t in range(q0 // 128 + NSUB):
                    mk = caus[:, (t * 128 - q0) // 128, :] if t * 128 >= q0 else None
                    tiles.append((KTs[g][:, t * 128:(t + 1) * 128], t // 16, mk, Vs[:, t, g, :], None))
                attend(hp, 1, tiles, False)
            P.tag = 'pJ'
            for s in range(NSUB):
                for c in range(4):
                    P.tr(psT[:, c * 128:(c + 1) * 128], convo[:, c, s * 128:(s + 1) * 128], identb[:])
                for g in range(2):
                    P.tr(psT[:, 512 + g * 128:512 + (g + 1) * 128], convo[:, 4 + g, s * 128:(s + 1) * 128], identb[:])
                P.copy(xtm1[:], psT[:, 0:512], eng='act')
                P.copy(Btm1[:], psT[:, 512:768].rearrange("p (g n) -> p g n", g=2))
                ssd_chunk(K, T, 128,
                          [convo[:, 4 + g, s * 128:(s + 1) * 128] for g in range(2)],
                          [convo[:, 6 + g, s * 128:(s + 1) * 128] for g in range(2)],
                          xtm1[:], [Btm1[:, g, :] for g in range(2)], dtv[:, s, :], av[:, s, :], zs[:, s, :],
                          hTf, hTb, ymix[:, s, :])
            P.tag = 'pK'
            mixT = hT
            for s in range(NSUB):
                P.copy(attb[:], att[:, s, :], eng='pool')
                for c in range(4):
                    P.tr(psT[:, c * 128:(c + 1) * 128], attb[:, c * 128:(c + 1) * 128], identb[:])
                for c in range(4):
                    P.tr(psT[:, 512 + c * 128:512 + (c + 1) * 128], ymix[:, s, c * 128:(c + 1) * 128], identb[:])
                for c in range(4):
                    P.copy(mixT[:, c, s * 128:(s + 1) * 128], psT[:, c * 128:(c + 1) * 128], eng='act')
                for c in range(4):
                    P.ts(mixT[:, 4 + c, s * 128:(s + 1) * 128], psT[:, 512 + c * 128:512 + (c + 1) * 128],
                         K['snw'][:, c:c + 1], None, op0=ALU.mult)
            for s in range(NSUB):
                r0 = base + q0 + s * 128
                P.dma(xres[:], x1_d[r0:r0 + 128, :])
                for h2 in range(2):
                    ps = nextM()
                    for k in range(8):
                        P.mm(ps[:, :], mixT[:, k, s * 128:(s + 1) * 128], Wout[:, k, h2 * 512:(h2 + 1) * 512],
                             start=(k == 0), stop=(k == 7))
                    P.tt(xres[:, h2 * 512:(h2 + 1) * 512], ps[:, :], xres[:, h2 * 512:(h2 + 1) * 512], ALU.add)
                P.dma(x2_d[r0:r0 + 128, :], xres[:])
                if cfg.dbg:
                    P.dma(dbg['att'][r0:r0 + 128, :], att[:, s, :])
                    P.dma(dbg['x2'][r0:r0 + 128, :], xres[:])
        pst = PSF[0]
        for c in range(8):
            P.tr(pst[0:3, c * 128:(c + 1) * 128] if False else PSF[c % 2][0:3, 0:128], hist[:, c, :], identf[:])
            P.copy(ostg[0:3, c * 128:(c + 1) * 128], PSF[c % 2][0:3, 0:128])
        P.dma(o_conv_p[sq, :, :], ostg[0:3, :])
        for c in range(4):
            ps = PSF[c % 2]
            P.tr(ps[:, 0:128], hTf[:, c * 128:(c + 1) * 128], identf[:])
            P.copy(ostg[:, c * 128:(c + 1) * 128], ps[:, 0:128])
            P.dma(o_ssm_p[sq, c * 128:(c + 1) * 128, :], ostg[:, c * 128:(c + 1) * 128])
        sc.close()

    def sample_group(K):
        P.tag = 's_group'
        sc = Scope()
        identf, identb, Win = K['identf'], K['identb'], K['Win']
        stat, junk = K['stat'], K['junk']
        NB = cfg.nseq_s
        hTs = sc.sb("g_hTs", [128, 8, NS], BF16)
        hb = sc.sb("g_hb", [128, D], BF16)
        xin = sc.sb("g_xin", [128, D], F32)
        qTs = sc.sb("g_qTs", [128, 8, NS], BF16)
        ksTs = sc.sb("g_ksTs", [128, 2, NS], BF16)
        kwTs = sc.sb("g_kwTs", [128, 2, NS], BF16)
        xbcT = sc.sb("g_xbcT", [128, 8, NS], F32)
        histT = sc.sb("g_histT", [128, 8, NB * 3], F32)
        ust = [sc.sb("g_ust%d" % i, [128, 512], F32) for i in range(2)]
        psM = PSF[0:2]
        psT = PSB
        mcnt = [0]

        def nextM():
            mcnt[0] += 1
            return psM[mcnt[0] % 2]

        for t0 in range(0, NS, 128):
            L = min(128, NS - t0)
            P.dma(xin[0:L, :], x1_d[NTP + t0:NTP + t0 + L, :])
            r = rms_rstd(stat, xin[0:L, :], L, 0, junk)
            P.ts(hb[0:L, :], xin[0:L, :], r, None, op0=ALU.mult)
            for k in range(8):
                P.tr(psT[:, k * 128:k * 128 + L], hb[0:L, k * 128:(k + 1) * 128], identb[0:L, 0:L])
            for k in range(8):
                P.ts(hTs[:, k, t0:t0 + L], psT[:, k * 128:k * 128 + L], K['nwT'][:, k:k + 1], None, op0=ALU.mult)

        def proj_fm(col0, ncol):
            ps = nextM()
            for k in range(8):
                P.mm(ps[0:ncol, 0:NS], Win[:, k, col0:col0 + ncol], hTs[:, k, :], start=(k == 0), stop=(k == 7))
            return ps

        for h in range(8):
            ps = proj_fm(Q0 + h * 64, 64)
            P.act(qTs[0:64, h, :], ps[0:64, 0:NS], AF.Copy, scale=0.125)
        for g in range(2):
            ps = proj_fm(KS + g * 64, 64)
            P.copy(ksTs[0:64, g, :], ps[0:64, 0:NS])
            ps = proj_fm(KW + g * 64, 64)
            P.copy(kwTs[0:64, g, :], ps[0:64, 0:NS])
        for c in range(8):
            ps = proj_fm(XB + c * 128, 128)
            P.copy(xbcT[:, c, :], ps[:, 0:NS], eng='act')
        for r0 in range(0, NB * 3, 96):
            L = min(96, NB * 3 - r0)
            P.dma(xin[0:L, :], state_conv[r0:r0 + L, :])
            for c in range(8):
                ps = nextM()
                P.tr(ps[:, 0:L], xin[0:L, c * 128:(c + 1) * 128], identf[0:L, 0:L])
                P.copy(histT[:, c, r0:r0 + L], ps[:, 0:L])

        P.dma(q_scr, qTs[0:64, :, :])
        P.dma(ks_scr, ksTs[0:64, :, :])
        P.dma(kw_scr, kwTs[0:64, :, :])
        P.dma(xb_scr, xbcT[:])
        P.dma(hs_scr, histT[:])
        gi = 0
        for t0 in range(0, NS, 128):
            L = min(128, NS - t0)
            for (c0, cwid, d0) in ((512, 512, 0), (1024, 280, 512), (Z0, 512, 792), (DTO, 8, 1304),
                                   (XB, 512, 1312), (XB + 512, 512, 1824)):
                ps = nextM()
                for k in range(8):
                    P.mm(ps[0:L, 0:cwid], hTs[:, k, t0:t0 + L], Win[:, k, c0:c0 + cwid], start=(k == 0), stop=(k == 7))
                st_ = ust[gi % 2]
                P.copy(st_[0:L, 0:cwid], ps[0:L, 0:cwid], eng=('act' if gi % 2 else 'dve'))
                gi += 1
                P.dma(u_scr[t0:t0 + L, d0:d0 + cwid], st_[0:L, 0:cwid])
        sc.close()

    def mixer_sample(K):
        P.tag = 's_init'
        sc = Scope()
        identf, identb, Win, Wout = K['identf'], K['identb'], K['Win'], K['Wout']
        stat, junk = K['stat'], K['junk']
        NB = cfg.nseq_s
        NPG = cfg.npages
        NKT = cfg.nkt_s
        NCH = (cfg.ncmp_s + 127) // 128
        NCB = cfg.ncmp_s
        NSL = cfg.nsel_s
        NG = (NSL + 31) // 32
        caus8 = sc.sb("caus8", [128, 32], BF16)
        far8 = sc.sb("far8", [128, 32], BF16)
        ovls = sc.sb("ovls", [128, NCH, NSL], BF16)
        vas = sc.sb("vas", [NQS, 2, NSL], F32)
        iot = sc.sb("iot", [128, NPG], F32)
        pidxf = sc.sb("pidxf", [128, NPG], F32)
        P.dma(caus8[:], C['caus8'])
        P.dma(far8[:], C['far8'])
        cmpms = sc.sb("cmpms", [128, 32], BF16)
        P.dma(cmpms[:], C['cmpm_s'])
        P.dma(ovls[:], C['ovl_s'])
        P.dma(vas[:], C['va_s'])
        P.dma(iot[:], iota_p)
        xin = sc.sb("xin_s", [128, 512], F32)
        qTs = sc.sb("qTs", [128, 8, NS], BF16)
        ksTs = sc.sb("ksTs", [128, 2, NS], BF16)
        kwTs = sc.sb("kwTs", [128, 2, NS], BF16)
        xbcT = sc.sb("xbcT", [128, 8, NS], F32)
        histT = sc.sb("histT", [128, 8, NB * 3], F32)
        xps = sc.sb("xps", [128, 8, 11], F32)
        accs = sc.sb("accs", [128, NQS], F32)
        convs = sc.sb("convs", [128, 8, NQS], BF16)
        ub = sc.sb("ub", [NQS, 1312], F32)
        pidx = sc.sb("pidx", [128, NPG], I32)
        pidxu = [sc.sb("pidxu%d" % i, [128, NPG], U32) for i in range(2)]
        raws = sc.sb("raws", [128, 2, NPG * 128], BF16)
        NPGB = 8
        PG = [sc.sb("PG%d" % i, [128, 256], F32) for i in range(NPGB)]
        Hs = [sc.sb("Hs%d" % i, [128, 2, NCH * 128], BF16) for i in range(2)]
        kcTs = sc.sb("kcTs", [128, 2, NCH * 128], BF16)
        vcs = sc.sb("vcs", [128, NCH, 2, 65], BF16)
        KTs = sc.sb("KTss", [128, 2, NKT * 128], BF16)
        Vs = sc.sb("Vss", [128, NKT, 2, 65], BF16)
        KTw = sc.sb("KTws", [128, 2, 640], BF16)
        Vw = sc.sb("Vws", [128, 5, 2, 65], BF16)
        QTa = sc.sb("QTas", [128, NG, 2, 32], BF16)
        PT = [sc.sb("PTs%d" % i, [128, 32], BF16) for i in range(3)]
        gate = sc.sb("gate_s", [NQS, 24], F32)
        zs = sc.sb("zs_s", [NQS, 512], BF16)
        dtv = sc.sb("dtv_s", [NQS, 8], F32)
        av = sc.sb("av_s", [NQS, 8], F32)
        att = sc.sb("att_s", [NQS, 512], F32)
        attb = sc.sb("attb_s", [NQS, 512], BF16)
        ymix = sc.sb("ymix_s", [NQS, 512], BF16)
        rz = sc.sb("rz_s", [NQS, 8], F32)
        coef = sc.sb("coef_s", [NQS, 8], F32)
        imp = sc.sb("imp_s", [NQS, 2, NSL], F32)
        NSLP = NG * 32
        impf = sc.sb("impf_s", [NQS, NSL], F32)
        impw = sc.sb("impw_s", [NQS, NSL], F32)
        m8 = sc.sb("m8_s", [NQS, 16], F32)
        msk = sc.sb("msk_s", [NQS, NSLP], F32)
        SELB = sc.sb("SELB_s", [NQS, NG, 96], F32)
        xtm = sc.sb("xtm_s", [NQS, 512], BF16)
        Btm = sc.sb("Btm_s", [NQS, 2, 128], BF16)
        hTf = sc.sb("hTf_s", [128, 512], F32)
        hTb = sc.sb("hTb_s", [128, 512], BF16)
        mixT = sc.sb("mixT_s", [128, 8, NQS], BF16)
        xres = sc.sb("xres_s", [NQS, D], F32)
        ostg = xin
        T = ssd_tmp(sc)
        psM = PSF[0:2]
        psS = PSF[2:4]
        psO = PSF[4]
        psI = PSF[5:7]
        psT = PSB
        mcnt = [0]

        def nextM():
            mcnt[0] += 1
            return psM[mcnt[0] % 2]

        P.dma(qTs[0:64, :, :], q_scr)
        P.dma(ksTs[0:64, :, :], ks_scr)
        P.dma(kwTs[0:64, :, :], kw_scr)
        P.dma(xbcT[:], xb_scr)
        P.dma(histT[:], hs_scr)

        P.memset(KTs[0:64, :, :], 0.0, eng='pool')
        P.memset(KTw[0:64, :, :], 0.0, eng='pool')
        for g in range(2):
            P.dma(KTs[64:128, g, :], C['kc_sel_s'])
            P.dma(KTw[64:128, g, :], C['kc_win_s'])
            P.dma(kcTs[64:128, g, :], C['kc_cmp_s'])
        P.memset(kcTs[0:64, :, :], 0.0)
        P.memset(Vs[:], 0.0, eng='pool')
        P.memset(Vs[:, 0:NPG, :, 64:65], 1.0, eng='pool')
        P.memset(Vs[0:NQS, NPG, :, 64:65], 1.0, eng='pool')
        P.memset(Vw[:], 0.0, eng='pool')
        P.memset(Vw[:, 0:4, :, 64:65], 1.0, eng='pool')
        P.memset(Vw[0:NQS, 4, :, 64:65], 1.0, eng='pool')
        P.memset(vcs[:], 0.0, eng='pool')
        P.memset(vcs[:, :, :, 64:65], 1.0, eng='pool')
        for i in range(2):
            P.memset(Hs[i][:], 0.0)
        P.memset(QTa[:], 0.0)
        for G in range(NG):
            P.dma(QTa[96:128, G, :, :], C['alq_s'])
        P.memset(SELB[:], 0.0)
        P.memset(msk[:], 0.0)

        def gather_page(dst, cache, idx):
            P.add('pool', lambda e: e.indirect_dma_start(out=dst, out_offset=None, in_=cache,
                                                          in_offset=bass.IndirectOffsetOnAxis(ap=idx, axis=0)),
                  [cache, idx], [dst], is_dma=True)

        def attend_s(g, tiles, branch, first_branch, with_imp, bg=None):
            n_t = len(tiles)
            for idx in range(n_t + 1):
                adv(bg, 1)
                if idx < n_t:
                    kt, rq, mk, vv, ov = tiles[idx]
                    sb_ = psS[idx % 2]
                    P.mm(sb_[:, 0:32], kt, rq, start=True, stop=(mk is None))
                    if mk is not None:
                        P.mm(sb_[:, 0:32], identb[:], mk, start=False, stop=True)
                    P.act(PT[idx % 3][:], sb_[:, 0:32], AF.Exp)
                if idx >= 1:
                    j = idx - 1
                    kt, rq, mk, vv, ov = tiles[j]
                    pt = PT[j % 3]
                    for hh in range(4):
                        P.mm(psO[0:NQS, hh * 65:(hh + 1) * 65], pt[:, hh * NQS:(hh + 1) * NQS], vv,
                             start=(j == 0 and hh == 0), stop=(j == n_t - 1), skip_group_check=True)
                    if with_imp:
                        for hh in range(4):
                            P.mm(psI[hh // 2][0:NQS, (hh % 2) * NSL:(hh % 2 + 1) * NSL], pt[:, hh * NQS:(hh + 1) * NQS], ov,
                                 start=(j == 0 and hh % 2 == 0), stop=(j == n_t - 1), skip_group_check=True)
            zc = psO[0:NQS, 64:64 + 65 * 3 + 1:65]
            P.ts(rz[:, 0:4], zc, 1e-30, None, op0=ALU.max)
            P.add('dve', lambda e: e.reciprocal(rz[:, 0:4], rz[:, 0:4]), [rz[:, 0:4]], [rz[:, 0:4]])
            P.tt(coef[:, 0:4], rz[:, 0:4], gate[:, branch * 8 + g * 4:branch * 8 + g * 4 + 4], ALU.mult)
            for hh in range(4):
                h = g * 4 + hh
                dst = att[:, h * 64:(h + 1) * 64]
                if first_branch:
                    P.ts(dst, psO[0:NQS, hh * 65:hh * 65 + 64], coef[:, hh:hh + 1], None, op0=ALU.mult)
                else:
                    P.stt(dst, psO[0:NQS, hh * 65:hh * 65 + 64], coef[:, hh:hh + 1], dst, ALU.mult, ALU.add)
            if with_imp:
                for hh in range(4):
                    src = psI[hh // 2][0:NQS, (hh % 2) * NSL:(hh % 2 + 1) * NSL]
                    if hh == 0:
                        P.ts(imp[:, g, :], src, rz[:, hh:hh + 1], None, op0=ALU.mult)
                    else:
                        P.stt(imp[:, g, :], src, rz[:, hh:hh + 1], imp[:, g, :], ALU.mult, ALU.add)

        pgc = [0]

        def adv(gen, n):
            if gen is None:
                return
            for _ in range(n):
                if next(gen, 'end') == 'end':
                    return

        def prep_idx(bb):
            P.dma(pidx[:], page_table[bb, :].partition_broadcast(128))
            P.ts(pidxf[:], pidx[:], 128.0, None, op0=ALU.mult)
            P.tt(pidxu[bb % 2][:], pidxf[:], iot[:], ALU.add)

        def gen_cmp(bb):
            for p in range(NPG):
                pg = PG[pgc[0] % NPGB]
                pgc[0] += 1
                gather_page(pg[:], cache_cmp, pidxu[bb % 2][:, p:p + 1])
                for i in range(2):
                    ps = nextM()
                    P.tr(ps[:, 0:128], pg[:, i * 128:(i + 1) * 128], identf[:])
                    P.copy(raws[:, i, :].rearrange("p (r c) -> p r c", r=16)[:, :, 8 * p:8 * p + 8],
                           ps[:, 0:128].rearrange("p (c r) -> p r c", r=16), eng=('act' if i == 0 else 'dve'))
                yield 1

        def gen_sel(bb):
            for p in range(NPG):
                pg = PG[pgc[0] % NPGB]
                pgc[0] += 1
                gather_page(pg[:], cache_sel, pidxu[bb % 2][:, p:p + 1])
                for g in range(2):
                    ps = nextM()
                    P.tr(ps[0:64, 0:128], pg[:, g * 64:(g + 1) * 64], identf[:])
                    P.copy(KTs[0:64, g, p * 128:(p + 1) * 128], ps[0:64, 0:128], eng=('act' if g == 0 else 'dve'))
                P.copy(Vs[:, p, :, 0:64], pg[:, 128:256].rearrange("p (g d) -> p g d", g=2), eng='act')
                yield 1

        prep_idx(0)
        gcur = gen_cmp(0)
        gsel = gen_sel(0)
        for b in range(NB):
            tb = b * NQS
            P.tag = 's_proj'
            P.dma(ub[:], u_scr[tb:tb + NQS, 0:1312])
            P.dma(xres[:], u_scr[tb:tb + NQS, 1312:2336])
            P.dma(o_cmp_s[tb:tb + NQS, :], ub[:, 0:256])
            P.dma(o_sel_s[tb:tb + NQS, :], ub[:, 256:512])
            P.dma(o_win_s[b, 512 - NQS:512, :], ub[:, 512:768])
            P.dma(o_win_s[b, 0:512 - NQS, :], cache_win[b, NQS:512, :])
            P.dma(o_conv_s[b, :, :], xres[NQS - 3:NQS, :])
            P.act(gate[:], ub[:, 768:792], AF.Sigmoid)
            P.act(zs[:], ub[:, 792:1304], AF.Silu)
            P.tt(dtv[:], ub[:, 1304:1312], K['dtb'][0:NQS, :], ALU.add)
            P.copy(xps[:, :, 0:3], histT[:, :, b * 3:(b + 1) * 3], eng='pool')
            P.copy(xps[:, :, 3:11], xbcT[:, :, tb:tb + NQS], eng='pool')
            for c in range(8):
                P.ts(accs[:], xps[:, c, 0:NQS], K['cw'][:, c, 0:1], K['cb'][:, c:c + 1], op0=ALU.mult, op1=ALU.add)
                for tap in range(1, 4):
                    P.stt(accs[:], xps[:, c, tap:tap + NQS], K['cw'][:, c, tap:tap + 1], accs[:], ALU.mult, ALU.add)
                P.act(convs[:, c, :], accs[:], AF.Silu)
            P.act(dtv[:], dtv[:], AF.Exp)
            P.act(dtv[:], dtv[:], AF.Ln, bias=1.0)
            P.tt(av[:], dtv[:], K['Aneg'][0:NQS, :], ALU.mult)
            if b + 1 < NB:
                prep_idx(b + 1)
            for g in range(2):
                P.copy(QTa[0:64, :, g, :].rearrange("p a (h t) -> p a h t", h=4),
                       qTs[0:64, g * 4:(g + 1) * 4, tb:tb + NQS].unsqueeze(1).broadcast_to([64, NG, 4, NQS]))
            P.tag = 's_cmp'
            adv(gcur, NPG)
            for i in range(2):
                for g in range(2):
                    for j0 in range(0, NCB, 512):
                        nb = min(512, NCB - j0)
                        ps = nextM()
                        for r in range(32):
                            st_ = (r % 16) * (NPG * 8) + j0 + r // 16
                            P.mm(ps[:, 0:nb], K['W1'][i][g * 64:(g + 1) * 64, r, :],
                                 raws[g * 64:(g + 1) * 64, i, st_:st_ + nb],
                                 start=(r == 0), stop=(r == 31))
                        P.act(Hs[i][:, g, j0:j0 + nb], ps[:, 0:nb], AF.Silu, bias=K['b1'][i][:, 0:1])
                        adv(gsel, 6)
            for g in range(2):
                for j0 in range(0, NCB, 512):
                    nb = min(512, NCB - j0)
                    ps = nextM()
                    P.mm(ps[0:64, 0:nb], K['W2'][0][:, :], Hs[0][:, g, j0:j0 + nb], start=True, stop=True)
                    P.copy(kcTs[0:64, g, j0:j0 + nb], ps[0:64, 0:nb])
                for c in range(NCH):
                    ps = nextM()
                    P.mm(ps[:, 0:64], Hs[1][:, g, c * 128:(c + 1) * 128], K['W2'][1][:, :], start=True, stop=True)
                    P.copy(vcs[:, c, g, 0:64], ps[:, 0:64], eng='act')
            for g in range(2):
                tiles = [(kcTs[:, g, c * 128:(c + 1) * 128], QTa[:, 0, g, :], (cmpms[:] if c == NCH - 1 else None),
                          vcs[:, c, g, :], ovls[:, c, :]) for c in range(NCH)]
                attend_s(g, tiles, 0, True, True)
            P.tag = 's_topk'
            for g in range(2):
                P.tt(impw[:], imp[:, g, :], vas[:, 0, :], ALU.mult)
                P.tt(impf[:], impw[:], vas[:, 1, :], ALU.add)
                P.add('dve', lambda e: e.max(m8[:, 0:8], impf[:]), [impf[:]], [m8[:, 0:8]])
                P.add('dve', lambda e: e.match_replace(impw[:], m8[:, 0:8], impf[:], -1e30),
                      [m8[:, 0:8], impf[:]], [impw[:]])
                P.add('dve', lambda e: e.max(m8[:, 8:16], impw[:]), [impw[:]], [m8[:, 8:16]])
                P.ts(msk[:, 0:NSL], impf[:], m8[:, 15:16], None, op0=ALU.is_ge)
                P.ts(SELB[:, :, 64:96], msk[:].rearrange("p (a b) -> p a b", a=NG), -1.0, -NEGM,
                     op0=ALU.add, op1=ALU.mult)
                for G in range(NG):
                    pst = nextM()
                    P.tr(pst[0:96, 0:NQS], SELB[:, G, :], identf[0:NQS, 0:NQS])
                    P.copy(QTa[64:96, G, g, :].rearrange("p (h t) -> p h t", h=4),
                           pst[64:96, 0:NQS].unsqueeze(1).broadcast_to([32, 4, NQS]))
            P.tag = 's_sel'
            adv(gsel, NPG)
            gcur = gen_cmp(b + 1) if b + 1 < NB else None
            for g in range(2):
                P.copy(KTs[0:64, g, NPG * 128:NPG * 128 + NQS], ksTs[0:64, g, tb:tb + NQS])
            P.copy(Vs[0:NQS, NPG, :, 0:64], ub[:, 384:512].rearrange("p (g d) -> p g d", g=2))
            for g in range(2):
                tiles = []
                for t in range(NKT):
                    mk = caus8[:] if t == NPG else None
                    tiles.append((KTs[:, g, t * 128:(t + 1) * 128], QTa[:, t // 16, g, :], mk, Vs[:, t, g, :], None))
                attend_s(g, tiles, 1, False, False, bg=gcur)
            P.tag = 's_win'
            adv(gcur, NPG)
            gsel = gen_sel(b + 1) if b + 1 < NB else None
            for t in range(4):
                pg = PG[pgc[0] % NPGB]
                pgc[0] += 1
                P.dma(pg[:], cache_win[b, t * 128:(t + 1) * 128, :])
                for g in range(2):
                    ps = nextM()
                    P.tr(ps[0:64, 0:128], pg[:, g * 64:(g + 1) * 64], identf[:])
                    P.copy(KTw[0:64, g, t * 128:(t + 1) * 128], ps[0:64, 0:128], eng=('act' if g == 0 else 'dve'))
                P.copy(Vw[:, t, :, 0:64], pg[:, 128:256].rearrange("p (g d) -> p g d", g=2), eng='act')
            for g in range(2):
                P.copy(KTw[0:64, g, 512:512 + NQS], kwTs[0:64, g, tb:tb + NQS])
            P.copy(Vw[0:NQS, 4, :, 0:64], ub[:, 640:768].rearrange("p (g d) -> p g d", g=2))
            for g in range(2):
                tiles = []
                for t in range(5):
                    mk = far8[:] if t == 0 else (caus8[:] if t == 4 else None)
                    tiles.append((KTw[:, g, t * 128:(t + 1) * 128], QTa[:, 0, g, :], mk, Vw[:, t, g, :], None))
                attend_s(g, tiles, 2, False, False, bg=gsel)
            P.tag = 's_ssd'
            for c in range(4):
                ps = nextM()
                P.dma(ostg[:, 0:128], state_ssm[b, c * 128:(c + 1) * 128, :])
                P.tr(ps[:, 0:128], ostg[:, 0:128], identf[:])
                P.copy(hTf[:, c * 128:(c + 1) * 128], ps[:, 0:128])
            P.copy(hTb[:], hTf[:], eng='pool')
            adv(gsel, 8)
            for c in range(4):
                P.tr(psT[0:NQS, c * 128:(c + 1) * 128], convs[:, c, :], identb[:])
            for g in range(2):
                P.tr(psT[0:NQS, 512 + g * 128:512 + (g + 1) * 128], convs[:, 4 + g, :], identb[:])
            P.copy(xtm[:], psT[0:NQS, 0:512], eng='act')
            P.copy(Btm[:], psT[0:NQS, 512:768].rearrange("p (g n) -> p g n", g=2))
            ssd_chunk(K, T, NQS,
                      [convs[:, 4 + g, :] for g in range(2)],
                      [convs[:, 6 + g, :] for g in range(2)],
                      xtm[:], [Btm[:, g, :] for g in range(2)], dtv[:], av[:], zs[:], hTf, hTb, ymix[:])
            for c in range(4):
                ps = nextM()
                P.tr(ps[:, 0:128], hTf[:, c * 128:(c + 1) * 128], identf[:])
                P.copy(ostg[:, c * 128:(c + 1) * 128], ps[:, 0:128])
                P.dma(o_ssm_s[b, c * 128:(c + 1) * 128, :], ostg[:, c * 128:(c + 1) * 128])
            P.tag = 's_mix'
            adv(gsel, 8)
            P.copy(attb[:], att[:], eng='pool')
            for c in range(4):
                P.tr(psT[:, c * 128:c * 128 + NQS], attb[:, c * 128:(c + 1) * 128], identb[0:NQS, 0:NQS])
            for c in range(4):
                P.tr(psT[:, 512 + c * 128:512 + c * 128 + NQS], ymix[:, c * 128:(c + 1) * 128], identb[0:NQS, 0:NQS])
            for c in range(4):
                P.copy(mixT[:, c, :], psT[:, c * 128:c * 128 + NQS], eng='act')
                P.ts(mixT[:, 4 + c, :], psT[:, 512 + c * 128:512 + c * 128 + NQS], K['snw'][:, c:c + 1], None,
                     op0=ALU.mult)
            P.dma(xres[:], x1_d[NTP + tb:NTP + tb + NQS, :])
            for h2 in range(2):
                ps = nextM()
                for k in range(8):
                    P.mm(ps[0:NQS, :], mixT[:, k, :], Wout[:, k, h2 * 512:(h2 + 1) * 512], start=(k == 0), stop=(k == 7))
                P.tt(xres[:, h2 * 512:(h2 + 1) * 512], ps[0:NQS, :], xres[:, h2 * 512:(h2 + 1) * 512], ALU.add)
            P.dma(x2_d[NTP + tb:NTP + tb + NQS, :], xres[:])
            if cfg.dbg:
                P.dma(dbg['att'][NTP + tb:NTP + tb + NQS, :], att[:])
                P.dma(dbg['x2'][NTP + tb:NTP + tb + NQS, :], xres[:])
        sc.close()

    cur = x_in
    if 'ffn1' in cfg.stages:
        ffn_phase("ffn1", x_in, x1_d, False)
    if 'mixp' in cfg.stages or 'mixs' in cfg.stages:
        mixer_phase()
    if 'mixs' not in cfg.stages and NS > 0:
        P.dma(x2_d[NTP:NT, :], x1_d[NTP:NT, :])
    if 'mixp' not in cfg.stages and NTP > 0:
        P.dma(x2_d[0:NTP, :], x1_d[0:NTP, :])
    if 'ffn2' in cfg.stages:
        ffn_phase("ffn2", x2_d, y_out, True)
    P.finalize(top)
    top.close()
    return nc, P, hc


WNAMES = ("ffn1_wg", "ffn1_wu", "ffn1_wd", "ffn2_wg", "ffn2_wu", "ffn2_wd", "norm_ffn1", "norm_ffn2", "norm_mix",
          "w_in", "w_out", "cmp_pe_k", "cmp_w1_k", "cmp_w2_k", "cmp_pe_v", "cmp_w1_v", "cmp_w2_v", "conv_w",
          "conv_b", "dt_bias", "a_log", "d_skip", "ssm_norm")


def make_in_maps(cfg, hc, inp, ncores):
    nsp, nss = cfg.nseq_p, cfg.nseq_s
    base = {}
    for n in WNAMES:
        base[n] = np.ascontiguousarray(inp[n][0], dtype=np.float32)
    base['norm_final'] = np.ascontiguousarray(inp['norm_final'], dtype=np.float32)
    for k, v in hc.items():
        base['c_' + k] = v
    base['cache_cmp'] = np.ascontiguousarray(inp['cache_cmp'][0]).reshape(-1, 256)
    base['cache_sel'] = np.ascontiguousarray(inp['cache_sel'][0]).reshape(-1, 256)
    base['iota_p'] = np.ascontiguousarray(np.tile(np.arange(128, dtype=np.float32)[:, None], (1, cfg.npages)))
    maps = []
    for c in range(ncores):
        m = dict(base)
        xp = inp['x_prompt'][c * nsp:(c + 1) * nsp].reshape(-1, D)
        xs = inp['x_sample'][c * nss:(c + 1) * nss].reshape(-1, D)
        m['x_in'] = np.ascontiguousarray(np.concatenate([xp, xs], axis=0))
        m['cache_win'] = np.ascontiguousarray(inp['cache_win'][0, c * nss:(c + 1) * nss]).reshape(nss, 512, 256)
        m['state_conv'] = np.ascontiguousarray(inp['state_conv'][0, c * nss:(c + 1) * nss]).reshape(nss * 3, 1024)
        m['state_ssm'] = np.ascontiguousarray(inp['state_ssm'][0, c * nss:(c + 1) * nss]).reshape(nss, 512, 128)
        m['page_table'] = np.ascontiguousarray(inp['page_table'][c * nss:(c + 1) * nss]).astype(np.int32)
        maps.append(m)
    return maps


def assemble(cfg, results, ncores):
    nsp, nss, S = cfg.nseq_p, cfg.nseq_s, cfg.seq
    cat = lambda name: [np.asarray(r[name]) for r in results]
    y = cat('y_out')
    y_p = np.concatenate([a[:cfg.ntok_p].reshape(nsp, S, D) for a in y], 0)
    y_s = np.concatenate([a[cfg.ntok_p:].reshape(nss, NQS, D) for a in y], 0)
    ncp = np.concatenate([a.reshape(nsp, S, 2, 2, 64) for a in cat('o_cmp_p')], 0)[None]
    nsl = np.concatenate([a.reshape(nsp, S, 2, 2, 64) for a in cat('o_sel_p')], 0)[None]
    nwp = np.concatenate([a.reshape(nsp, 512, 2, 2, 64) for a in cat('o_win_p')], 0)[None]
    ncv = np.concatenate([a.reshape(nsp, 3, 1024) for a in cat('o_conv_p')], 0)[None]
    nsm = np.concatenate([a.reshape(nsp, 8, 64, 128) for a in cat('o_ssm_p')], 0)[None]
    scp = np.concatenate([a.reshape(nss, NQS, 2, 2, 64) for a in cat('o_cmp_s')], 0)[None]
    ssl = np.concatenate([a.reshape(nss, NQS, 2, 2, 64) for a in cat('o_sel_s')], 0)[None]
    swp = np.concatenate([a.reshape(nss, 512, 2, 2, 64) for a in cat('o_win_s')], 0)[None]
    scv = np.concatenate([a.reshape(nss, 3, 1024) for a in cat('o_conv_s')], 0)[None]
    ssm = np.concatenate([a.reshape(nss, 8, 64, 128) for a in cat('o_ssm_s')], 0)[None]
    return (y_p, y_s, ncp, nsl, nwp, ncv, nsm, scp, ssl, swp, scv, ssm)


def kernel(**inputs):
    inputs = {k: np.asarray(v) for k, v in inputs.items()}
    B, S = inputs['x_prompt'].shape[0], inputs['x_prompt'].shape[1]
    NBS = inputs['x_sample'].shape[0]
    npages = inputs['page_table'].shape[1]
    ncores = 8
    cfg = Cfg(nseq_p=B // ncores, seq=S, nseq_s=NBS // ncores, past=npages * 128,
              n_phys=inputs['cache_cmp'].shape[1])
    nc, P, hc = build(cfg)
    maps = make_in_maps(cfg, hc, inputs, ncores)
    res = run_bass_kernel_spmd(nc, maps, core_ids=list(range(ncores)))
    outs = assemble(cfg, res.results, ncores)
    return tuple(np.ascontiguousarray(o, dtype=np.float32) for o in outs)
```

```python
import numpy as np
import ml_dtypes
from contextlib import ExitStack
import concourse.bass as bass
import concourse.mybir as mybir
from concourse.bass_utils import run_bass_kernel_spmd


F32 = mybir.dt.float32
BF16 = mybir.dt.bfloat16
I32 = mybir.dt.int32
U32 = mybir.dt.uint32
AF = mybir.ActivationFunctionType
ALU = mybir.AluOpType
AX = mybir.AxisListType

DMA_K = 6
COMPUTE = ('pe', 'act', 'dve', 'pool')


class Op:
    __slots__ = ('stream', 'fn', 'is_dma', 'deps', 'signal', 'sem', 'val', 'waits', 'clock', 'idx', 'tag')


class Prog:
    def __init__(self, nc):
        self.nc = nc
        self.ops = []
        self.streams = {s: [] for s in ('pe', 'act', 'dve', 'pool', 'sp')}
        self.acc = {}
        self.dma_count = {s: 0 for s in self.streams}
        self.pending = {s: set() for s in self.streams}
        self.recent_dma = {s: [] for s in self.streams}

    @staticmethod
    def region(ap):
        t = ap.tensor
        name = t.name
        tn = type(t).__name__
        ext = 0
        aps = list(ap.ap)
        if tn.startswith('DRam'):
            for s, c in aps:
                ext += (c - 1) * abs(s)
            return (name, 0, 1, ap.offset, ap.offset + ext + 1)
        shape = list(t.shape)
        pstride = 1
        for d in shape[1:]:
            pstride *= d
        p0 = ap.offset // pstride
        f0 = ap.offset % pstride
        npart = aps[0][1]
        if tn.startswith('PSum'):
            return (name, 0, 128, 0, 1 << 30)
        for s, c in aps[1:]:
            ext += (c - 1) * abs(s)
        return (name, p0, p0 + npart, f0, f0 + ext + 1)

    def _deps_for(self, idx, stream, is_dma, reads, writes):
        deps = set()
        for is_w, aps in ((False, reads), (True, writes)):
            for ap in aps:
                if ap is None:
                    continue
                name, p0, p1, f0, f1 = self.region(ap)
                if type(ap.tensor).__name__.startswith('PSum'):
                    is_w = True
                lst = self.acc.setdefault(name, [])
                keep = []
                for a in lst:
                    oi, ow, q0, q1, g0, g1 = a
                    ov = not (q1 <= p0 or p1 <= q0 or g1 <= f0 or f1 <= g0)
                    if ov and (is_w or ow) and oi != idx:
                        o = self.ops[oi]
                        same = (o.stream == stream) and not o.is_dma and not is_dma
                        if same and stream == 'pe':
                            pass
                        elif same and stream != 'pool' and (not ow) and is_w:
                            pass
                        else:
                            deps.add(oi)
                    covered = is_w and ov and p0 <= q0 and q1 <= p1 and f0 <= g0 and g1 <= f1
                    if not covered:
                        keep.append(a)
                keep.append([idx, is_w, p0, p1, f0, f1])
                self.acc[name] = keep
        return deps

    def add(self, stream, fn, reads=(), writes=(), is_dma=False):
        op = Op()
        op.idx = len(self.ops)
        op.stream = stream
        op.fn = fn
        op.is_dma = is_dma
        op.signal = is_dma
        op.sem = None
        op.val = 0
        op.waits = []
        op.tag = getattr(self, 'tag', '')
        self.ops.append(op)
        op.deps = self._deps_for(op.idx, stream, is_dma, reads, writes)
        if self.pending[stream]:
            op.deps |= self.pending[stream]
            self.pending[stream] = set()
        if is_dma:
            self.recent_dma[stream] = (self.recent_dma[stream] + [op.idx])[-DMA_K:]
        for d in op.deps:
            self.ops[d].signal = True
        self.streams[stream].append(op)
        return op

    def fence(self):
        F = set()
        for st, lst in self.streams.items():
            for o in reversed(lst):
                if not o.is_dma:
                    F.add(o.idx)
                    break
            F |= set(self.recent_dma[st])
        for st in self.streams:
            self.pending[st] = set(F)
        self.acc = {}

    def dma(self, out, in_, stream='sp', **kw):
        return self.add(stream, lambda e: e.dma_start(out=out, in_=in_, **kw), [in_], [out], is_dma=True)

    def mm(self, out, lhsT, rhs, start=True, stop=True, **kw):
        return self.add('pe', lambda e: e.matmul(out, lhsT, rhs, start=start, stop=stop, **kw),
                        [lhsT, rhs], [out])

    def tr(self, out, in_, ident):
        return self.add('pe', lambda e: e.transpose(out, in_, ident), [in_, ident], [out])

    def act(self, out, in_, func, bias=None, scale=None, accum_out=None, eng='act'):
        rd = [in_]
        kw = {}
        if bias is not None:
            kw['bias'] = bias
            if not isinstance(bias, (int, float)):
                rd.append(bias)
        if scale is not None:
            kw['scale'] = scale
            if not isinstance(scale, (int, float)):
                rd.append(scale)
        wr = [out]
        if accum_out is not None:
            kw['accum_out'] = accum_out
            wr.append(accum_out)
        return self.add('act', lambda e: e.activation(out, in_, func, **kw), rd, wr)

    def tt(self, out, in0, in1, op, eng='dve'):
        return self.add(eng, lambda e: e.tensor_tensor(out, in0, in1, op), [in0, in1], [out])

    def ts(self, out, in0, s1, s2=None, op0=ALU.mult, op1=None, eng='dve', accum_out=None):
        rd = [in0]
        if not isinstance(s1, (int, float)):
            rd.append(s1)
        if s2 is not None and not isinstance(s2, (int, float)):
            rd.append(s2)
        kw = {}
        if op1 is not None:
            kw['op1'] = op1
        wr = [out]
        if accum_out is not None:
            kw['accum_out'] = accum_out
            wr.append(accum_out)
        return self.add(eng, lambda e: e.tensor_scalar(out, in0, s1, s2, op0, **kw), rd, wr)

    def stt(self, out, in0, scalar, in1, op0, op1):
        rd = [in0, in1]
        if not isinstance(scalar, (int, float)):
            rd.append(scalar)
        return self.add('dve', lambda e: e.scalar_tensor_tensor(out, in0, scalar, in1, op0, op1), rd, [out])

    def copy(self, out, in_, eng='dve'):
        if eng == 'act':
            return self.add('act', lambda e: e.copy(out, in_), [in_], [out])
        return self.add(eng, lambda e: e.tensor_copy(out, in_), [in_], [out])

    def memset(self, ap, val, eng='dve'):
        return self.add(eng, lambda e: e.memset(ap, val), [], [ap])

    def finalize(self, stack):
        nc = self.nc
        sems = {}
        for s in COMPUTE:
            sems[s] = stack.enter_context(nc.semaphore('s_' + s))
        dsem = {}
        for s in ('sp', 'pool', 'act'):
            dsem[s] = [stack.enter_context(nc.semaphore('d_%s%d' % (s, i))) for i in range(DMA_K)]
        cnt = {s: 0 for s in COMPUTE}
        dcnt = {}
        dlast = {}
        for op in self.ops:
            if op.is_dma:
                n = dcnt.get(op.stream, 0)
                dcnt[op.stream] = n + 1
                k = n % DMA_K
                op.sem = ('d', op.stream, k)
                op.val = 16 * (n // DMA_K + 1)
                prev = dlast.get((op.stream, k))
                if prev is not None:
                    op.deps.add(prev)
                dlast[(op.stream, k)] = op.idx
            elif op.signal:
                cnt[op.stream] += 1
                op.sem = ('c', op.stream)
                op.val = cnt[op.stream]
        known = {s: {} for s in self.streams}
        for op in self.ops:
            kn = known[op.stream]
            for d in sorted(op.deps):
                p = self.ops[d]
                if kn.get(p.sem, 0) >= p.val:
                    continue
                op.waits.append((p.sem, p.val))
                for k2, v2 in p.clock.items():
                    if kn.get(k2, 0) < v2:
                        kn[k2] = v2
            clk = dict(kn)
            if op.sem is not None:
                clk[op.sem] = op.val
                if not op.is_dma:
                    pass
            op.clock = clk
        self.total_waits = sum(len(o.waits) for o in self.ops)

        def semobj(key):
            if key[0] == 'c':
                return sems[key[1]]
            return dsem[key[1]][key[2]]

        block = stack.enter_context(nc.Block())

        def emit_stream(eng, name):
            for op in self.streams[name]:
                for key, v in op.waits:
                    eng.wait_ge(semobj(key), v)
                inst = op.fn(eng)
                if op.sem is not None:
                    inst.then_inc(semobj(op.sem), 16 if op.is_dma else 1)
            n = dcnt.get(name, 0)
            for k in range(min(n, DMA_K)):
                last = (n - 1 - k) // DMA_K * DMA_K + k
                tot = 16 * (last // DMA_K + 1)
                eng.wait_ge(dsem[name][k], tot)

        @block.sync
        def _(e):
            emit_stream(e, 'sp')

        @block.tensor
        def _(e):
            emit_stream(e, 'pe')

        @block.scalar
        def _(e):
            emit_stream(e, 'act')

        @block.vector
        def _(e):
            emit_stream(e, 'dve')

        @block.gpsimd
        def _(e):
            emit_stream(e, 'pool')


NPBF = ml_dtypes.bfloat16
D = 1024
DFF = 2816
NF = DFF // 128
EPS = 1e-6
Q0, KC, VC, KS, VS, KW, VW, GT, Z0, XB, DTO = 0, 512, 640, 768, 896, 1024, 1152, 1280, 1304, 1816, 2840
INW = 2848
NEGM = -30000.0
SLOPES = [2.0 ** (-(h + 1)) for h in range(8)]
NQS = 8
TQ = 256
NSUB = TQ // 128
NCM = 2048 // TQ + 1


class Cfg:
    def __init__(self, nseq_p=2, seq=4096, nseq_s=16, past=8192, n_phys=10240,
                 stages=('ffn1', 'mixp', 'mixs', 'ffn2'), dbg=False):
        self.nseq_p = nseq_p
        self.seq = seq
        self.nseq_s = nseq_s
        self.past = past
        self.n_phys = n_phys
        self.stages = stages
        self.dbg = dbg
        self.ntok_p = nseq_p * seq
        self.ntok_s = nseq_s * NQS
        self.ntok = self.ntok_p + self.ntok_s
        self.npages = past // 128
        self.nkt_s = self.npages + 1
        self.ncmp_s = past // 16 - 1
        self.nsel_s = past // 64 + 1


def k_rows(pos, with_e, cmp_n=None):
    n = len(pos) if pos is not None else len(cmp_n)
    r = np.zeros((64, n), np.float32)
    if cmp_n is None:
        if with_e:
            jj = (pos // 64) % 32
            r[jj, np.arange(n)] = 1.0
        r[32] = pos % 128
        r[33] = pos // 128
        r[34] = 1.0
        r[35] = 1.0
    else:
        r[34] = 1.0
        r[35] = 1.0
        r[36] = cmp_n % 128
        r[37] = cmp_n // 128
        r[38] = 1.0
    return r.astype(NPBF)


def q_rows(qpos, slope):
    n = len(qpos)
    r = np.zeros((32, n), np.float32)
    r[0] = slope
    r[1] = 128 * slope
    r[2] = -slope * 64 * (qpos // 64)
    r[3] = -slope * (qpos % 64)
    r[4] = 16 * slope
    r[5] = 2048 * slope
    r[6] = 31 * slope
    return r


def host_consts(cfg):
    c = {}
    S = cfg.seq
    c['ident'] = np.eye(128, dtype=np.float32)
    c['tri'] = np.triu(np.ones((128, 128), np.float32))
    c['ones'] = np.ones((128, 128), np.float32)
    qp = np.arange(S)
    c['alq'] = np.stack([q_rows(qp, SLOPES[h]) for h in range(8)], axis=1).astype(NPBF)
    c['kc_sel'] = k_rows(qp, True)
    c['kc_win'] = k_rows(qp, False)
    c['kc_cmp'] = k_rows(None, False, cmp_n=np.arange(256))
    ki = np.arange(128)[:, None]
    qi = np.arange(TQ)[None, :]
    c['caus'] = np.stack([np.where(128 * v + ki > qi, NEGM, 0.0) for v in range(NSUB)], axis=1).astype(NPBF)
    c['far'] = np.stack([np.where(qi - ki >= 512 - 128 * u, NEGM, 0.0) for u in range(1, 5)], axis=1).astype(NPBF)
    c['cmpm'] = np.stack([np.where(16 * ki + 31 > qi + TQ * r, NEGM, 0.0) for r in range(NCM)], axis=1).astype(NPBF)
    nb = S // 16 - 1
    nselp = S // 64
    ov = np.zeros((256, 64), np.float32)
    for i in range(min(nb, 256)):
        for j in (i // 4, (i + 1) // 4):
            if j < 64:
                ov[i, j] += 1.0
    c['ovl'] = ov.reshape(2, 128, 64).transpose(1, 0, 2).copy().astype(NPBF)
    j = np.arange(64)[None, :]
    q = qp[:, None]
    forced = (j == q // 64) | (j == 0)
    valid = (j * 64 <= q) & (j < nselp)
    VAL = np.where(forced, 0.0, np.where(valid, 1.0, 0.0))
    ADD = np.where(forced, 1e9, np.where(valid, 0.0, -1.0))
    c['va'] = np.concatenate([VAL, ADD], axis=1).astype(np.float32)
    PAST = cfg.past
    qs = PAST + np.arange(NQS)
    aq = np.zeros((32, 2, 4, NQS), np.float32)
    for g in range(2):
        for hh in range(4):
            aq[:, g, hh, :] = q_rows(qs, SLOPES[g * 4 + hh])
    c['alq_s'] = aq.reshape(32, 2, 32).astype(NPBF)
    nk = cfg.nkt_s * 128
    c['kc_sel_s'] = k_rows(np.arange(nk), True)
    c['kc_win_s'] = k_rows(PAST - 512 + np.arange(640), False)
    c['kc_cmp_s'] = k_rows(None, False, cmp_n=np.arange(((cfg.ncmp_s + 127) // 128) * 128))
    ki = np.arange(128)[:, None]
    qi8 = np.tile(np.arange(NQS), 4)[None, :]
    c['caus8'] = np.where((ki > qi8) | (ki >= NQS), NEGM, 0.0).astype(NPBF)
    c['far8'] = np.where(ki <= qi8, NEGM, 0.0).astype(NPBF)
    nlast = cfg.ncmp_s - ((cfg.ncmp_s + 127) // 128 - 1) * 128
    c['cmpm_s'] = np.where((ki >= nlast) & (qi8 >= 0), NEGM, 0.0).astype(NPBF)
    nch = (cfg.ncmp_s + 127) // 128
    nsl = cfg.nsel_s
    ovs = np.zeros((nch * 128, nsl), np.float32)
    for i in range(cfg.ncmp_s):
        for jx in (i // 4, (i + 1) // 4):
            if jx < nsl:
                ovs[i, jx] += 1.0
    c['ovl_s'] = ovs.reshape(nch, 128, nsl).transpose(1, 0, 2).copy().astype(NPBF)
    js = np.arange(nsl)[None, :]
    forced_s = (js == (qs[:, None] // 64)) | (js == 0)
    vas = np.zeros((NQS, 2, nsl), np.float32)
    vas[:, 0] = np.where(forced_s, 0.0, 1.0)
    vas[:, 1] = np.where(forced_s, 1e9, 0.0)
    c['va_s'] = vas
    return c


CONST_DT = {'alq': BF16, 'kc_sel': BF16, 'kc_win': BF16, 'kc_cmp': BF16, 'caus': BF16, 'far': BF16, 'cmpm': BF16,
            'ovl': BF16, 'alq_s': BF16, 'kc_sel_s': BF16, 'kc_win_s': BF16, 'kc_cmp_s': BF16, 'caus8': BF16,
            'far8': BF16, 'ovl_s': BF16, 'cmpm_s': BF16}


def build(cfg):
    nc = bass.Bass("TRN2", target_bir_lowering=False)
    P = Prog(nc)
    top = ExitStack()
    NT, NTP, NS = cfg.ntok, cfg.ntok_p, cfg.ntok_s
    S = cfg.seq
    PAST = cfg.past

    def din(name, shape, dt=F32):
        return nc.dram_tensor(name, list(shape), dt, kind="ExternalInput").ap()

    def dout(name, shape, dt=F32):
        return nc.dram_tensor(name, list(shape), dt, kind="ExternalOutput").ap()

    def dscr(name, shape, dt=F32):
        return nc.dram_tensor(name, list(shape), dt, kind="Internal").ap()

    x_in = din("x_in", [NT, D])
    hc = host_consts(cfg)
    C = {k: din("c_" + k, v.shape, CONST_DT.get(k, F32)) for k, v in hc.items()}
    w = {}
    for nm, shp in (("ffn1_wg", [D, DFF]), ("ffn1_wu", [D, DFF]), ("ffn1_wd", [DFF, D]),
                    ("ffn2_wg", [D, DFF]), ("ffn2_wu", [D, DFF]), ("ffn2_wd", [DFF, D]),
                    ("norm_ffn1", [D]), ("norm_ffn2", [D]), ("norm_final", [D]), ("norm_mix", [D]),
                    ("w_in", [D, INW]), ("w_out", [D, D]),
                    ("cmp_pe_k", [32, 64]), ("cmp_w1_k", [2048, 128]), ("cmp_w2_k", [128, 64]),
                    ("cmp_pe_v", [32, 64]), ("cmp_w1_v", [2048, 128]), ("cmp_w2_v", [128, 64]),
                    ("conv_w", [4, 1024]), ("conv_b", [1024]), ("dt_bias", [8]), ("a_log", [8]),
                    ("d_skip", [8]), ("ssm_norm", [512])):
        w[nm] = din(nm, shp)
    cache_cmp = din("cache_cmp", [cfg.n_phys * 128, 256])
    cache_sel = din("cache_sel", [cfg.n_phys * 128, 256])
    cache_win = din("cache_win", [max(cfg.nseq_s, 1), 512, 256])
    state_conv = din("state_conv", [max(cfg.nseq_s, 1) * 3, 1024])
    state_ssm = din("state_ssm", [max(cfg.nseq_s, 1), 512, 128])
    page_table = din("page_table", [max(cfg.nseq_s, 1), cfg.npages], I32)
    iota_p = din("iota_p", [128, cfg.npages], F32)

    y_out = dout("y_out", [NT, D])
    o_cmp_p = dout("o_cmp_p", [max(NTP, 1), 256])
    o_sel_p = dout("o_sel_p", [max(NTP, 1), 256])
    o_win_p = dout("o_win_p", [max(cfg.nseq_p, 1), 512, 256])
    o_conv_p = dout("o_conv_p", [max(cfg.nseq_p, 1), 3, 1024])
    o_ssm_p = dout("o_ssm_p", [max(cfg.nseq_p, 1), 512, 128])
    o_cmp_s = dout("o_cmp_s", [max(NS, 1), 256])
    o_sel_s = dout("o_sel_s", [max(NS, 1), 256])
    o_win_s = dout("o_win_s", [max(cfg.nseq_s, 1), 512, 256])
    o_conv_s = dout("o_conv_s", [max(cfg.nseq_s, 1), 3, 1024])
    o_ssm_s = dout("o_ssm_s", [max(cfg.nseq_s, 1), 512, 128])
    x1_d = dscr("x1_scr", [NT, D])
    x2_d = dscr("x2_scr", [NT, D])
    NSs = max(NS, 1)
    u_scr = dscr("u_scr", [NSs, 2336])
    q_scr = dscr("q_scr", [64, 8, NSs], BF16)
    ks_scr = dscr("ks_scr", [64, 2, NSs], BF16)
    kw_scr = dscr("kw_scr", [64, 2, NSs], BF16)
    xb_scr = dscr("xb_scr", [128, 8, NSs])
    hs_scr = dscr("hs_scr", [128, 8, max(cfg.nseq_s, 1) * 3])
    dbg = {}
    if cfg.dbg:
        dbg['att'] = dout("dbg_att", [NT, 512])
        dbg['ssm'] = dout("dbg_ssm", [NT, 512])
        dbg['x2'] = dout("dbg_x2", [NT, D])

    PSF = [top.enter_context(nc.psum_tensor("psf%d" % i, [128, 512], F32)) for i in range(7)]
    PSB = top.enter_context(nc.psum_tensor("psb", [128, 1024], BF16))

    uniq = [0]

    class Scope:
        def __init__(self):
            self.st = ExitStack()

        def sb(self, name, shape, dt=F32):
            uniq[0] += 1
            return self.st.enter_context(nc.sbuf_tensor("%s_u%d" % (name, uniq[0]), list(shape), dt))

        def close(self):
            P.fence()
            self.st.close()

    def rms_rstd(sc_stat, X, L, col, junk):
        st = sc_stat
        width = X.shape[-1]
        P.act(junk[0:L, :width], X, AF.Square, accum_out=st[0:L, col:col + 1])
        P.ts(st[0:L, col + 1:col + 2], st[0:L, col:col + 1], 1.0 / width, EPS, op0=ALU.mult, op1=ALU.add)
        P.act(st[0:L, col + 2:col + 3], st[0:L, col + 1:col + 2], AF.Sqrt)
        o, i = st[0:L, col + 3:col + 4], st[0:L, col + 2:col + 3]
        P.add('dve', lambda e, o=o, i=i: e.reciprocal(o, i), [i], [o])
        return o

    def ffn_phase(pfx, src_d, dst_d, final):
        P.tag = pfx
        sc = Scope()
        identf = sc.sb("identf", [128, 128], F32)
        identb = sc.sb("identb", [128, 128], BF16)
        P.dma(identf[:], C['ident'])
        P.copy(identb[:], identf[:])
        WG = sc.sb("WG", [128, 8, DFF], BF16)
        WU = sc.sb("WU", [128, 8, DFF], BF16)
        WD = sc.sb("WD", [128, NF, D], BF16)
        stg = [sc.sb("stg%d" % i, [128, 1408], F32) for i in range(2)]
        xin = [sc.sb("xin%d" % i, [128, D], F32) for i in range(2)]
        xres = [sc.sb("xres%d" % i, [128, D], F32) for i in range(2)]
        hb = sc.sb("hb", [128, D], BF16)
        hT = sc.sb("hT", [128, 8, 512], BF16)
        AT = sc.sb("AT", [128, NF, 512], BF16)
        sil = [sc.sb("sil%d" % i, [128, 512], BF16) for i in range(2)]
        nwT = sc.sb("nwT", [128, 8], F32)
        nfin = sc.sb("nfin", [128, D], F32)
        stat = sc.sb("stat", [128, 32], F32)
        junk = sc.sb("junk", [128, D], BF16)
        psT = PSB
        psG, psU, psO = PSF[0:2], PSF[2:4], PSF[4:6]
        cnt = [0]

        def load_w(dst3, src2, nchunk, width):
            for c in range(nchunk):
                for c0 in range(0, width, 1408):
                    cw = min(1408, width - c0)
                    s = stg[cnt[0] % 2]
                    eng = ('act', 'dve', 'pool')[cnt[0] % 3]
                    cnt[0] += 1
                    P.dma(s[:, :cw], src2[c * 128:(c + 1) * 128, c0:c0 + cw])
                    P.copy(dst3[:, c, c0:c0 + cw], s[:, :cw], eng=eng)

        load_w(WG, w[pfx + "_wg"], 8, DFF)
        load_w(WU, w[pfx + "_wu"], 8, DFF)
        load_w(WD, w[pfx + "_wd"], NF, D)
        nrm = w["norm_ffn1" if pfx == "ffn1" else "norm_ffn2"]
        P.add('sp', lambda e: e.dma_start(out=nwT[:], in_=nrm.rearrange("(c p) -> p c", p=128),
                                          allow_slow_non_contiguous=True), [nrm], [nwT[:]], is_dma=True)
        if final:
            P.dma(nfin[:], w["norm_final"].partition_broadcast(128))
        tiles = []
        t0 = 0
        while t0 < NT:
            n = min(512, NT - t0)
            tiles.append((t0, n))
            t0 += n
        scn = [0]

        def subs_of(ti):
            t0, n = tiles[ti]
            return [(s0, min(128, n - s0)) for s0 in range(0, n, 128)]

        def prologue_sub(ti, si):
            t0, n = tiles[ti]
            s0, L = subs_of(ti)[si]
            X = xin[scn[0] % 2]
            scn[0] += 1
            P.dma(X[0:L, :], src_d[t0 + s0:t0 + s0 + L, :])
            r = rms_rstd(stat, X[0:L, :], L, 4 * si, junk)
            P.ts(hb[0:L, :], X[0:L, :], r, None, op0=ALU.mult)
            for k in range(8):
                P.tr(psT[:, k * 128:k * 128 + L], hb[0:L, k * 128:(k + 1) * 128], identb[0:L, 0:L])
            for k in range(8):
                if k % 2 == 0:
                    P.act(hT[:, k, s0:s0 + L], psT[:, k * 128:k * 128 + L], AF.Copy, scale=nwT[:, k:k + 1])
                else:
                    P.ts(hT[:, k, s0:s0 + L], psT[:, k * 128:k * 128 + L], nwT[:, k:k + 1], None, op0=ALU.mult)

        for si in range(len(subs_of(0))):
            prologue_sub(0, si)
        rc = 0
        for ti, (t0, n) in enumerate(tiles):
            subs = subs_of(ti)
            for f in range(NF):
                g = psG[f % 2]
                u = psU[f % 2]
                for k in range(8):
                    P.mm(g[:, :n], WG[:, k, f * 128:(f + 1) * 128], hT[:, k, :n], start=(k == 0), stop=(k == 7))
                for k in range(8):
                    P.mm(u[:, :n], WU[:, k, f * 128:(f + 1) * 128], hT[:, k, :n], start=(k == 0), stop=(k == 7))
                sl = sil[f % 2]
                P.act(sl[:, :n], g[:, :n], AF.Silu)
                P.tt(AT[:, f, :n], sl[:, :n], u[:, :n], ALU.mult)
            nxt = subs_of(ti + 1) if ti + 1 < len(tiles) else []
            for si, (s0, L) in enumerate(subs):
                X = xres[rc % 2]
                rc += 1
                P.dma(X[0:L, :], src_d[t0 + s0:t0 + s0 + L, :])
                for h2 in range(2):
                    o = psO[(si * 2 + h2) % 2]
                    for f in range(NF):
                        P.mm(o[0:L, :], AT[:, f, s0:s0 + L], WD[:, f, h2 * 512:(h2 + 1) * 512],
                             start=(f == 0), stop=(f == NF - 1))
                    P.stt(X[0:L, h2 * 512:(h2 + 1) * 512], o[0:L, :], 0.5, X[0:L, h2 * 512:(h2 + 1) * 512],
                          ALU.mult, ALU.add)
                if final:
                    r = rms_rstd(stat, X[0:L, :], L, 16 + 4 * si, junk)
                    P.stt(X[0:L, :], X[0:L, :], r, nfin[0:L, :], ALU.mult, ALU.mult)
                P.dma(dst_d[t0 + s0:t0 + s0 + L, :], X[0:L, :])
                if si < len(nxt):
                    prologue_sub(ti + 1, si)
            for si in range(len(subs), len(nxt)):
                prologue_sub(ti + 1, si)
        sc.close()

    def mixer_phase():
        P.tag = 'mix_init'
        mc = Scope()
        identf = mc.sb("identf", [128, 128], F32)
        identb = mc.sb("identb", [128, 128], BF16)
        tri = mc.sb("tri", [128, 128], F32)
        onesf = mc.sb("onesf", [128, 128], F32)
        P.dma(identf[:], C['ident'])
        P.copy(identb[:], identf[:])
        P.dma(tri[:], C['tri'])
        P.dma(onesf[:], C['ones'])
        Wout = mc.sb("Wout", [128, 8, D], BF16)
        W1 = [mc.sb("W1_%d" % i, [128, 32, 128], BF16) for i in range(2)]
        W2 = [mc.sb("W2_%d" % i, [128, 64], BF16) for i in range(2)]
        peT = [mc.sb("peT_%d" % i, [128, 32], BF16) for i in range(2)]
        b1 = [mc.sb("b1_%d" % i, [128, 1], F32) for i in range(2)]
        nwT = mc.sb("nwT", [128, 8], F32)
        cw = mc.sb("cw", [128, 8, 4], F32)
        cb = mc.sb("cb", [128, 8], F32)
        dtb = mc.sb("dtb", [128, 8], F32)
        Aneg = mc.sb("Aneg", [128, 8], F32)
        dsk = mc.sb("dsk", [128, 8], F32)
        snw = mc.sb("snw", [128, 4], F32)
        stat = mc.sb("stat", [128, 32], F32)
        junk = mc.sb("junk", [128, D], BF16)
        wsc = Scope()
        Win = wsc.sb("Win", [128, 8, INW], BF16)
        tmpsc = Scope()
        stg = [tmpsc.sb("mstg%d" % i, [128, 1424], F32) for i in range(2)]
        cnt = [0]

        def slow_dma(out, in_):
            P.add('sp', lambda e: e.dma_start(out=out, in_=in_, allow_slow_non_contiguous=True), [in_], [out],
                  is_dma=True)

        def load_w(dst3, src2, nchunk, width, piece=1424):
            for c in range(nchunk):
                for c0 in range(0, width, piece):
                    cwid = min(piece, width - c0)
                    s = stg[cnt[0] % 2]
                    eng = ('act', 'dve', 'pool')[cnt[0] % 3]
                    cnt[0] += 1
                    P.dma(s[:, :cwid], src2[c * 128:(c + 1) * 128, c0:c0 + cwid])
                    P.copy(dst3[:, c, c0:c0 + cwid], s[:, :cwid], eng=eng)

        load_w(Win, w['w_in'], 8, INW)
        load_w(Wout, w['w_out'], 8, D)
        for i, kv in enumerate(('k', 'v')):
            w1 = w['cmp_w1_' + kv].rearrange("(r d) h -> d r h", d=64)
            for half in range(2):
                for r0 in range(0, 32, 8):
                    s = stg[cnt[0] % 2]
                    cnt[0] += 1
                    sv = s[half * 64:(half + 1) * 64, 0:1024].rearrange("p (r h) -> p r h", h=128)
                    P.dma(sv, w1[:, r0:r0 + 8, :])
                    P.copy(W1[i][half * 64:(half + 1) * 64, r0:r0 + 8, :], sv)
            s = stg[cnt[0] % 2]
            cnt[0] += 1
            P.dma(s[:, 0:64], w['cmp_w2_' + kv])
            P.copy(W2[i][:], s[:, 0:64])
            s = stg[cnt[0] % 2]
            cnt[0] += 1
            for half in range(2):
                slow_dma(s[half * 64:(half + 1) * 64, 0:32], w['cmp_pe_' + kv].rearrange("r d -> d r"))
            P.copy(peT[i][:], s[:, 0:32])
            ps = PSF[0]
            for r in range(32):
                P.mm(ps[:, 0:1], W1[i][0:64, r, :], peT[i][0:64, r:r + 1], start=(r == 0), stop=(r == 31))
            P.copy(b1[i][:], ps[:, 0:1])
        slow_dma(nwT[:], w['norm_mix'].rearrange("(c p) -> p c", p=128))
        for tap in range(4):
            slow_dma(cw[:, :, tap], w['conv_w'][tap, :].rearrange("(c p) -> p c", p=128))
        slow_dma(cb[:], w['conv_b'].rearrange("(c p) -> p c", p=128))
        slow_dma(snw[:], w['ssm_norm'].rearrange("(c p) -> p c", p=128))
        P.dma(dtb[:], w['dt_bias'].partition_broadcast(128))
        P.dma(dsk[:], w['d_skip'].partition_broadcast(128))
        P.dma(Aneg[:], w['a_log'].partition_broadcast(128))
        P.act(Aneg[:], Aneg[:], AF.Exp)
        P.ts(Aneg[:], Aneg[:], -1.0, None, op0=ALU.mult)

        tmpsc.close()
        K = dict(identf=identf, identb=identb, tri=tri, onesf=onesf, Win=Win, Wout=Wout, W1=W1, W2=W2, b1=b1,
                 nwT=nwT, cw=cw, cb=cb, dtb=dtb, Aneg=Aneg, dsk=dsk, snw=snw, stat=stat, junk=junk)
        if 'mixp' in cfg.stages:
            for sq in range(cfg.nseq_p):
                mixer_prompt(K, sq)
        if 'mixs' in cfg.stages and cfg.nseq_s > 0:
            sample_group(K)
        wsc.close()
        if 'mixs' in cfg.stages and cfg.nseq_s > 0:
            mixer_sample(K)
        mc.close()

    def ssd_chunk(K, T, L, BT, CT, xtm, Btm, dtv, av, zs, hTf, hTb, ymix_out):
        tri, onesf = K['tri'], K['onesf']
        psC, psCB, psE, psY, psYo, psSt = PSF[0], PSF[1], PSF[2:4], PSF[4], PSF[5], PSF[6]
        acsc, expacs, dend, cd = T['acsc'], T['expacs'], T['dend'], T['cd']
        aTri, CBm, segc, Eb, MTb, xdt, xdtd, Ys, Yt, tmp2 = (T[k] for k in
                                                             ('aTri', 'CBm', 'segc', 'Eb', 'MTb', 'xdt', 'xdtd', 'Ys',
                                                              'Yt', 'tmp2'))
        P.mm(psC[0:L, 0:8], tri[0:L, 0:L], av, start=True, stop=True)
        P.copy(acsc[0:L, :], psC[0:L, 0:8])
        P.act(expacs[0:L, :], acsc[0:L, :], AF.Exp)
        P.tt(aTri[0:L, :, 0:L], tri[0:L, 0:L].unsqueeze(1).broadcast_to([L, 8, L]),
             av.unsqueeze(2).broadcast_to([L, 8, L]), ALU.mult)
        for g in range(2):
            P.mm(psCB[0:L, g * 128:g * 128 + L], BT[g], CT[g], start=True, stop=True)
            P.tt(CBm[0:L, g, 0:L], psCB[0:L, g * 128:g * 128 + L], tri[0:L, 0:L], ALU.mult)
        P.tt(xdt[0:L, :].rearrange("p (h d) -> p h d", h=8), xtm.rearrange("p (h d) -> p h d", h=8),
             dtv.unsqueeze(2).broadcast_to([L, 8, 64]), ALU.mult)
        for hq in range(2):
            g = hq
            pe = psE[hq]
            pe3 = pe[0:L, 0:4 * L].rearrange("p (h l) -> p h l", h=4)
            P.mm(pe3, onesf[0:L, 0:L], aTri[0:L, 4 * hq:4 * hq + 4, 0:L], start=True, stop=True)
            E = Eb[hq]
            E3 = E[0:L, 0:4 * L].rearrange("p (h l) -> p h l", h=4)
            P.tt(E3, pe3, acsc[0:L, 4 * hq:4 * hq + 4].unsqueeze(2).broadcast_to([L, 4, L]), ALU.subtract)
            P.act(E3, E3, AF.Relu, scale=-1.0)
            P.act(E3, E3, AF.Exp, scale=-1.0)
            P.copy(dend[0:L, 4 * hq:4 * hq + 4], E3[:, :, L - 1], eng='pool')
            M = MTb[hq]
            M3 = M[0:L, 0:4 * L].rearrange("p (h l) -> p h l", h=4)
            P.tt(M3, E3, CBm[0:L, g, 0:L].unsqueeze(1).broadcast_to([L, 4, L]), ALU.mult)
            for hh in range(4):
                h = 4 * hq + hh
                P.mm(psY[0:L, h * 64:(h + 1) * 64], M3[:, hh, :], xdt[0:L, h * 64:(h + 1) * 64],
                     start=(h == 0), stop=True, skip_group_check=True)
        for g in range(2):
            P.mm(psYo[0:L, g * 256:(g + 1) * 256], CT[g], hTb[:, g * 256:(g + 1) * 256],
                 start=(g == 0), stop=True, skip_group_check=True)
        P.tt(Ys[0:L, :].rearrange("p (h d) -> p h d", h=8), psYo[0:L, :].rearrange("p (h d) -> p h d", h=8),
             expacs[0:L, :].unsqueeze(2).broadcast_to([L, 8, 64]), ALU.mult)
        P.tt(Yt[0:L, :], psY[0:L, :], Ys[0:L, :], ALU.add)
        P.tt(tmp2[0:L, :].rearrange("p (h d) -> p h d", h=8), xtm.rearrange("p (h d) -> p h d", h=8),
             K['dsk'][0:L, :].unsqueeze(2).broadcast_to([L, 8, 64]), ALU.mult)
        P.tt(Yt[0:L, :], Yt[0:L, :], tmp2[0:L, :], ALU.add)
        P.tt(Yt[0:L, :], Yt[0:L, :], zs, ALU.mult)
        st = K['stat']
        for g in range(2):
            r = rms_rstd(st, Yt[0:L, g * 256:(g + 1) * 256], L, 16 + 4 * g, K['junk'])
            P.ts(ymix_out[:, g * 256:(g + 1) * 256], Yt[0:L, g * 256:(g + 1) * 256], r, None, op0=ALU.mult)
        P.tt(xdtd[0:L, :].rearrange("p (h d) -> p h d", h=8), xdt[0:L, :].rearrange("p (h d) -> p h d", h=8),
             dend[0:L, :].unsqueeze(2).broadcast_to([L, 8, 64]), ALU.mult)
        for g in range(2):
            P.mm(psSt[:, g * 256:(g + 1) * 256], Btm[g], xdtd[0:L, g * 256:(g + 1) * 256],
                 start=(g == 0), stop=True, skip_group_check=True)
        P.mm(psC[:, 8:16], onesf[0:L, :], av, start=True, stop=True)
        P.act(T['cdall'][:, :], psC[:, 8:16], AF.Exp)
        P.tt(hTf[:, :].rearrange("p (h d) -> p h d", h=8), hTf[:, :].rearrange("p (h d) -> p h d", h=8),
             T['cdall'][:, :].unsqueeze(2).broadcast_to([128, 8, 64]), ALU.mult)
        P.tt(hTf[:, :], hTf[:, :], psSt[:, :], ALU.add)
        P.copy(hTb[:, :], hTf[:, :], eng='pool')

    def ssd_tmp(sc):
        T = {}
        for nm, shp, dt in (('acsc', [128, 8], F32), ('expacs', [128, 8], F32), ('dend', [128, 8], F32),
                            ('cd', [128, 8], F32), ('cdall', [128, 8], F32), ('aTri', [128, 8, 128], F32),
                            ('CBm', [128, 2, 128], F32), ('xdt', [128, 512], BF16), ('xdtd', [128, 512], BF16),
                            ('Ys', [128, 512], F32), ('Yt', [128, 512], F32), ('tmp2', [128, 512], F32)):
            T[nm] = sc.sb("ssd_" + nm, shp, dt)
        T['segc'] = None
        T['Eb'] = [sc.sb("ssd_Eb%d" % i, [128, 512], F32) for i in range(2)]
        T['MTb'] = [sc.sb("ssd_MT%d" % i, [128, 512], BF16) for i in range(2)]
        return T

    def mixer_prompt(K, sq):
        sc = Scope()
        identf, identb, Win, Wout = K['identf'], K['identb'], K['Win'], K['Wout']
        stat, junk = K['stat'], K['junk']
        base = sq * S
        ntile = S // TQ
        far_nz = [bool(np.any(hc['far'][:, u, :].astype(np.float32) != 0)) for u in range(4)]
        nsel_t = S // 128
        caus = sc.sb("caus", [128, NSUB, TQ], BF16)
        far = sc.sb("far", [128, 4, TQ], BF16)
        cmpm = sc.sb("cmpm", [128, NCM, TQ], BF16)
        ovl = sc.sb("ovl", [128, 2, 64], BF16)
        P.dma(caus[:], C['caus'])
        P.dma(far[:], C['far'])
        P.dma(cmpm[:], C['cmpm'])
        P.dma(ovl[:], C['ovl'])
        KTs = [sc.sb("KTs%d" % g, [128, S], BF16) for g in range(2)]
        KTw = sc.sb("KTw", [128, 2, 1024], BF16)
        kcT = sc.sb("kcT", [128, 2, 256], BF16)
        Vs = sc.sb("Vs", [128, nsel_t, 2, 65], BF16)
        Vw = sc.sb("Vw", [128, 8, 2, 65], BF16)
        vca = sc.sb("vca", [128, 2, 2, 65], BF16)
        Hh = [sc.sb("Hh%d" % i, [128, 2, 256], BF16) for i in range(2)]
        raw = [sc.sb("raw%d" % i, [128, 16 + TQ], BF16) for i in range(2)]
        QTa = sc.sb("QTa", [128, 2, 8, TQ], BF16)
        hT = sc.sb("hT", [128, 8, TQ], BF16)
        hb = sc.sb("hb", [128, D], BF16)
        xin = sc.sb("xio", [128, D], F32)
        xres = xin
        kvst = [sc.sb("kvst", [128, 768], F32)] * 2
        gate = sc.sb("gate", [128, NSUB, 24], F32)
        zs = sc.sb("zs", [128, NSUB, 512], BF16)
        dtv = sc.sb("dtv", [128, NSUB, 8], F32)
        av = sc.sb("av", [128, NSUB, 8], F32)
        att = sc.sb("att", [128, NSUB, 512], F32)
        attb = sc.sb("attb", [128, 512], BF16)
        ymix = sc.sb("ymix", [128, NSUB, 512], BF16)
        PT = [sc.sb("PT%d" % i, [128, 2 * TQ], BF16) for i in range(3)]
        rz = sc.sb("rz", [128, 8], F32)
        coef = sc.sb("coef", [128, 8], F32)
        imp = sc.sb("imp", [128, NSUB, 2, 64], F32)
        impf = sc.sb("impf", [128, 64], F32)
        impw = sc.sb("impw", [128, 64], F32)
        m8 = sc.sb("m8", [128, 16], F32)
        msk = sc.sb("msk", [128, 64], F32)
        SELB = sc.sb("SELB", [128, NSUB, 2, 2, 96], F32)
        va = sc.sb("va", [128, NSUB, 128], F32)
        xp = [sc.sb("xp%d" % i, [128, 3 + TQ], F32) for i in range(2)]
        accf = sc.sb("accf", [128, TQ], F32)
        hist = sc.sb("hist", [128, 8, 3], F32)
        convo = sc.sb("convo", [128, 8, TQ], BF16)
        xtm1 = sc.sb("xtm", [128, 512], BF16)
        Btm1 = sc.sb("Btm", [128, 2, 128], BF16)
        hTf = sc.sb("hTf", [128, 512], F32)
        hTb = sc.sb("hTb", [128, 512], BF16)
        ostg = xin
        T = ssd_tmp(sc)
        psM = PSF[0:2]
        psS = PSF[2:4]
        psO = PSF[4:6]
        psI = PSF[6]
        psT = PSB
        mcnt = [0]

        def nextM():
            mcnt[0] += 1
            return psM[mcnt[0] % 2]

        for g in range(2):
            P.dma(KTs[g][64:128, :], C['kc_sel'])
            P.memset(KTs[g][0:64, :], 0.0, eng='pool')
        P.memset(KTw[:], 0.0, eng='pool')
        P.memset(kcT[0:64, :, :], 0.0)
        for g in range(2):
            P.dma(kcT[64:128, g, :], C['kc_cmp'])
        P.memset(Vs[:], 0.0, eng='pool')
        P.memset(Vs[:, :, :, 64:65], 1.0, eng='pool')
        P.memset(Vw[:], 0.0, eng='pool')
        P.memset(Vw[:, :, :, 64:65], 1.0, eng='pool')
        P.memset(vca[:], 0.0, eng='pool')
        P.memset(vca[:, :, :, 64:65], 1.0, eng='pool')
        for i in range(2):
            P.memset(Hh[i][:], 0.0)
            P.memset(raw[i][:], 0.0)
        P.memset(QTa[:], 0.0, eng='pool')
        P.memset(SELB[:], 0.0)
        P.memset(hist[:], 0.0)
        P.memset(hTf[:], 0.0)
        P.memset(hTb[:], 0.0)

        for ti in range(ntile):
            q0 = ti * TQ
            useG1 = q0 >= 2048
            P.tag = 'pA'
            for s in range(NSUB):
                r0 = base + q0 + s * 128
                P.dma(xin[:], x1_d[r0:r0 + 128, :])
                r = rms_rstd(stat, xin[:], 128, 4 * s, junk)
                P.ts(hb[:], xin[:], r, None, op0=ALU.mult)
                for k in range(8):
                    P.tr(psT[:, k * 128:(k + 1) * 128], hb[:, k * 128:(k + 1) * 128], identb[:])
                for k in range(8):
                    if k % 2 == 0:
                        P.act(hT[:, k, s * 128:(s + 1) * 128], psT[:, k * 128:(k + 1) * 128], AF.Copy,
                              scale=K['nwT'][:, k:k + 1])
                    else:
                        P.ts(hT[:, k, s * 128:(s + 1) * 128], psT[:, k * 128:(k + 1) * 128], K['nwT'][:, k:k + 1],
                             None, op0=ALU.mult)

            def proj_fm(col0, ncol):
                ps = nextM()
                for k in range(8):
                    P.mm(ps[0:ncol, 0:TQ], Win[:, k, col0:col0 + ncol], hT[:, k, :], start=(k == 0), stop=(k == 7))
                return ps

            P.tag = 'pB'
            for h in range(8):
                ps = proj_fm(Q0 + h * 64, 64)
                P.act(QTa[0:64, 0, h, :], ps[0:64, 0:TQ], AF.Copy, scale=0.125)
                if useG1:
                    P.ts(QTa[0:64, 1, h, :], ps[0:64, 0:TQ], 0.125, None, op0=ALU.mult)
            wc = q0 % 1024
            for g in range(2):
                ps = proj_fm(KS + g * 64, 64)
                P.copy(KTs[g][0:64, q0:q0 + TQ], ps[0:64, 0:TQ], eng='act')
                ps = proj_fm(KW + g * 64, 64)
                P.copy(KTw[0:64, g, wc:wc + TQ], ps[0:64, 0:TQ])
                P.dma(KTw[64:128, g, wc:wc + TQ], C['kc_win'][:, q0:q0 + TQ])
            for i, c0 in enumerate((KC, VC)):
                ps = proj_fm(c0, 128)
                P.copy(raw[i][:, 16:16 + TQ], ps[:, 0:TQ], eng=('act' if i == 0 else 'dve'))
            for c in range(8):
                ps = proj_fm(XB + c * 128, 128)
                X = xp[c % 2]
                P.copy(X[:, 0:3], hist[:, c, :], eng='pool')
                P.copy(X[:, 3:3 + TQ], ps[:, 0:TQ], eng='act')
                P.ts(accf[:], X[:, 0:TQ], K['cw'][:, c, 0:1], K['cb'][:, c:c + 1], op0=ALU.mult, op1=ALU.add)
                for tap in range(1, 4):
                    P.stt(accf[:], X[:, tap:tap + TQ], K['cw'][:, c, tap:tap + 1], accf[:], ALU.mult, ALU.add)
                P.act(convo[:, c, :], accf[:], AF.Silu)
                P.copy(hist[:, c, :], X[:, TQ:TQ + 3], eng='pool')
            P.tag = 'pC'
            for s in range(NSUB):
                kv = kvst[s % 2]
                tok0 = base + q0 + s * 128
                ps = nextM()
                for k in range(8):
                    P.mm(ps[:, :], hT[:, k, s * 128:(s + 1) * 128], Win[:, k, 512:1024], start=(k == 0), stop=(k == 7))
                P.copy(kv[:, 0:512], ps[:, :], eng='act')
                ps = nextM()
                for k in range(8):
                    P.mm(ps[:, 0:280], hT[:, k, s * 128:(s + 1) * 128], Win[:, k, 1024:1304], start=(k == 0),
                         stop=(k == 7))
                P.copy(kv[:, 512:768], ps[:, 0:256])
                P.act(gate[:, s, :], ps[:, 256:280], AF.Sigmoid)
                P.dma(o_cmp_p[tok0:tok0 + 128, :], kv[:, 0:256])
                P.dma(o_sel_p[tok0:tok0 + 128, :], kv[:, 256:512])
                tl = q0 + s * 128
                if tl >= S - 512:
                    P.dma(o_win_p[sq, tl - (S - 512):tl - (S - 512) + 128, :], kv[:, 512:768])
                P.copy(Vs[:, tl // 128, :, 0:64], kv[:, 384:512].rearrange("p (g d) -> p g d", g=2), eng='pool')
                P.copy(Vw[:, (tl // 128) % 8, :, 0:64], kv[:, 640:768].rearrange("p (g d) -> p g d", g=2), eng='pool')
                ps = nextM()
                for k in range(8):
                    P.mm(ps[:, :], hT[:, k, s * 128:(s + 1) * 128], Win[:, k, Z0:Z0 + 512], start=(k == 0), stop=(k == 7))
                P.act(zs[:, s, :], ps[:, :], AF.Silu)
                ps = nextM()
                for k in range(8):
                    P.mm(ps[:, 0:8], hT[:, k, s * 128:(s + 1) * 128], Win[:, k, DTO:DTO + 8], start=(k == 0),
                         stop=(k == 7))
                P.tt(dtv[:, s, :], ps[:, 0:8], K['dtb'][:], ALU.add)
                P.act(dtv[:, s, :], dtv[:, s, :], AF.Exp)
                P.act(dtv[:, s, :], dtv[:, s, :], AF.Ln, bias=1.0)
                P.tt(av[:, s, :], dtv[:, s, :], K['Aneg'][:], ALU.mult)
            P.tag = 'pD'
            jlo = max(0, q0 // 16 - 1)
            jhi = (q0 + TQ) // 16 - 1
            nb = jhi - jlo
            cst = 16 * jlo - q0 + 16
            for i in range(2):
                for g in range(2):
                    ps = nextM()
                    for r in range(32):
                        P.mm(ps[:, 0:nb], K['W1'][i][g * 64:(g + 1) * 64, r, :],
                             raw[i][g * 64:(g + 1) * 64, cst + r:cst + r + 16 * (nb - 1) + 1:16],
                             start=(r == 0), stop=(r == 31))
                    P.act(Hh[i][:, g, jlo:jhi], ps[:, 0:nb], AF.Silu, bias=K['b1'][i][:, 0:1])
                P.copy(raw[i][:, 0:16], raw[i][:, TQ:TQ + 16], eng='pool')
            for g in range(2):
                ps = nextM()
                P.mm(ps[0:64, 0:nb], K['W2'][0][:, :], Hh[0][:, g, jlo:jhi], start=True, stop=True)
                P.copy(kcT[0:64, g, jlo:jhi], ps[0:64, 0:nb])
                for c in range(jlo // 128, (jhi - 1) // 128 + 1):
                    ps = nextM()
                    P.mm(ps[:, 0:64], Hh[1][:, g, c * 128:(c + 1) * 128], K['W2'][1][:, :], start=True, stop=True)
                    P.copy(vca[:, c, g, 0:64], ps[:, 0:64], eng='act')
            P.tag = 'pE'
            P.dma(QTa[96:128, 0, :, :], C['alq'][:, :, q0:q0 + TQ])
            if useG1:
                P.dma(QTa[96:128, 1, :, :], C['alq'][:, :, q0:q0 + TQ])
            P.dma(va[:], C['va'][q0:q0 + TQ, :].rearrange("(s p) c -> p s c", p=128))

            def attend(hp, branch, tiles, first_branch):
                po = psO[hp % 2]
                n_t = len(tiles)
                for idx in range(n_t + 1):
                    if idx < n_t:
                        kt, G, mk, vv, ov = tiles[idx]
                        sb_ = psS[idx % 2]
                        P.mm(sb_[:, 0:2 * TQ], kt, QTa[:, G, 2 * hp:2 * hp + 2, :], start=True, stop=(mk is None))
                        if mk is not None:
                            P.mm(sb_[:, 0:2 * TQ], identb[:], mk.unsqueeze(1).broadcast_to([128, 2, TQ]),
                                 start=False, stop=True)
                        P.act(PT[idx % 3][:], sb_[:, 0:2 * TQ], AF.Exp)
                    if idx >= 1:
                        j = idx - 1
                        kt, G, mk, vv, ov = tiles[j]
                        pt = PT[j % 3]
                        for hl in range(2):
                            for s in range(NSUB):
                                P.mm(po[:, hl * 65 * NSUB + s * 65:hl * 65 * NSUB + (s + 1) * 65],
                                     pt[:, hl * TQ + s * 128:hl * TQ + (s + 1) * 128], vv,
                                     start=(j == 0 and hl == 0 and s == 0), stop=(j == n_t - 1), skip_group_check=True)
                        if ov is not None:
                            for hl in range(2):
                                for s in range(NSUB):
                                    P.mm(psI[:, hl * 64 * NSUB + s * 64:hl * 64 * NSUB + (s + 1) * 64],
                                         pt[:, hl * TQ + s * 128:hl * TQ + (s + 1) * 128], ov,
                                         start=(j == 0 and hl == 0 and s == 0), stop=(j == n_t - 1),
                                         skip_group_check=True)
                for hl in range(2):
                    h = 2 * hp + hl
                    o0 = hl * 65 * NSUB
                    zc = po[:, o0 + 64:o0 + 64 + 65 * (NSUB - 1) + 1:65]
                    rzh = rz[:, hl * NSUB:(hl + 1) * NSUB]
                    cfh = coef[:, hl * NSUB:(hl + 1) * NSUB]
                    P.ts(rzh, zc, 1e-30, None, op0=ALU.max)
                    P.add('dve', lambda e, rzh=rzh: e.reciprocal(rzh, rzh), [rzh], [rzh])
                    P.tt(cfh, rzh, gate[:, :, branch * 8 + h], ALU.mult)
                    for s in range(NSUB):
                        dst = att[:, s, h * 64:(h + 1) * 64]
                        src = po[:, o0 + s * 65:o0 + s * 65 + 64]
                        if first_branch:
                            P.ts(dst, src, coef[:, hl * NSUB + s:hl * NSUB + s + 1], None, op0=ALU.mult)
                        else:
                            P.stt(dst, src, coef[:, hl * NSUB + s:hl * NSUB + s + 1], dst, ALU.mult, ALU.add)

            P.tag = 'pF'
            for hp in range(4):
                g = hp // 2
                tiles = []
                for c in range(2):
                    if c == 1 and q0 < 2048:
                        continue
                    rel = (q0 - 2048 * c) // TQ
                    mk = cmpm[:, rel, :] if rel < NCM else None
                    tiles.append((kcT[:, g, c * 128:(c + 1) * 128], 0, mk, vca[:, c, g, :], ovl[:, c, :]))
                attend(hp, 0, tiles, True)
                for hl in range(2):
                    for s in range(NSUB):
                        dst = imp[:, s, g, :]
                        src = psI[:, hl * 64 * NSUB + s * 64:hl * 64 * NSUB + (s + 1) * 64]
                        rzc = rz[:, hl * NSUB + s:hl * NSUB + s + 1]
                        if hp % 2 == 0 and hl == 0:
                            P.ts(dst, src, rzc, None, op0=ALU.mult)
                        else:
                            P.stt(dst, src, rzc, dst, ALU.mult, ALU.add)
            P.tag = 'pG1'
            for s in range(NSUB):
                for g in range(2):
                    P.tt(impw[:], imp[:, s, g, :], va[:, s, 0:64], ALU.mult)
                    P.tt(impf[:], impw[:], va[:, s, 64:128], ALU.add)
                    P.add('dve', lambda e: e.max(m8[:, 0:8], impf[:]), [impf[:]], [m8[:, 0:8]])
                    P.add('dve', lambda e: e.match_replace(impw[:], m8[:, 0:8], impf[:], -1e30),
                          [m8[:, 0:8], impf[:]], [impw[:]])
                    P.add('dve', lambda e: e.max(m8[:, 8:16], impw[:]), [impw[:]], [m8[:, 8:16]])
                    P.ts(msk[:], impf[:], m8[:, 15:16], None, op0=ALU.is_ge)
                    P.ts(SELB[:, s, g, :, 64:96], msk[:].rearrange("p (a b) -> p a b", a=2), -1.0, -NEGM,
                         op0=ALU.add, op1=ALU.mult)
            P.tag = 'pI'
            for hp in range(4):
                g = hp // 2
                tiles = []
                for i in range(4 + NSUB):
                    k0 = q0 - 512 + 128 * i
                    if k0 < 0:
                        continue
                    mk = far[:, (4 - i) - 1, :] if i < 4 else caus[:, i - 4, :]
                    if i < 4 and not far_nz[(4 - i) - 1]:
                        mk = None
                    col = k0 % 1024
                    tiles.append((KTw[:, g, col:col + 128], 0, mk, Vw[:, (k0 // 128) % 8, g, :], None))
                attend(hp, 2, tiles, False)
            P.tag = 'pG2'
            for s in range(NSUB):
                for g in range(2):
                    for G in range(2 if useG1 else 1):
                        pst = nextM()
                        P.tr(pst[0:96, 0:128], SELB[:, s, g, G, :], identf[:])
                        P.copy(QTa[64:96, G, g * 4:(g + 1) * 4, s * 128:(s + 1) * 128],
                               pst[64:96, 0:128].unsqueeze(1).broadcast_to([32, 4, 128]))
            P.tag = 'pH'
            for hp in range(4):
                g = hp // 2
                tiles = []
                for t in range(q0 // 128 + NSUB):
                    mk = caus[:, (t * 128 - q0) // 128, :] if t * 128 >= q0 else None
                    tiles.append((KTs[g][:, t * 128:(t + 1) * 128], t // 16, mk, Vs[:, t, g, :], None))
                attend(hp, 1, tiles, False)
            P.tag = 'pJ'
            for s in range(NSUB):
                for c in range(4):
                    P.tr(psT[:, c * 128:(c + 1) * 128], convo[:, c, s * 128:(s + 1) * 128], identb[:])
                for g in range(2):
                    P.tr(psT[:, 512 + g * 128:512 + (g + 1) * 128], convo[:, 4 + g, s * 128:(s + 1) * 128], identb[:])
                P.copy(xtm1[:], psT[:, 0:512], eng='act')
                P.copy(Btm1[:], psT[:, 512:768].rearrange("p (g n) -> p g n", g=2))
                ssd_chunk(K, T, 128,
                          [convo[:, 4 + g, s * 128:(s + 1) * 128] for g in range(2)],
                          [convo[:, 6 + g, s * 128:(s + 1) * 128] for g in range(2)],
                          xtm1[:], [Btm1[:, g, :] for g in range(2)], dtv[:, s, :], av[:, s, :], zs[:, s, :],
                          hTf, hTb, ymix[:, s, :])
            P.tag = 'pK'
            mixT = hT
            for s in range(NSUB):
                P.copy(attb[:], att[:, s, :], eng='pool')
                for c in range(4):
                    P.tr(psT[:, c * 128:(c + 1) * 128], attb[:, c * 128:(c + 1) * 128], identb[:])
                for c in range(4):
                    P.tr(psT[:, 512 + c * 128:512 + (c + 1) * 128], ymix[:, s, c * 128:(c + 1) * 128], identb[:])
                for c in range(4):
                    P.copy(mixT[:, c, s * 128:(s + 1) * 128], psT[:, c * 128:(c + 1) * 128], eng='act')
                for c in range(4):
                    P.ts(mixT[:, 4 + c, s * 128:(s + 1) * 128], psT[:, 512 + c * 128:512 + (c + 1) * 128],
                         K['snw'][:, c:c + 1], None, op0=ALU.mult)
            for s in range(NSUB):
                r0 = base + q0 + s * 128
                P.dma(xres[:], x1_d[r0:r0 + 128, :])
                for h2 in range(2):
                    ps = nextM()
                    for k in range(8):
                        P.mm(ps[:, :], mixT[:, k, s * 128:(s + 1) * 128], Wout[:, k, h2 * 512:(h2 + 1) * 512],
                             start=(k == 0), stop=(k == 7))
                    P.tt(xres[:, h2 * 512:(h2 + 1) * 512], ps[:, :], xres[:, h2 * 512:(h2 + 1) * 512], ALU.add)
                P.dma(x2_d[r0:r0 + 128, :], xres[:])
                if cfg.dbg:
                    P.dma(dbg['att'][r0:r0 + 128, :], att[:, s, :])
                    P.dma(dbg['x2'][r0:r0 + 128, :], xres[:])
        pst = PSF[0]
        for c in range(8):
            P.tr(pst[0:3, c * 128:(c + 1) * 128] if False else PSF[c % 2][0:3, 0:128], hist[:, c, :], identf[:])
            P.copy(ostg[0:3, c * 128:(c + 1) * 128], PSF[c % 2][0:3, 0:128])
        P.dma(o_conv_p[sq, :, :], ostg[0:3, :])
        for c in range(4):
            ps = PSF[c % 2]
            P.tr(ps[:, 0:128], hTf[:, c * 128:(c + 1) * 128], identf[:])
            P.copy(ostg[:, c * 128:(c + 1) * 128], ps[:, 0:128])
            P.dma(o_ssm_p[sq, c * 128:(c + 1) * 128, :], ostg[:, c * 128:(c + 1) * 128])
        sc.close()

    def sample_group(K):
        P.tag = 's_group'
        sc = Scope()
        identf, identb, Win = K['identf'], K['identb'], K['Win']
        stat, junk = K['stat'], K['junk']
        NB = cfg.nseq_s
        hTs = sc.sb("g_hTs", [128, 8, NS], BF16)
        hb = sc.sb("g_hb", [128, D], BF16)
        xin = sc.sb("g_xin", [128, D], F32)
        qTs = sc.sb("g_qTs", [128, 8, NS], BF16)
        ksTs = sc.sb("g_ksTs", [128, 2, NS], BF16)
        kwTs = sc.sb("g_kwTs", [128, 2, NS], BF16)
        xbcT = sc.sb("g_xbcT", [128, 8, NS], F32)
        histT = sc.sb("g_histT", [128, 8, NB * 3], F32)
        ust = [sc.sb("g_ust%d" % i, [128, 512], F32) for i in range(2)]
        psM = PSF[0:2]
        psT = PSB
        mcnt = [0]

        def nextM():
            mcnt[0] += 1
            return psM[mcnt[0] % 2]

        for t0 in range(0, NS, 128):
            L = min(128, NS - t0)
            P.dma(xin[0:L, :], x1_d[NTP + t0:NTP + t0 + L, :])
            r = rms_rstd(stat, xin[0:L, :], L, 0, junk)
            P.ts(hb[0:L, :], xin[0:L, :], r, None, op0=ALU.mult)
            for k in range(8):
                P.tr(psT[:, k * 128:k * 128 + L], hb[0:L, k * 128:(k + 1) * 128], identb[0:L, 0:L])
            for k in range(8):
                P.ts(hTs[:, k, t0:t0 + L], psT[:, k * 128:k * 128 + L], K['nwT'][:, k:k + 1], None, op0=ALU.mult)

        def proj_fm(col0, ncol):
            ps = nextM()
            for k in range(8):
                P.mm(ps[0:ncol, 0:NS], Win[:, k, col0:col0 + ncol], hTs[:, k, :], start=(k == 0), stop=(k == 7))
            return ps

        for h in range(8):
            ps = proj_fm(Q0 + h * 64, 64)
            P.act(qTs[0:64, h, :], ps[0:64, 0:NS], AF.Copy, scale=0.125)
        for g in range(2):
            ps = proj_fm(KS + g * 64, 64)
            P.copy(ksTs[0:64, g, :], ps[0:64, 0:NS])
            ps = proj_fm(KW + g * 64, 64)
            P.copy(kwTs[0:64, g, :], ps[0:64, 0:NS])
        for c in range(8):
            ps = proj_fm(XB + c * 128, 128)
            P.copy(xbcT[:, c, :], ps[:, 0:NS], eng='act')
        for r0 in range(0, NB * 3, 96):
            L = min(96, NB * 3 - r0)
            P.dma(xin[0:L, :], state_conv[r0:r0 + L, :])
            for c in range(8):
                ps = nextM()
                P.tr(ps[:, 0:L], xin[0:L, c * 128:(c + 1) * 128], identf[0:L, 0:L])
                P.copy(histT[:, c, r0:r0 + L], ps[:, 0:L])

        P.dma(q_scr, qTs[0:64, :, :])
        P.dma(ks_scr, ksTs[0:64, :, :])
        P.dma(kw_scr, kwTs[0:64, :, :])
        P.dma(xb_scr, xbcT[:])
        P.dma(hs_scr, histT[:])
        gi = 0
        for t0 in range(0, NS, 128):
            L = min(128, NS - t0)
            for (c0, cwid, d0) in ((512, 512, 0), (1024, 280, 512), (Z0, 512, 792), (DTO, 8, 1304),
                                   (XB, 512, 1312), (XB + 512, 512, 1824)):
                ps = nextM()
                for k in range(8):
                    P.mm(ps[0:L, 0:cwid], hTs[:, k, t0:t0 + L], Win[:, k, c0:c0 + cwid], start=(k == 0), stop=(k == 7))
                st_ = ust[gi % 2]
                P.copy(st_[0:L, 0:cwid], ps[0:L, 0:cwid], eng=('act' if gi % 2 else 'dve'))
                gi += 1
                P.dma(u_scr[t0:t0 + L, d0:d0 + cwid], st_[0:L, 0:cwid])
        sc.close()

    def mixer_sample(K):
        P.tag = 's_init'
        sc = Scope()
        identf, identb, Win, Wout = K['identf'], K['identb'], K['Win'], K['Wout']
        stat, junk = K['stat'], K['junk']
        NB = cfg.nseq_s
        NPG = cfg.npages
        NKT = cfg.nkt_s
        NCH = (cfg.ncmp_s + 127) // 128
        NCB = cfg.ncmp_s
        NSL = cfg.nsel_s
        NG = (NSL + 31) // 32
        caus8 = sc.sb("caus8", [128, 32], BF16)
        far8 = sc.sb("far8", [128, 32], BF16)
        ovls = sc.sb("ovls", [128, NCH, NSL], BF16)
        vas = sc.sb("vas", [NQS, 2, NSL], F32)
        iot = sc.sb("iot", [128, NPG], F32)
        pidxf = sc.sb("pidxf", [128, NPG], F32)
        P.dma(caus8[:], C['caus8'])
        P.dma(far8[:], C['far8'])
        cmpms = sc.sb("cmpms", [128, 32], BF16)
        P.dma(cmpms[:], C['cmpm_s'])
        P.dma(ovls[:], C['ovl_s'])
        P.dma(vas[:], C['va_s'])
        P.dma(iot[:], iota_p)
        xin = sc.sb("xin_s", [128, 512], F32)
        qTs = sc.sb("qTs", [128, 8, NS], BF16)
        ksTs = sc.sb("ksTs", [128, 2, NS], BF16)
        kwTs = sc.sb("kwTs", [128, 2, NS], BF16)
        xbcT = sc.sb("xbcT", [128, 8, NS], F32)
        histT = sc.sb("histT", [128, 8, NB * 3], F32)
        xps = sc.sb("xps", [128, 8, 11], F32)
        accs = sc.sb("accs", [128, NQS], F32)
        convs = sc.sb("convs", [128, 8, NQS], BF16)
        ub = sc.sb("ub", [NQS, 1312], F32)
        pidx = sc.sb("pidx", [128, NPG], I32)
        pidxu = [sc.sb("pidxu%d" % i, [128, NPG], U32) for i in range(2)]
        raws = sc.sb("raws", [128, 2, NPG * 128], BF16)
        NPGB = 8
        PG = [sc.sb("PG%d" % i, [128, 256], F32) for i in range(NPGB)]
        Hs = [sc.sb("Hs%d" % i, [128, 2, NCH * 128], BF16) for i in range(2)]
        kcTs = sc.sb("kcTs", [128, 2, NCH * 128], BF16)
        vcs = sc.sb("vcs", [128, NCH, 2, 65], BF16)
        KTs = sc.sb("KTss", [128, 2, NKT * 128], BF16)
        Vs = sc.sb("Vss", [128, NKT, 2, 65], BF16)
        KTw = sc.sb("KTws", [128, 2, 640], BF16)
        Vw = sc.sb("Vws", [128, 5, 2, 65], BF16)
        QTa = sc.sb("QTas", [128, NG, 2, 32], BF16)
        PT = [sc.sb("PTs%d" % i, [128, 32], BF16) for i in range(3)]
        gate = sc.sb("gate_s", [NQS, 24], F32)
        zs = sc.sb("zs_s", [NQS, 512], BF16)
        dtv = sc.sb("dtv_s", [NQS, 8], F32)
        av = sc.sb("av_s", [NQS, 8], F32)
        att = sc.sb("att_s", [NQS, 512], F32)
        attb = sc.sb("attb_s", [NQS, 512], BF16)
        ymix = sc.sb("ymix_s", [NQS, 512], BF16)
        rz = sc.sb("rz_s", [NQS, 8], F32)
        coef = sc.sb("coef_s", [NQS, 8], F32)
        imp = sc.sb("imp_s", [NQS, 2, NSL], F32)
        NSLP = NG * 32
        impf = sc.sb("impf_s", [NQS, NSL], F32)
        impw = sc.sb("impw_s", [NQS, NSL], F32)
        m8 = sc.sb("m8_s", [NQS, 16], F32)
        msk = sc.sb("msk_s", [NQS, NSLP], F32)
        SELB = sc.sb("SELB_s", [NQS, NG, 96], F32)
        xtm = sc.sb("xtm_s", [NQS, 512], BF16)
        Btm = sc.sb("Btm_s", [NQS, 2, 128], BF16)
        hTf = sc.sb("hTf_s", [128, 512], F32)
        hTb = sc.sb("hTb_s", [128, 512], BF16)
        mixT = sc.sb("mixT_s", [128, 8, NQS], BF16)
        xres = sc.sb("xres_s", [NQS, D], F32)
        ostg = xin
        T = ssd_tmp(sc)
        psM = PSF[0:2]
        psS = PSF[2:4]
        psO = PSF[4]
        psI = PSF[5:7]
        psT = PSB
        mcnt = [0]

        def nextM():
            mcnt[0] += 1
            return psM[mcnt[0] % 2]

        P.dma(qTs[0:64, :, :], q_scr)
        P.dma(ksTs[0:64, :, :], ks_scr)
        P.dma(kwTs[0:64, :, :], kw_scr)
        P.dma(xbcT[:], xb_scr)
        P.dma(histT[:], hs_scr)

        P.memset(KTs[0:64, :, :], 0.0, eng='pool')
        P.memset(KTw[0:64, :, :], 0.0, eng='pool')
        for g in range(2):
            P.dma(KTs[64:128, g, :], C['kc_sel_s'])
            P.dma(KTw[64:128, g, :], C['kc_win_s'])
            P.dma(kcTs[64:128, g, :], C['kc_cmp_s'])
        P.memset(kcTs[0:64, :, :], 0.0)
        P.memset(Vs[:], 0.0, eng='pool')
        P.memset(Vs[:, 0:NPG, :, 64:65], 1.0, eng='pool')
        P.memset(Vs[0:NQS, NPG, :, 64:65], 1.0, eng='pool')
        P.memset(Vw[:], 0.0, eng='pool')
        P.memset(Vw[:, 0:4, :, 64:65], 1.0, eng='pool')
        P.memset(Vw[0:NQS, 4, :, 64:65], 1.0, eng='pool')
        P.memset(vcs[:], 0.0, eng='pool')
        P.memset(vcs[:, :, :, 64:65], 1.0, eng='pool')
        for i in range(2):
            P.memset(Hs[i][:], 0.0)
        P.memset(QTa[:], 0.0)
        for G in range(NG):
            P.dma(QTa[96:128, G, :, :], C['alq_s'])
        P.memset(SELB[:], 0.0)
        P.memset(msk[:], 0.0)

        def gather_page(dst, cache, idx):
            P.add('pool', lambda e: e.indirect_dma_start(out=dst, out_offset=None, in_=cache,
                                                          in_offset=bass.IndirectOffsetOnAxis(ap=idx, axis=0)),
                  [cache, idx], [dst], is_dma=True)

        def attend_s(g, tiles, branch, first_branch, with_imp, bg=None):
            n_t = len(tiles)
            for idx in range(n_t + 1):
                adv(bg, 1)
                if idx < n_t:
                    kt, rq, mk, vv, ov = tiles[idx]
                    sb_ = psS[idx % 2]
                    P.mm(sb_[:, 0:32], kt, rq, start=True, stop=(mk is None))
                    if mk is not None:
                        P.mm(sb_[:, 0:32], identb[:], mk, start=False, stop=True)
                    P.act(PT[idx % 3][:], sb_[:, 0:32], AF.Exp)
                if idx >= 1:
                    j = idx - 1
                    kt, rq, mk, vv, ov = tiles[j]
                    pt = PT[j % 3]
                    for hh in range(4):
                        P.mm(psO[0:NQS, hh * 65:(hh + 1) * 65], pt[:, hh * NQS:(hh + 1) * NQS], vv,
                             start=(j == 0 and hh == 0), stop=(j == n_t - 1), skip_group_check=True)
                    if with_imp:
                        for hh in range(4):
                            P.mm(psI[hh // 2][0:NQS, (hh % 2) * NSL:(hh % 2 + 1) * NSL], pt[:, hh * NQS:(hh + 1) * NQS], ov,
                                 start=(j == 0 and hh % 2 == 0), stop=(j == n_t - 1), skip_group_check=True)
            zc = psO[0:NQS, 64:64 + 65 * 3 + 1:65]
            P.ts(rz[:, 0:4], zc, 1e-30, None, op0=ALU.max)
            P.add('dve', lambda e: e.reciprocal(rz[:, 0:4], rz[:, 0:4]), [rz[:, 0:4]], [rz[:, 0:4]])
            P.tt(coef[:, 0:4], rz[:, 0:4], gate[:, branch * 8 + g * 4:branch * 8 + g * 4 + 4], ALU.mult)
            for hh in range(4):
                h = g * 4 + hh
                dst = att[:, h * 64:(h + 1) * 64]
                if first_branch:
                    P.ts(dst, psO[0:NQS, hh * 65:hh * 65 + 64], coef[:, hh:hh + 1], None, op0=ALU.mult)
                else:
                    P.stt(dst, psO[0:NQS, hh * 65:hh * 65 + 64], coef[:, hh:hh + 1], dst, ALU.mult, ALU.add)
            if with_imp:
                for hh in range(4):
                    src = psI[hh // 2][0:NQS, (hh % 2) * NSL:(hh % 2 + 1) * NSL]
                    if hh == 0:
                        P.ts(imp[:, g, :], src, rz[:, hh:hh + 1], None, op0=ALU.mult)
                    else:
                        P.stt(imp[:, g, :], src, rz[:, hh:hh + 1], imp[:, g, :], ALU.mult, ALU.add)

        pgc = [0]

        def adv(gen, n):
            if gen is None:
                return
            for _ in range(n):
                if next(gen, 'end') == 'end':
                    return

        def prep_idx(bb):
            P.dma(pidx[:], page_table[bb, :].partition_broadcast(128))
            P.ts(pidxf[:], pidx[:], 128.0, None, op0=ALU.mult)
            P.tt(pidxu[bb % 2][:], pidxf[:], iot[:], ALU.add)

        def gen_cmp(bb):
            for p in range(NPG):
                pg = PG[pgc[0] % NPGB]
                pgc[0] += 1
                gather_page(pg[:], cache_cmp, pidxu[bb % 2][:, p:p + 1])
                for i in range(2):
                    ps = nextM()
                    P.tr(ps[:, 0:128], pg[:, i * 128:(i + 1) * 128], identf[:])
                    P.copy(raws[:, i, :].rearrange("p (r c) -> p r c", r=16)[:, :, 8 * p:8 * p + 8],
                           ps[:, 0:128].rearrange("p (c r) -> p r c", r=16), eng=('act' if i == 0 else 'dve'))
                yield 1

        def gen_sel(bb):
            for p in range(NPG):
                pg = PG[pgc[0] % NPGB]
                pgc[0] += 1
                gather_page(pg[:], cache_sel, pidxu[bb % 2][:, p:p + 1])
                for g in range(2):
                    ps = nextM()
                    P.tr(ps[0:64, 0:128], pg[:, g * 64:(g + 1) * 64], identf[:])
                    P.copy(KTs[0:64, g, p * 128:(p + 1) * 128], ps[0:64, 0:128], eng=('act' if g == 0 else 'dve'))
                P.copy(Vs[:, p, :, 0:64], pg[:, 128:256].rearrange("p (g d) -> p g d", g=2), eng='act')
                yield 1

        prep_idx(0)
        gcur = gen_cmp(0)
        gsel = gen_sel(0)
        for b in range(NB):
            tb = b * NQS
            P.tag = 's_proj'
            P.dma(ub[:], u_scr[tb:tb + NQS, 0:1312])
            P.dma(xres[:], u_scr[tb:tb + NQS, 1312:2336])
            P.dma(o_cmp_s[tb:tb + NQS, :], ub[:, 0:256])
            P.dma(o_sel_s[tb:tb + NQS, :], ub[:, 256:512])
            P.dma(o_win_s[b, 512 - NQS:512, :], ub[:, 512:768])
            P.dma(o_win_s[b, 0:512 - NQS, :], cache_win[b, NQS:512, :])
            P.dma(o_conv_s[b, :, :], xres[NQS - 3:NQS, :])
            P.act(gate[:], ub[:, 768:792], AF.Sigmoid)
            P.act(zs[:], ub[:, 792:1304], AF.Silu)
            P.tt(dtv[:], ub[:, 1304:1312], K['dtb'][0:NQS, :], ALU.add)
            P.copy(xps[:, :, 0:3], histT[:, :, b * 3:(b + 1) * 3], eng='pool')
            P.copy(xps[:, :, 3:11], xbcT[:, :, tb:tb + NQS], eng='pool')
            for c in range(8):
                P.ts(accs[:], xps[:, c, 0:NQS], K['cw'][:, c, 0:1], K['cb'][:, c:c + 1], op0=ALU.mult, op1=ALU.add)
                for tap in range(1, 4):
                    P.stt(accs[:], xps[:, c, tap:tap + NQS], K['cw'][:, c, tap:tap + 1], accs[:], ALU.mult, ALU.add)
                P.act(convs[:, c, :], accs[:], AF.Silu)
            P.act(dtv[:], dtv[:], AF.Exp)
            P.act(dtv[:], dtv[:], AF.Ln, bias=1.0)
            P.tt(av[:], dtv[:], K['Aneg'][0:NQS, :], ALU.mult)
            if b + 1 < NB:
                prep_idx(b + 1)
            for g in range(2):
                P.copy(QTa[0:64, :, g, :].rearrange("p a (h t) -> p a h t", h=4),
                       qTs[0:64, g * 4:(g + 1) * 4, tb:tb + NQS].unsqueeze(1).broadcast_to([64, NG, 4, NQS]))
            P.tag = 's_cmp'
            adv(gcur, NPG)
            for i in range(2):
                for g in range(2):
                    for j0 in range(0, NCB, 512):
                        nb = min(512, NCB - j0)
                        ps = nextM()
                        for r in range(32):
                            st_ = (r % 16) * (NPG * 8) + j0 + r // 16
                            P.mm(ps[:, 0:nb], K['W1'][i][g * 64:(g + 1) * 64, r, :],
                                 raws[g * 64:(g + 1) * 64, i, st_:st_ + nb],
                                 start=(r == 0), stop=(r == 31))
                        P.act(Hs[i][:, g, j0:j0 + nb], ps[:, 0:nb], AF.Silu, bias=K['b1'][i][:, 0:1])
                        adv(gsel, 6)
            for g in range(2):
                for j0 in range(0, NCB, 512):
                    nb = min(512, NCB - j0)
                    ps = nextM()
                    P.mm(ps[0:64, 0:nb], K['W2'][0][:, :], Hs[0][:, g, j0:j0 + nb], start=True, stop=True)
                    P.copy(kcTs[0:64, g, j0:j0 + nb], ps[0:64, 0:nb])
                for c in range(NCH):
                    ps = nextM()
                    P.mm(ps[:, 0:64], Hs[1][:, g, c * 128:(c + 1) * 128], K['W2'][1][:, :], start=True, stop=True)
                    P.copy(vcs[:, c, g, 0:64], ps[:, 0:64], eng='act')
            for g in range(2):
                tiles = [(kcTs[:, g, c * 128:(c + 1) * 128], QTa[:, 0, g, :], (cmpms[:] if c == NCH - 1 else None),
                          vcs[:, c, g, :], ovls[:, c, :]) for c in range(NCH)]
                attend_s(g, tiles, 0, True, True)
            P.tag = 's_topk'
            for g in range(2):
                P.tt(impw[:], imp[:, g, :], vas[:, 0, :], ALU.mult)
                P.tt(impf[:], impw[:], vas[:, 1, :], ALU.add)
                P.add('dve', lambda e: e.max(m8[:, 0:8], impf[:]), [impf[:]], [m8[:, 0:8]])
                P.add('dve', lambda e: e.match_replace(impw[:], m8[:, 0:8], impf[:], -1e30),
                      [m8[:, 0:8], impf[:]], [impw[:]])
                P.add('dve', lambda e: e.max(m8[:, 8:16], impw[:]), [impw[:]], [m8[:, 8:16]])
                P.ts(msk[:, 0:NSL], impf[:], m8[:, 15:16], None, op0=ALU.is_ge)
                P.ts(SELB[:, :, 64:96], msk[:].rearrange("p (a b) -> p a b", a=NG), -1.0, -NEGM,
                     op0=ALU.add, op1=ALU.mult)
                for G in range(NG):
                    pst = nextM()
                    P.tr(pst[0:96, 0:NQS], SELB[:, G, :], identf[0:NQS, 0:NQS])
                    P.copy(QTa[64:96, G, g, :].rearrange("p (h t) -> p h t", h=4),
                           pst[64:96, 0:NQS].unsqueeze(1).broadcast_to([32, 4, NQS]))
            P.tag = 's_sel'
            adv(gsel, NPG)
            gcur = gen_cmp(b + 1) if b + 1 < NB else None
            for g in range(2):
                P.copy(KTs[0:64, g, NPG * 128:NPG * 128 + NQS], ksTs[0:64, g, tb:tb + NQS])
            P.copy(Vs[0:NQS, NPG, :, 0:64], ub[:, 384:512].rearrange("p (g d) -> p g d", g=2))
            for g in range(2):
                tiles = []
                for t in range(NKT):
                    mk = caus8[:] if t == NPG else None
                    tiles.append((KTs[:, g, t * 128:(t + 1) * 128], QTa[:, t // 16, g, :], mk, Vs[:, t, g, :], None))
                attend_s(g, tiles, 1, False, False, bg=gcur)
            P.tag = 's_win'
            adv(gcur, NPG)
            gsel = gen_sel(b + 1) if b + 1 < NB else None
            for t in range(4):
                pg = PG[pgc[0] % NPGB]
                pgc[0] += 1
                P.dma(pg[:], cache_win[b, t * 128:(t + 1) * 128, :])
                for g in range(2):
                    ps = nextM()
                    P.tr(ps[0:64, 0:128], pg[:, g * 64:(g + 1) * 64], identf[:])
                    P.copy(KTw[0:64, g, t * 128:(t + 1) * 128], ps[0:64, 0:128], eng=('act' if g == 0 else 'dve'))
                P.copy(Vw[:, t, :, 0:64], pg[:, 128:256].rearrange("p (g d) -> p g d", g=2), eng='act')
            for g in range(2):
                P.copy(KTw[0:64, g, 512:512 + NQS], kwTs[0:64, g, tb:tb + NQS])
            P.copy(Vw[0:NQS, 4, :, 0:64], ub[:, 640:768].rearrange("p (g d) -> p g d", g=2))
            for g in range(2):
                tiles = []
                for t in range(5):
                    mk = far8[:] if t == 0 else (caus8[:] if t == 4 else None)
                    tiles.append((KTw[:, g, t * 128:(t + 1) * 128], QTa[:, 0, g, :], mk, Vw[:, t, g, :], None))
                attend_s(g, tiles, 2, False, False, bg=gsel)
            P.tag = 's_ssd'
            for c in range(4):
                ps = nextM()
                P.dma(ostg[:, 0:128], state_ssm[b, c * 128:(c + 1) * 128, :])
                P.tr(ps[:, 0:128], ostg[:, 0:128], identf[:])
                P.copy(hTf[:, c * 128:(c + 1) * 128], ps[:, 0:128])
            P.copy(hTb[:], hTf[:], eng='pool')
            adv(gsel, 8)
            for c in range(4):
                P.tr(psT[0:NQS, c * 128:(c + 1) * 128], convs[:, c, :], identb[:])
            for g in range(2):
                P.tr(psT[0:NQS, 512 + g * 128:512 + (g + 1) * 128], convs[:, 4 + g, :], identb[:])
            P.copy(xtm[:], psT[0:NQS, 0:512], eng='act')
            P.copy(Btm[:], psT[0:NQS, 512:768].rearrange("p (g n) -> p g n", g=2))
            ssd_chunk(K, T, NQS,
                      [convs[:, 4 + g, :] for g in range(2)],
                      [convs[:, 6 + g, :] for g in range(2)],
                      xtm[:], [Btm[:, g, :] for g in range(2)], dtv[:], av[:], zs[:], hTf, hTb, ymix[:])
            for c in range(4):
                ps = nextM()
                P.tr(ps[:, 0:128], hTf[:, c * 128:(c + 1) * 128], identf[:])
                P.copy(ostg[:, c * 128:(c + 1) * 128], ps[:, 0:128])
                P.dma(o_ssm_s[b, c * 128:(c + 1) * 128, :], ostg[:, c * 128:(c + 1) * 128])
            P.tag = 's_mix'
            adv(gsel, 8)
            P.copy(attb[:], att[:], eng='pool')
            for c in range(4):
                P.tr(psT[:, c * 128:c * 128 + NQS], attb[:, c * 128:(c + 1) * 128], identb[0:NQS, 0:NQS])
            for c in range(4):
                P.tr(psT[:, 512 + c * 128:512 + c * 128 + NQS], ymix[:, c * 128:(c + 1) * 128], identb[0:NQS, 0:NQS])
            for c in range(4):
                P.copy(mixT[:, c, :], psT[:, c * 128:c * 128 + NQS], eng='act')
                P.ts(mixT[:, 4 + c, :], psT[:, 512 + c * 128:512 + c * 128 + NQS], K['snw'][:, c:c + 1], None,
                     op0=ALU.mult)
            P.dma(xres[:], x1_d[NTP + tb:NTP + tb + NQS, :])
            for h2 in range(2):
                ps = nextM()
                for k in range(8):
                    P.mm(ps[0:NQS, :], mixT[:, k, :], Wout[:, k, h2 * 512:(h2 + 1) * 512], start=(k == 0), stop=(k == 7))
                P.tt(xres[:, h2 * 512:(h2 + 1) * 512], ps[0:NQS, :], xres[:, h2 * 512:(h2 + 1) * 512], ALU.add)
            P.dma(x2_d[NTP + tb:NTP + tb + NQS, :], xres[:])
            if cfg.dbg:
                P.dma(dbg['att'][NTP + tb:NTP + tb + NQS, :], att[:])
                P.dma(dbg['x2'][NTP + tb:NTP + tb + NQS, :], xres[:])
        sc.close()

    cur = x_in
    if 'ffn1' in cfg.stages:
        ffn_phase("ffn1", x_in, x1_d, False)
    if 'mixp' in cfg.stages or 'mixs' in cfg.stages:
        mixer_phase()
    if 'mixs' not in cfg.stages and NS > 0:
        P.dma(x2_d[NTP:NT, :], x1_d[NTP:NT, :])
    if 'mixp' not in cfg.stages and NTP > 0:
        P.dma(x2_d[0:NTP, :], x1_d[0:NTP, :])
    if 'ffn2' in cfg.stages:
        ffn_phase("ffn2", x2_d, y_out, True)
    P.finalize(top)
    top.close()
    return nc, P, hc


WNAMES = ("ffn1_wg", "ffn1_wu", "ffn1_wd", "ffn2_wg", "ffn2_wu", "ffn2_wd", "norm_ffn1", "norm_ffn2", "norm_mix",
          "w_in", "w_out", "cmp_pe_k", "cmp_w1_k", "cmp_w2_k", "cmp_pe_v", "cmp_w1_v", "cmp_w2_v", "conv_w",
          "conv_b", "dt_bias", "a_log", "d_skip", "ssm_norm")


def make_in_maps(cfg, hc, inp, ncores):
    nsp, nss = cfg.nseq_p, cfg.nseq_s
    base = {}
    for n in WNAMES:
        base[n] = np.ascontiguousarray(inp[n][0], dtype=np.float32)
    base['norm_final'] = np.ascontiguousarray(inp['norm_final'], dtype=np.float32)
    for k, v in hc.items():
        base['c_' + k] = v
    base['cache_cmp'] = np.ascontiguousarray(inp['cache_cmp'][0]).reshape(-1, 256)
    base['cache_sel'] = np.ascontiguousarray(inp['cache_sel'][0]).reshape(-1, 256)
    base['iota_p'] = np.ascontiguousarray(np.tile(np.arange(128, dtype=np.float32)[:, None], (1, cfg.npages)))
    maps = []
    for c in range(ncores):
        m = dict(base)
        xp = inp['x_prompt'][c * nsp:(c + 1) * nsp].reshape(-1, D)
        xs = inp['x_sample'][c * nss:(c + 1) * nss].reshape(-1, D)
        m['x_in'] = np.ascontiguousarray(np.concatenate([xp, xs], axis=0))
        m['cache_win'] = np.ascontiguousarray(inp['cache_win'][0, c * nss:(c + 1) * nss]).reshape(nss, 512, 256)
        m['state_conv'] = np.ascontiguousarray(inp['state_conv'][0, c * nss:(c + 1) * nss]).reshape(nss * 3, 1024)
        m['state_ssm'] = np.ascontiguousarray(inp['state_ssm'][0, c * nss:(c + 1) * nss]).reshape(nss, 512, 128)
        m['page_table'] = np.ascontiguousarray(inp['page_table'][c * nss:(c + 1) * nss]).astype(np.int32)
        maps.append(m)
    return maps


def assemble(cfg, results, ncores):
    nsp, nss, S = cfg.nseq_p, cfg.nseq_s, cfg.seq
    cat = lambda name: [np.asarray(r[name]) for r in results]
    y = cat('y_out')
    y_p = np.concatenate([a[:cfg.ntok_p].reshape(nsp, S, D) for a in y], 0)
    y_s = np.concatenate([a[cfg.ntok_p:].reshape(nss, NQS, D) for a in y], 0)
    ncp = np.concatenate([a.reshape(nsp, S, 2, 2, 64) for a in cat('o_cmp_p')], 0)[None]
    nsl = np.concatenate([a.reshape(nsp, S, 2, 2, 64) for a in cat('o_sel_p')], 0)[None]
    nwp = np.concatenate([a.reshape(nsp, 512, 2, 2, 64) for a in cat('o_win_p')], 0)[None]
    ncv = np.concatenate([a.reshape(nsp, 3, 1024) for a in cat('o_conv_p')], 0)[None]
    nsm = np.concatenate([a.reshape(nsp, 8, 64, 128) for a in cat('o_ssm_p')], 0)[None]
    scp = np.concatenate([a.reshape(nss, NQS, 2, 2, 64) for a in cat('o_cmp_s')], 0)[None]
    ssl = np.concatenate([a.reshape(nss, NQS, 2, 2, 64) for a in cat('o_sel_s')], 0)[None]
    swp = np.concatenate([a.reshape(nss, 512, 2, 2, 64) for a in cat('o_win_s')], 0)[None]
    scv = np.concatenate([a.reshape(nss, 3, 1024) for a in cat('o_conv_s')], 0)[None]
    ssm = np.concatenate([a.reshape(nss, 8, 64, 128) for a in cat('o_ssm_s')], 0)[None]
    return (y_p, y_s, ncp, nsl, nwp, ncv, nsm, scp, ssl, swp, scv, ssm)


def kernel(**inputs):
    inputs = {k: np.asarray(v) for k, v in inputs.items()}
    B, S = inputs['x_prompt'].shape[0], inputs['x_prompt'].shape[1]
    NBS = inputs['x_sample'].shape[0]
    npages = inputs['page_table'].shape[1]
    ncores = 8
    cfg = Cfg(nseq_p=B // ncores, seq=S, nseq_s=NBS // ncores, past=npages * 128,
              n_phys=inputs['cache_cmp'].shape[1])
    nc, P, hc = build(cfg)
    maps = make_in_maps(cfg, hc, inputs, ncores)
    res = run_bass_kernel_spmd(nc, maps, core_ids=list(range(ncores)))
    outs = assemble(cfg, res.results, ncores)
    return tuple(np.ascontiguousarray(o, dtype=np.float32) for o in outs)
```

```python
import numpy as np
import ml_dtypes
from contextlib import ExitStack
import concourse.bass as bass
import concourse.mybir as mybir
from concourse.bass_utils import run_bass_kernel_spmd


F32 = mybir.dt.float32
BF16 = mybir.dt.bfloat16
I32 = mybir.dt.int32
U32 = mybir.dt.uint32
AF = mybir.ActivationFunctionType
ALU = mybir.AluOpType
AX = mybir.AxisListType

DMA_K = 6
COMPUTE = ('pe', 'act', 'dve', 'pool')


class Op:
    __slots__ = ('stream', 'fn', 'is_dma', 'deps', 'signal', 'sem', 'val', 'waits', 'clock', 'idx', 'tag')


class Prog:
    def __init__(self, nc):
        self.nc = nc
        self.ops = []
        self.streams = {s: [] for s in ('pe', 'act', 'dve', 'pool', 'sp')}
        self.acc = {}
        self.dma_count = {s: 0 for s in self.streams}
        self.pending = {s: set() for s in self.streams}
        self.recent_dma = {s: [] for s in self.streams}

    @staticmethod
    def region(ap):
        t = ap.tensor
        name = t.name
        tn = type(t).__name__
        ext = 0
        aps = list(ap.ap)
        if tn.startswith('DRam'):
            for s, c in aps:
                ext += (c - 1) * abs(s)
            return (name, 0, 1, ap.offset, ap.offset + ext + 1)
        shape = list(t.shape)
        pstride = 1
        for d in shape[1:]:
            pstride *= d
        p0 = ap.offset // pstride
        f0 = ap.offset % pstride
        npart = aps[0][1]
        if tn.startswith('PSum'):
            return (name, 0, 128, 0, 1 << 30)
        for s, c in aps[1:]:
            ext += (c - 1) * abs(s)
        return (name, p0, p0 + npart, f0, f0 + ext + 1)

    def _deps_for(self, idx, stream, is_dma, reads, writes):
        deps = set()
        for is_w, aps in ((False, reads), (True, writes)):
            for ap in aps:
                if ap is None:
                    continue
                name, p0, p1, f0, f1 = self.region(ap)
                if type(ap.tensor).__name__.startswith('PSum'):
                    is_w = True
                lst = self.acc.setdefault(name, [])
                keep = []
                for a in lst:
                    oi, ow, q0, q1, g0, g1 = a
                    ov = not (q1 <= p0 or p1 <= q0 or g1 <= f0 or f1 <= g0)
                    if ov and (is_w or ow) and oi != idx:
                        o = self.ops[oi]
                        same = (o.stream == stream) and not o.is_dma and not is_dma
                        if same and stream == 'pe':
                            pass
                        elif same and stream != 'pool' and (not ow) and is_w:
                            pass
                        else:
                            deps.add(oi)
                    covered = is_w and ov and p0 <= q0 and q1 <= p1 and f0 <= g0 and g1 <= f1
                    if not covered:
                        keep.append(a)
                keep.append([idx, is_w, p0, p1, f0, f1])
                self.acc[name] = keep
        return deps

    def add(self, stream, fn, reads=(), writes=(), is_dma=False):
        op = Op()
        op.idx = len(self.ops)
        op.stream = stream
        op.fn = fn
        op.is_dma = is_dma
        op.signal = is_dma
        op.sem = None
        op.val = 0
        op.waits = []
        op.tag = getattr(self, 'tag', '')
        self.ops.append(op)
        op.deps = self._deps_for(op.idx, stream, is_dma, reads, writes)
        if self.pending[stream]:
            op.deps |= self.pending[stream]
            self.pending[stream] = set()
        if is_dma:
            self.recent_dma[stream] = (self.recent_dma[stream] + [op.idx])[-DMA_K:]
        for d in op.deps:
            self.ops[d].signal = True
        self.streams[stream].append(op)
        return op

    def fence(self):
        F = set()
        for st, lst in self.streams.items():
            for o in reversed(lst):
                if not o.is_dma:
                    F.add(o.idx)
                    break
            F |= set(self.recent_dma[st])
        for st in self.streams:
            self.pending[st] = set(F)
        self.acc = {}

    def dma(self, out, in_, stream='sp', **kw):
        return self.add(stream, lambda e: e.dma_start(out=out, in_=in_, **kw), [in_], [out], is_dma=True)

    def mm(self, out, lhsT, rhs, start=True, stop=True, **kw):
        return self.add('pe', lambda e: e.matmul(out, lhsT, rhs, start=start, stop=stop, **kw),
                        [lhsT, rhs], [out])

    def tr(self, out, in_, ident):
        return self.add('pe', lambda e: e.transpose(out, in_, ident), [in_, ident], [out])

    def act(self, out, in_, func, bias=None, scale=None, accum_out=None, eng='act'):
        rd = [in_]
        kw = {}
        if bias is not None:
            kw['bias'] = bias
            if not isinstance(bias, (int, float)):
                rd.append(bias)
        if scale is not None:
            kw['scale'] = scale
            if not isinstance(scale, (int, float)):
                rd.append(scale)
        wr = [out]
        if accum_out is not None:
            kw['accum_out'] = accum_out
            wr.append(accum_out)
        return self.add('act', lambda e: e.activation(out, in_, func, **kw), rd, wr)

    def tt(self, out, in0, in1, op, eng='dve'):
        return self.add(eng, lambda e: e.tensor_tensor(out, in0, in1, op), [in0, in1], [out])

    def ts(self, out, in0, s1, s2=None, op0=ALU.mult, op1=None, eng='dve', accum_out=None):
        rd = [in0]
        if not isinstance(s1, (int, float)):
            rd.append(s1)
        if s2 is not None and not isinstance(s2, (int, float)):
            rd.append(s2)
        kw = {}
        if op1 is not None:
            kw['op1'] = op1
        wr = [out]
        if accum_out is not None:
            kw['accum_out'] = accum_out
            wr.append(accum_out)
        return self.add(eng, lambda e: e.tensor_scalar(out, in0, s1, s2, op0, **kw), rd, wr)

    def stt(self, out, in0, scalar, in1, op0, op1):
        rd = [in0, in1]
        if not isinstance(scalar, (int, float)):
            rd.append(scalar)
        return self.add('dve', lambda e: e.scalar_tensor_tensor(out, in0, scalar, in1, op0, op1), rd, [out])

    def copy(self, out, in_, eng='dve'):
        if eng == 'act':
            return self.add('act', lambda e: e.copy(out, in_), [in_], [out])
        return self.add(eng, lambda e: e.tensor_copy(out, in_), [in_], [out])

    def memset(self, ap, val, eng='dve'):
        return self.add(eng, lambda e: e.memset(ap, val), [], [ap])

    def finalize(self, stack):
        nc = self.nc
        sems = {}
        for s in COMPUTE:
            sems[s] = stack.enter_context(nc.semaphore('s_' + s))
        dsem = {}
        for s in ('sp', 'pool', 'act'):
            dsem[s] = [stack.enter_context(nc.semaphore('d_%s%d' % (s, i))) for i in range(DMA_K)]
        cnt = {s: 0 for s in COMPUTE}
        dcnt = {}
        dlast = {}
        for op in self.ops:
            if op.is_dma:
                n = dcnt.get(op.stream, 0)
                dcnt[op.stream] = n + 1
                k = n % DMA_K
                op.sem = ('d', op.stream, k)
                op.val = 16 * (n // DMA_K + 1)
                prev = dlast.get((op.stream, k))
                if prev is not None:
                    op.deps.add(prev)
                dlast[(op.stream, k)] = op.idx
            elif op.signal:
                cnt[op.stream] += 1
                op.sem = ('c', op.stream)
                op.val = cnt[op.stream]
        known = {s: {} for s in self.streams}
        for op in self.ops:
            kn = known[op.stream]
            for d in sorted(op.deps):
                p = self.ops[d]
                if kn.get(p.sem, 0) >= p.val:
                    continue
                op.waits.append((p.sem, p.val))
                for k2, v2 in p.clock.items():
                    if kn.get(k2, 0) < v2:
                        kn[k2] = v2
            clk = dict(kn)
            if op.sem is not None:
                clk[op.sem] = op.val
                if not op.is_dma:
                    pass
            op.clock = clk
        self.total_waits = sum(len(o.waits) for o in self.ops)

        def semobj(key):
            if key[0] == 'c':
                return sems[key[1]]
            return dsem[key[1]][key[2]]

        block = stack.enter_context(nc.Block())

        def emit_stream(eng, name):
            for op in self.streams[name]:
                for key, v in op.waits:
                    eng.wait_ge(semobj(key), v)
                inst = op.fn(eng)
                if op.sem is not None:
                    inst.then_inc(semobj(op.sem), 16 if op.is_dma else 1)
            n = dcnt.get(name, 0)
            for k in range(min(n, DMA_K)):
                last = (n - 1 - k) // DMA_K * DMA_K + k
                tot = 16 * (last // DMA_K + 1)
                eng.wait_ge(dsem[name][k], tot)

        @block.sync
        def _(e):
            emit_stream(e, 'sp')

        @block.tensor
        def _(e):
            emit_stream(e, 'pe')

        @block.scalar
        def _(e):
            emit_stream(e, 'act')

        @block.vector
        def _(e):
            emit_stream(e, 'dve')

        @block.gpsimd
        def _(e):
            emit_stream(e, 'pool')


NPBF = ml_dtypes.bfloat16
D = 1024
DFF = 2816
NF = DFF // 128
EPS = 1e-6
Q0, KC, VC, KS, VS, KW, VW, GT, Z0, XB, DTO = 0, 512, 640, 768, 896, 1024, 1152, 1280, 1304, 1816, 2840
INW = 2848
NEGM = -30000.0
SLOPES = [2.0 ** (-(h + 1)) for h in range(8)]
NQS = 8
TQ = 256
NSUB = TQ // 128
NCM = 2048 // TQ + 1


class Cfg:
    def __init__(self, nseq_p=2, seq=4096, nseq_s=16, past=8192, n_phys=10240,
                 stages=('ffn1', 'mixp', 'mixs', 'ffn2'), dbg=False):
        self.nseq_p = nseq_p
        self.seq = seq
        self.nseq_s = nseq_s
        self.past = past
        self.n_phys = n_phys
        self.stages = stages
        self.dbg = dbg
        self.ntok_p = nseq_p * seq
        self.ntok_s = nseq_s * NQS
        self.ntok = self.ntok_p + self.ntok_s
        self.npages = past // 128
        self.nkt_s = self.npages + 1
        self.ncmp_s = past // 16 - 1
        self.nsel_s = past // 64 + 1


def k_rows(pos, with_e, cmp_n=None):
    n = len(pos) if pos is not None else len(cmp_n)
    r = np.zeros((64, n), np.float32)
    if cmp_n is None:
        if with_e:
            jj = (pos // 64) % 32
            r[jj, np.arange(n)] = 1.0
        r[32] = pos % 128
        r[33] = pos // 128
        r[34] = 1.0
        r[35] = 1.0
    else:
        r[34] = 1.0
        r[35] = 1.0
        r[36] = cmp_n % 128
        r[37] = cmp_n // 128
        r[38] = 1.0
    return r.astype(NPBF)


def q_rows(qpos, slope):
    n = len(qpos)
    r = np.zeros((32, n), np.float32)
    r[0] = slope
    r[1] = 128 * slope
    r[2] = -slope * 64 * (qpos // 64)
    r[3] = -slope * (qpos % 64)
    r[4] = 16 * slope
    r[5] = 2048 * slope
    r[6] = 31 * slope
    return r


def host_consts(cfg):
    c = {}
    S = cfg.seq
    c['ident'] = np.eye(128, dtype=np.float32)
    c['tri'] = np.triu(np.ones((128, 128), np.float32))
    c['ones'] = np.ones((128, 128), np.float32)
    qp = np.arange(S)
    c['alq'] = np.stack([q_rows(qp, SLOPES[h]) for h in range(8)], axis=1).astype(NPBF)
    c['kc_sel'] = k_rows(qp, True)
    c['kc_win'] = k_rows(qp, False)
    c['kc_cmp'] = k_rows(None, False, cmp_n=np.arange(256))
    ki = np.arange(128)[:, None]
    qi = np.arange(TQ)[None, :]
    c['caus'] = np.stack([np.where(128 * v + ki > qi, NEGM, 0.0) for v in range(NSUB)], axis=1).astype(NPBF)
    c['far'] = np.stack([np.where(qi - ki >= 512 - 128 * u, NEGM, 0.0) for u in range(1, 5)], axis=1).astype(NPBF)
    c['cmpm'] = np.stack([np.where(16 * ki + 31 > qi + TQ * r, NEGM, 0.0) for r in range(NCM)], axis=1).astype(NPBF)
    nb = S // 16 - 1
    nselp = S // 64
    ov = np.zeros((256, 64), np.float32)
    for i in range(min(nb, 256)):
        for j in (i // 4, (i + 1) // 4):
            if j < 64:
                ov[i, j] += 1.0
    c['ovl'] = ov.reshape(2, 128, 64).transpose(1, 0, 2).copy().astype(NPBF)
    j = np.arange(64)[None, :]
    q = qp[:, None]
    forced = (j == q // 64) | (j == 0)
    valid = (j * 64 <= q) & (j < nselp)
    VAL = np.where(forced, 0.0, np.where(valid, 1.0, 0.0))
    ADD = np.where(forced, 1e9, np.where(valid, 0.0, -1.0))
    c['va'] = np.concatenate([VAL, ADD], axis=1).astype(np.float32)
    PAST = cfg.past
    qs = PAST + np.arange(NQS)
    aq = np.zeros((32, 2, 4, NQS), np.float32)
    for g in range(2):
        for hh in range(4):
            aq[:, g, hh, :] = q_rows(qs, SLOPES[g * 4 + hh])
    c['alq_s'] = aq.reshape(32, 2, 32).astype(NPBF)
    nk = cfg.nkt_s * 128
    c['kc_sel_s'] = k_rows(np.arange(nk), True)
    c['kc_win_s'] = k_rows(PAST - 512 + np.arange(640), False)
    c['kc_cmp_s'] = k_rows(None, False, cmp_n=np.arange(((cfg.ncmp_s + 127) // 128) * 128))
    ki = np.arange(128)[:, None]
    qi8 = np.tile(np.arange(NQS), 4)[None, :]
    c['caus8'] = np.where((ki > qi8) | (ki >= NQS), NEGM, 0.0).astype(NPBF)
    c['far8'] = np.where(ki <= qi8, NEGM, 0.0).astype(NPBF)
    nlast = cfg.ncmp_s - ((cfg.ncmp_s + 127) // 128 - 1) * 128
    c['cmpm_s'] = np.where((ki >= nlast) & (qi8 >= 0), NEGM, 0.0).astype(NPBF)
    nch = (cfg.ncmp_s + 127) // 128
    nsl = cfg.nsel_s
    ovs = np.zeros((nch * 128, nsl), np.float32)
    for i in range(cfg.ncmp_s):
        for jx in (i // 4, (i + 1) // 4):
            if jx < nsl:
                ovs[i, jx] += 1.0
    c['ovl_s'] = ovs.reshape(nch, 128, nsl).transpose(1, 0, 2).copy().astype(NPBF)
    js = np.arange(nsl)[None, :]
    forced_s = (js == (qs[:, None] // 64)) | (js == 0)
    vas = np.zeros((NQS, 2, nsl), np.float32)
    vas[:, 0] = np.where(forced_s, 0.0, 1.0)
    vas[:, 1] = np.where(forced_s, 1e9, 0.0)
    c['va_s'] = vas
    return c


CONST_DT = {'alq': BF16, 'kc_sel': BF16, 'kc_win': BF16, 'kc_cmp': BF16, 'caus': BF16, 'far': BF16, 'cmpm': BF16,
            'ovl': BF16, 'alq_s': BF16, 'kc_sel_s': BF16, 'kc_win_s': BF16, 'kc_cmp_s': BF16, 'caus8': BF16,
            'far8': BF16, 'ovl_s': BF16, 'cmpm_s': BF16}


def build(cfg):
    nc = bass.Bass("TRN2", target_bir_lowering=False)
    P = Prog(nc)
    top = ExitStack()
    NT, NTP, NS = cfg.ntok, cfg.ntok_p, cfg.ntok_s
    S = cfg.seq
    PAST = cfg.past

    def din(name, shape, dt=F32):
        return nc.dram_tensor(name, list(shape), dt, kind="ExternalInput").ap()

    def dout(name, shape, dt=F32):
        return nc.dram_tensor(name, list(shape), dt, kind="ExternalOutput").ap()

    def dscr(name, shape, dt=F32):
        return nc.dram_tensor(name, list(shape), dt, kind="Internal").ap()

    x_in = din("x_in", [NT, D])
    hc = host_consts(cfg)
    C = {k: din("c_" + k, v.shape, CONST_DT.get(k, F32)) for k, v in hc.items()}
    w = {}
    for nm, shp in (("ffn1_wg", [D, DFF]), ("ffn1_wu", [D, DFF]), ("ffn1_wd", [DFF, D]),
                    ("ffn2_wg", [D, DFF]), ("ffn2_wu", [D, DFF]), ("ffn2_wd", [DFF, D]),
                    ("norm_ffn1", [D]), ("norm_ffn2", [D]), ("norm_final", [D]), ("norm_mix", [D]),
                    ("w_in", [D, INW]), ("w_out", [D, D]),
                    ("cmp_pe_k", [32, 64]), ("cmp_w1_k", [2048, 128]), ("cmp_w2_k", [128, 64]),
                    ("cmp_pe_v", [32, 64]), ("cmp_w1_v", [2048, 128]), ("cmp_w2_v", [128, 64]),
                    ("conv_w", [4, 1024]), ("conv_b", [1024]), ("dt_bias", [8]), ("a_log", [8]),
                    ("d_skip", [8]), ("ssm_norm", [512])):
        w[nm] = din(nm, shp)
    cache_cmp = din("cache_cmp", [cfg.n_phys * 128, 256])
    cache_sel = din("cache_sel", [cfg.n_phys * 128, 256])
    cache_win = din("cache_win", [max(cfg.nseq_s, 1), 512, 256])
    state_conv = din("state_conv", [max(cfg.nseq_s, 1) * 3, 1024])
    state_ssm = din("state_ssm", [max(cfg.nseq_s, 1), 512, 128])
    page_table = din("page_table", [max(cfg.nseq_s, 1), cfg.npages], I32)
    iota_p = din("iota_p", [128, cfg.npages], F32)

    y_out = dout("y_out", [NT, D])
    o_cmp_p = dout("o_cmp_p", [max(NTP, 1), 256])
    o_sel_p = dout("o_sel_p", [max(NTP, 1), 256])
    o_win_p = dout("o_win_p", [max(cfg.nseq_p, 1), 512, 256])
    o_conv_p = dout("o_conv_p", [max(cfg.nseq_p, 1), 3, 1024])
    o_ssm_p = dout("o_ssm_p", [max(cfg.nseq_p, 1), 512, 128])
    o_cmp_s = dout("o_cmp_s", [max(NS, 1), 256])
    o_sel_s = dout("o_sel_s", [max(NS, 1), 256])
    o_win_s = dout("o_win_s", [max(cfg.nseq_s, 1), 512, 256])
    o_conv_s = dout("o_conv_s", [max(cfg.nseq_s, 1), 3, 1024])
    o_ssm_s = dout("o_ssm_s", [max(cfg.nseq_s, 1), 512, 128])
    x1_d = dscr("x1_scr", [NT, D])
    x2_d = dscr("x2_scr", [NT, D])
    NSs = max(NS, 1)
    u_scr = dscr("u_scr", [NSs, 2336])
    q_scr = dscr("q_scr", [64, 8, NSs], BF16)
    ks_scr = dscr("ks_scr", [64, 2, NSs], BF16)
    kw_scr = dscr("kw_scr", [64, 2, NSs], BF16)
    xb_scr = dscr("xb_scr", [128, 8, NSs])
    hs_scr = dscr("hs_scr", [128, 8, max(cfg.nseq_s, 1) * 3])
    dbg = {}
    if cfg.dbg:
        dbg['att'] = dout("dbg_att", [NT, 512])
        dbg['ssm'] = dout("dbg_ssm", [NT, 512])
        dbg['x2'] = dout("dbg_x2", [NT, D])

    PSF = [top.enter_context(nc.psum_tensor("psf%d" % i, [128, 512], F32)) for i in range(7)]
    PSB = top.enter_context(nc.psum_tensor("psb", [128, 1024], BF16))

    uniq = [0]

    class Scope:
        def __init__(self):
            self.st = ExitStack()

        def sb(self, name, shape, dt=F32):
            uniq[0] += 1
            return self.st.enter_context(nc.sbuf_tensor("%s_u%d" % (name, uniq[0]), list(shape), dt))

        def close(self):
            P.fence()
            self.st.close()

    def rms_rstd(sc_stat, X, L, col, junk):
        st = sc_stat
        width = X.shape[-1]
        P.act(junk[0:L, :width], X, AF.Square, accum_out=st[0:L, col:col + 1])
        P.ts(st[0:L, col + 1:col + 2], st[0:L, col:col + 1], 1.0 / width, EPS, op0=ALU.mult, op1=ALU.add)
        P.act(st[0:L, col + 2:col + 3], st[0:L, col + 1:col + 2], AF.Sqrt)
        o, i = st[0:L, col + 3:col + 4], st[0:L, col + 2:col + 3]
        P.add('dve', lambda e, o=o, i=i: e.reciprocal(o, i), [i], [o])
        return o

    def ffn_phase(pfx, src_d, dst_d, final):
        P.tag = pfx
        sc = Scope()
        identf = sc.sb("identf", [128, 128], F32)
        identb = sc.sb("identb", [128, 128], BF16)
        P.dma(identf[:], C['ident'])
        P.copy(identb[:], identf[:])
        WG = sc.sb("WG", [128, 8, DFF], BF16)
        WU = sc.sb("WU", [128, 8, DFF], BF16)
        WD = sc.sb("WD", [128, NF, D], BF16)
        stg = [sc.sb("stg%d" % i, [128, 1408], F32) for i in range(2)]
        xin = [sc.sb("xin%d" % i, [128, D], F32) for i in range(2)]
        xres = [sc.sb("xres%d" % i, [128, D], F32) for i in range(2)]
        hb = sc.sb("hb", [128, D], BF16)
        hT = sc.sb("hT", [128, 8, 512], BF16)
        AT = sc.sb("AT", [128, NF, 512], BF16)
        sil = [sc.sb("sil%d" % i, [128, 512], BF16) for i in range(2)]
        nwT = sc.sb("nwT", [128, 8], F32)
        nfin = sc.sb("nfin", [128, D], F32)
        stat = sc.sb("stat", [128, 32], F32)
        junk = sc.sb("junk", [128, D], BF16)
        psT = PSB
        psG, psU, psO = PSF[0:2], PSF[2:4], PSF[4:6]
        cnt = [0]

        def load_w(dst3, src2, nchunk, width):
            for c in range(nchunk):
                for c0 in range(0, width, 1408):
                    cw = min(1408, width - c0)
                    s = stg[cnt[0] % 2]
                    eng = ('act', 'dve', 'pool')[cnt[0] % 3]
                    cnt[0] += 1
                    P.dma(s[:, :cw], src2[c * 128:(c + 1) * 128, c0:c0 + cw])
                    P.copy(dst3[:, c, c0:c0 + cw], s[:, :cw], eng=eng)

        load_w(WG, w[pfx + "_wg"], 8, DFF)
        load_w(WU, w[pfx + "_wu"], 8, DFF)
        load_w(WD, w[pfx + "_wd"], NF, D)
        nrm = w["norm_ffn1" if pfx == "ffn1" else "norm_ffn2"]
        P.add('sp', lambda e: e.dma_start(out=nwT[:], in_=nrm.rearrange("(c p) -> p c", p=128),
                                          allow_slow_non_contiguous=True), [nrm], [nwT[:]], is_dma=True)
        if final:
            P.dma(nfin[:], w["norm_final"].partition_broadcast(128))
        tiles = []
        t0 = 0
        while t0 < NT:
            n = min(512, NT - t0)
            tiles.append((t0, n))
            t0 += n
        scn = [0]

        def subs_of(ti):
            t0, n = tiles[ti]
            return [(s0, min(128, n - s0)) for s0 in range(0, n, 128)]

        def prologue_sub(ti, si):
            t0, n = tiles[ti]
            s0, L = subs_of(ti)[si]
            X = xin[scn[0] % 2]
            scn[0] += 1
            P.dma(X[0:L, :], src_d[t0 + s0:t0 + s0 + L, :])
            r = rms_rstd(stat, X[0:L, :], L, 4 * si, junk)
            P.ts(hb[0:L, :], X[0:L, :], r, None, op0=ALU.mult)
            for k in range(8):
                P.tr(psT[:, k * 128:k * 128 + L], hb[0:L, k * 128:(k + 1) * 128], identb[0:L, 0:L])
            for k in range(8):
                if k % 2 == 0:
                    P.act(hT[:, k, s0:s0 + L], psT[:, k * 128:k * 128 + L], AF.Copy, scale=nwT[:, k:k + 1])
                else:
                    P.ts(hT[:, k, s0:s0 + L], psT[:, k * 128:k * 128 + L], nwT[:, k:k + 1], None, op0=ALU.mult)

        for si in range(len(subs_of(0))):
            prologue_sub(0, si)
        rc = 0
        for ti, (t0, n) in enumerate(tiles):
            subs = subs_of(ti)
            for f in range(NF):
                g = psG[f % 2]
                u = psU[f % 2]
                for k in range(8):
                    P.mm(g[:, :n], WG[:, k, f * 128:(f + 1) * 128], hT[:, k, :n], start=(k == 0), stop=(k == 7))
                for k in range(8):
                    P.mm(u[:, :n], WU[:, k, f * 128:(f + 1) * 128], hT[:, k, :n], start=(k == 0), stop=(k == 7))
                sl = sil[f % 2]
                P.act(sl[:, :n], g[:, :n], AF.Silu)
                P.tt(AT[:, f, :n], sl[:, :n], u[:, :n], ALU.mult)
            nxt = subs_of(ti + 1) if ti + 1 < len(tiles) else []
            for si, (s0, L) in enumerate(subs):
                X = xres[rc % 2]
                rc += 1
                P.dma(X[0:L, :], src_d[t0 + s0:t0 + s0 + L, :])
                for h2 in range(2):
                    o = psO[(si * 2 + h2) % 2]
                    for f in range(NF):
                        P.mm(o[0:L, :], AT[:, f, s0:s0 + L], WD[:, f, h2 * 512:(h2 + 1) * 512],
                             start=(f == 0), stop=(f == NF - 1))
                    P.stt(X[0:L, h2 * 512:(h2 + 1) * 512], o[0:L, :], 0.5, X[0:L, h2 * 512:(h2 + 1) * 512],
                          ALU.mult, ALU.add)
                if final:
                    r = rms_rstd(stat, X[0:L, :], L, 16 + 4 * si, junk)
                    P.stt(X[0:L, :], X[0:L, :], r, nfin[0:L, :], ALU.mult, ALU.mult)
                P.dma(dst_d[t0 + s0:t0 + s0 + L, :], X[0:L, :], stream='act')
                if si < len(nxt):
                    prologue_sub(ti + 1, si)
            for si in range(len(subs), len(nxt)):
                prologue_sub(ti + 1, si)
        sc.close()

    def mixer_phase():
        P.tag = 'mix_init'
        mc = Scope()
        identf = mc.sb("identf", [128, 128], F32)
        identb = mc.sb("identb", [128, 128], BF16)
        tri = mc.sb("tri", [128, 128], F32)
        onesf = mc.sb("onesf", [128, 128], F32)
        P.dma(identf[:], C['ident'])
        P.copy(identb[:], identf[:])
        P.dma(tri[:], C['tri'])
        P.dma(onesf[:], C['ones'])
        Wout = mc.sb("Wout", [128, 8, D], BF16)
        W1 = [mc.sb("W1_%d" % i, [128, 32, 128], BF16) for i in range(2)]
        W2 = [mc.sb("W2_%d" % i, [128, 64], BF16) for i in range(2)]
        peT = [mc.sb("peT_%d" % i, [128, 32], BF16) for i in range(2)]
        b1 = [mc.sb("b1_%d" % i, [128, 1], F32) for i in range(2)]
        nwT = mc.sb("nwT", [128, 8], F32)
        cw = mc.sb("cw", [128, 8, 4], F32)
        cb = mc.sb("cb", [128, 8], F32)
        dtb = mc.sb("dtb", [128, 8], F32)
        Aneg = mc.sb("Aneg", [128, 8], F32)
        dsk = mc.sb("dsk", [128, 8], F32)
        snw = mc.sb("snw", [128, 4], F32)
        stat = mc.sb("stat", [128, 32], F32)
        junk = mc.sb("junk", [128, D], BF16)
        wsc = Scope()
        Win = wsc.sb("Win", [128, 8, INW], BF16)
        tmpsc = Scope()
        stg = [tmpsc.sb("mstg%d" % i, [128, 1424], F32) for i in range(2)]
        cnt = [0]

        def slow_dma(out, in_):
            P.add('sp', lambda e: e.dma_start(out=out, in_=in_, allow_slow_non_contiguous=True), [in_], [out],
                  is_dma=True)

        def load_w(dst3, src2, nchunk, width, piece=1424):
            for c in range(nchunk):
                for c0 in range(0, width, piece):
                    cwid = min(piece, width - c0)
                    s = stg[cnt[0] % 2]
                    eng = ('act', 'dve', 'pool')[cnt[0] % 3]
                    cnt[0] += 1
                    P.dma(s[:, :cwid], src2[c * 128:(c + 1) * 128, c0:c0 + cwid])
                    P.copy(dst3[:, c, c0:c0 + cwid], s[:, :cwid], eng=eng)

        load_w(Win, w['w_in'], 8, INW)
        load_w(Wout, w['w_out'], 8, D)
        for i, kv in enumerate(('k', 'v')):
            w1 = w['cmp_w1_' + kv].rearrange("(r d) h -> d r h", d=64)
            for half in range(2):
                for r0 in range(0, 32, 8):
                    s = stg[cnt[0] % 2]
                    cnt[0] += 1
                    sv = s[half * 64:(half + 1) * 64, 0:1024].rearrange("p (r h) -> p r h", h=128)
                    P.dma(sv, w1[:, r0:r0 + 8, :])
                    P.copy(W1[i][half * 64:(half + 1) * 64, r0:r0 + 8, :], sv)
            s = stg[cnt[0] % 2]
            cnt[0] += 1
            P.dma(s[:, 0:64], w['cmp_w2_' + kv])
            P.copy(W2[i][:], s[:, 0:64])
            s = stg[cnt[0] % 2]
            cnt[0] += 1
            for half in range(2):
                slow_dma(s[half * 64:(half + 1) * 64, 0:32], w['cmp_pe_' + kv].rearrange("r d -> d r"))
            P.copy(peT[i][:], s[:, 0:32])
            ps = PSF[0]
            for r in range(32):
                P.mm(ps[:, 0:1], W1[i][0:64, r, :], peT[i][0:64, r:r + 1], start=(r == 0), stop=(r == 31))
            P.copy(b1[i][:], ps[:, 0:1])
        slow_dma(nwT[:], w['norm_mix'].rearrange("(c p) -> p c", p=128))
        for tap in range(4):
            slow_dma(cw[:, :, tap], w['conv_w'][tap, :].rearrange("(c p) -> p c", p=128))
        slow_dma(cb[:], w['conv_b'].rearrange("(c p) -> p c", p=128))
        slow_dma(snw[:], w['ssm_norm'].rearrange("(c p) -> p c", p=128))
        P.dma(dtb[:], w['dt_bias'].partition_broadcast(128))
        P.dma(dsk[:], w['d_skip'].partition_broadcast(128))
        P.dma(Aneg[:], w['a_log'].partition_broadcast(128))
        P.act(Aneg[:], Aneg[:], AF.Exp)
        P.ts(Aneg[:], Aneg[:], -1.0, None, op0=ALU.mult)

        tmpsc.close()
        K = dict(identf=identf, identb=identb, tri=tri, onesf=onesf, Win=Win, Wout=Wout, W1=W1, W2=W2, b1=b1,
                 nwT=nwT, cw=cw, cb=cb, dtb=dtb, Aneg=Aneg, dsk=dsk, snw=snw, stat=stat, junk=junk)
        if 'mixp' in cfg.stages:
            for sq in range(cfg.nseq_p):
                mixer_prompt(K, sq)
        if 'mixs' in cfg.stages and cfg.nseq_s > 0:
            sample_group(K)
        wsc.close()
        if 'mixs' in cfg.stages and cfg.nseq_s > 0:
            mixer_sample(K)
        mc.close()

    def ssd_chunk(K, T, L, BT, CT, xtm, Btm, dtv, av, zs, hTf, hTb, ymix_out):
        tri, onesf = K['tri'], K['onesf']
        psC, psCB, psE, psY, psYo, psSt = PSF[0], PSF[1], PSF[2:4], PSF[4], PSF[5], PSF[6]
        acsc, expacs, dend, cd = T['acsc'], T['expacs'], T['dend'], T['cd']
        aTri, CBm, segc, Eb, MTb, xdt, xdtd, Ys, Yt, tmp2 = (T[k] for k in
                                                             ('aTri', 'CBm', 'segc', 'Eb', 'MTb', 'xdt', 'xdtd', 'Ys',
                                                              'Yt', 'tmp2'))
        P.mm(psC[0:L, 0:8], tri[0:L, 0:L], av, start=True, stop=True)
        P.copy(acsc[0:L, :], psC[0:L, 0:8])
        P.act(expacs[0:L, :], acsc[0:L, :], AF.Exp)
        P.tt(aTri[0:L, :, 0:L], tri[0:L, 0:L].unsqueeze(1).broadcast_to([L, 8, L]),
             av.unsqueeze(2).broadcast_to([L, 8, L]), ALU.mult)
        for g in range(2):
            P.mm(psCB[0:L, g * 128:g * 128 + L], BT[g], CT[g], start=True, stop=True)
            P.tt(CBm[0:L, g, 0:L], psCB[0:L, g * 128:g * 128 + L], tri[0:L, 0:L], ALU.mult)
        P.tt(xdt[0:L, :].rearrange("p (h d) -> p h d", h=8), xtm.rearrange("p (h d) -> p h d", h=8),
             dtv.unsqueeze(2).broadcast_to([L, 8, 64]), ALU.mult)
        for hq in range(2):
            g = hq
            pe = psE[hq]
            pe3 = pe[0:L, 0:4 * L].rearrange("p (h l) -> p h l", h=4)
            P.mm(pe3, onesf[0:L, 0:L], aTri[0:L, 4 * hq:4 * hq + 4, 0:L], start=True, stop=True)
            E = Eb[hq]
            E3 = E[0:L, 0:4 * L].rearrange("p (h l) -> p h l", h=4)
            P.tt(E3, pe3, acsc[0:L, 4 * hq:4 * hq + 4].unsqueeze(2).broadcast_to([L, 4, L]), ALU.subtract)
            P.act(E3, E3, AF.Relu, scale=-1.0)
            P.act(E3, E3, AF.Exp, scale=-1.0)
            P.copy(dend[0:L, 4 * hq:4 * hq + 4], E3[:, :, L - 1], eng='pool')
            M = MTb[hq]
            M3 = M[0:L, 0:4 * L].rearrange("p (h l) -> p h l", h=4)
            P.tt(M3, E3, CBm[0:L, g, 0:L].unsqueeze(1).broadcast_to([L, 4, L]), ALU.mult)
            for hh in range(4):
                h = 4 * hq + hh
                P.mm(psY[0:L, h * 64:(h + 1) * 64], M3[:, hh, :], xdt[0:L, h * 64:(h + 1) * 64],
                     start=(h == 0), stop=True, skip_group_check=True)
        for g in range(2):
            P.mm(psYo[0:L, g * 256:(g + 1) * 256], CT[g], hTb[:, g * 256:(g + 1) * 256],
                 start=(g == 0), stop=True, skip_group_check=True)
        P.tt(Ys[0:L, :].rearrange("p (h d) -> p h d", h=8), psYo[0:L, :].rearrange("p (h d) -> p h d", h=8),
             expacs[0:L, :].unsqueeze(2).broadcast_to([L, 8, 64]), ALU.mult)
        P.tt(Yt[0:L, :], psY[0:L, :], Ys[0:L, :], ALU.add)
        P.tt(tmp2[0:L, :].rearrange("p (h d) -> p h d", h=8), xtm.rearrange("p (h d) -> p h d", h=8),
             K['dsk'][0:L, :].unsqueeze(2).broadcast_to([L, 8, 64]), ALU.mult)
        P.tt(Yt[0:L, :], Yt[0:L, :], tmp2[0:L, :], ALU.add)
        P.tt(Yt[0:L, :], Yt[0:L, :], zs, ALU.mult)
        st = K['stat']
        for g in range(2):
            r = rms_rstd(st, Yt[0:L, g * 256:(g + 1) * 256], L, 16 + 4 * g, K['junk'])
            P.ts(ymix_out[:, g * 256:(g + 1) * 256], Yt[0:L, g * 256:(g + 1) * 256], r, None, op0=ALU.mult)
        P.tt(xdtd[0:L, :].rearrange("p (h d) -> p h d", h=8), xdt[0:L, :].rearrange("p (h d) -> p h d", h=8),
             dend[0:L, :].unsqueeze(2).broadcast_to([L, 8, 64]), ALU.mult)
        for g in range(2):
            P.mm(psSt[:, g * 256:(g + 1) * 256], Btm[g], xdtd[0:L, g * 256:(g + 1) * 256],
                 start=(g == 0), stop=True, skip_group_check=True)
        P.mm(psC[:, 8:16], onesf[0:L, :], av, start=True, stop=True)
        P.act(T['cdall'][:, :], psC[:, 8:16], AF.Exp)
        P.tt(hTf[:, :].rearrange("p (h d) -> p h d", h=8), hTf[:, :].rearrange("p (h d) -> p h d", h=8),
             T['cdall'][:, :].unsqueeze(2).broadcast_to([128, 8, 64]), ALU.mult)
        P.tt(hTf[:, :], hTf[:, :], psSt[:, :], ALU.add)
        P.copy(hTb[:, :], hTf[:, :], eng='pool')

    def ssd_tmp(sc):
        T = {}
        for nm, shp, dt in (('acsc', [128, 8], F32), ('expacs', [128, 8], F32), ('dend', [128, 8], F32),
                            ('cd', [128, 8], F32), ('cdall', [128, 8], F32), ('aTri', [128, 8, 128], F32),
                            ('CBm', [128, 2, 128], F32), ('xdt', [128, 512], BF16), ('xdtd', [128, 512], BF16),
                            ('Ys', [128, 512], F32), ('Yt', [128, 512], F32), ('tmp2', [128, 512], F32)):
            T[nm] = sc.sb("ssd_" + nm, shp, dt)
        T['segc'] = None
        T['Eb'] = [sc.sb("ssd_Eb%d" % i, [128, 512], F32) for i in range(2)]
        T['MTb'] = [sc.sb("ssd_MT%d" % i, [128, 512], BF16) for i in range(2)]
        return T

    def mixer_prompt(K, sq):
        sc = Scope()
        identf, identb, Win, Wout = K['identf'], K['identb'], K['Win'], K['Wout']
        stat, junk = K['stat'], K['junk']
        base = sq * S
        ntile = S // TQ
        far_nz = [bool(np.any(hc['far'][:, u, :].astype(np.float32) != 0)) for u in range(4)]
        nsel_t = S // 128
        caus = sc.sb("caus", [128, NSUB, TQ], BF16)
        far = sc.sb("far", [128, 4, TQ], BF16)
        cmpm = sc.sb("cmpm", [128, NCM, TQ], BF16)
        ovl = sc.sb("ovl", [128, 2, 64], BF16)
        P.dma(caus[:], C['caus'])
        P.dma(far[:], C['far'])
        P.dma(cmpm[:], C['cmpm'])
        P.dma(ovl[:], C['ovl'])
        KTs = [sc.sb("KTs%d" % g, [128, S], BF16) for g in range(2)]
        KTw = sc.sb("KTw", [128, 2, 1024], BF16)
        kcT = sc.sb("kcT", [128, 2, 256], BF16)
        Vs = sc.sb("Vs", [128, nsel_t, 2, 65], BF16)
        Vw = sc.sb("Vw", [128, 8, 2, 65], BF16)
        vca = sc.sb("vca", [128, 2, 2, 65], BF16)
        Hh = [sc.sb("Hh%d" % i, [128, 2, 256], BF16) for i in range(2)]
        raw = [sc.sb("raw%d" % i, [128, 16 + TQ], BF16) for i in range(2)]
        QTa = sc.sb("QTa", [128, 2, 8, TQ], BF16)
        hT = sc.sb("hT", [128, 8, TQ], BF16)
        hb = sc.sb("hb", [128, D], BF16)
        xin = sc.sb("xio", [128, D], F32)
        xres = xin
        kvst = [sc.sb("kvst", [128, 768], F32)] * 2
        gate = sc.sb("gate", [128, NSUB, 24], F32)
        zs = sc.sb("zs", [128, NSUB, 512], BF16)
        dtv = sc.sb("dtv", [128, NSUB, 8], F32)
        av = sc.sb("av", [128, NSUB, 8], F32)
        att = sc.sb("att", [128, NSUB, 512], F32)
        attb = sc.sb("attb", [128, 512], BF16)
        ymix = sc.sb("ymix", [128, NSUB, 512], BF16)
        PT = [sc.sb("PT%d" % i, [128, 2 * TQ], BF16) for i in range(3)]
        rz = sc.sb("rz", [128, 8], F32)
        coef = sc.sb("coef", [128, 8], F32)
        imp = sc.sb("imp", [128, NSUB, 2, 64], F32)
        impf = sc.sb("impf", [128, 64], F32)
        impw = sc.sb("impw", [128, 64], F32)
        m8 = sc.sb("m8", [128, 16], F32)
        msk = sc.sb("msk", [128, 64], F32)
        SELB = sc.sb("SELB", [128, NSUB, 2, 2, 96], F32)
        va = sc.sb("va", [128, NSUB, 128], F32)
        xp = [sc.sb("xp%d" % i, [128, 3 + TQ], F32) for i in range(2)]
        accf = sc.sb("accf", [128, TQ], F32)
        hist = sc.sb("hist", [128, 8, 3], F32)
        convo = sc.sb("convo", [128, 8, TQ], BF16)
        xtm1 = sc.sb("xtm", [128, 512], BF16)
        Btm1 = sc.sb("Btm", [128, 2, 128], BF16)
        hTf = sc.sb("hTf", [128, 512], F32)
        hTb = sc.sb("hTb", [128, 512], BF16)
        ostg = xin
        T = ssd_tmp(sc)
        psM = PSF[0:2]
        psS = PSF[2:4]
        psO = PSF[4:6]
        psI = PSF[6]
        psT = PSB
        mcnt = [0]

        def nextM():
            mcnt[0] += 1
            return psM[mcnt[0] % 2]

        for g in range(2):
            P.dma(KTs[g][64:128, :], C['kc_sel'])
            P.memset(KTs[g][0:64, :], 0.0, eng='pool')
        P.memset(KTw[:], 0.0, eng='pool')
        P.memset(kcT[0:64, :, :], 0.0)
        for g in range(2):
            P.dma(kcT[64:128, g, :], C['kc_cmp'])
        P.memset(Vs[:], 0.0, eng='pool')
        P.memset(Vs[:, :, :, 64:65], 1.0, eng='pool')
        P.memset(Vw[:], 0.0, eng='pool')
        P.memset(Vw[:, :, :, 64:65], 1.0, eng='pool')
        P.memset(vca[:], 0.0, eng='pool')
        P.memset(vca[:, :, :, 64:65], 1.0, eng='pool')
        for i in range(2):
            P.memset(Hh[i][:], 0.0)
            P.memset(raw[i][:], 0.0)
        P.memset(QTa[:], 0.0, eng='pool')
        P.memset(SELB[:], 0.0)
        P.memset(hist[:], 0.0)
        P.memset(hTf[:], 0.0)
        P.memset(hTb[:], 0.0)

        for ti in range(ntile):
            q0 = ti * TQ
            useG1 = q0 >= 2048
            P.tag = 'pA'
            for s in range(NSUB):
                r0 = base + q0 + s * 128
                P.dma(xin[:], x1_d[r0:r0 + 128, :])
                r = rms_rstd(stat, xin[:], 128, 4 * s, junk)
                P.ts(hb[:], xin[:], r, None, op0=ALU.mult)
                for k in range(8):
                    P.tr(psT[:, k * 128:(k + 1) * 128], hb[:, k * 128:(k + 1) * 128], identb[:])
                for k in range(8):
                    if k % 2 == 0:
                        P.act(hT[:, k, s * 128:(s + 1) * 128], psT[:, k * 128:(k + 1) * 128], AF.Copy,
                              scale=K['nwT'][:, k:k + 1])
                    else:
                        P.ts(hT[:, k, s * 128:(s + 1) * 128], psT[:, k * 128:(k + 1) * 128], K['nwT'][:, k:k + 1],
                             None, op0=ALU.mult)

            def proj_fm(col0, ncol):
                ps = nextM()
                for k in range(8):
                    P.mm(ps[0:ncol, 0:TQ], Win[:, k, col0:col0 + ncol], hT[:, k, :], start=(k == 0), stop=(k == 7))
                return ps

            P.tag = 'pB'
            for h in range(8):
                ps = proj_fm(Q0 + h * 64, 64)
                P.act(QTa[0:64, 0, h, :], ps[0:64, 0:TQ], AF.Copy, scale=0.125)
                if useG1:
                    P.ts(QTa[0:64, 1, h, :], ps[0:64, 0:TQ], 0.125, None, op0=ALU.mult)
            wc = q0 % 1024
            for g in range(2):
                ps = proj_fm(KS + g * 64, 64)
                P.copy(KTs[g][0:64, q0:q0 + TQ], ps[0:64, 0:TQ], eng='act')
                ps = proj_fm(KW + g * 64, 64)
                P.copy(KTw[0:64, g, wc:wc + TQ], ps[0:64, 0:TQ])
                P.dma(KTw[64:128, g, wc:wc + TQ], C['kc_win'][:, q0:q0 + TQ])
            for i, c0 in enumerate((KC, VC)):
                ps = proj_fm(c0, 128)
                P.copy(raw[i][:, 16:16 + TQ], ps[:, 0:TQ], eng=('act' if i == 0 else 'dve'))
            for c in range(8):
                ps = proj_fm(XB + c * 128, 128)
                X = xp[c % 2]
                P.copy(X[:, 0:3], hist[:, c, :], eng='pool')
                P.copy(X[:, 3:3 + TQ], ps[:, 0:TQ], eng='act')
                P.ts(accf[:], X[:, 0:TQ], K['cw'][:, c, 0:1], K['cb'][:, c:c + 1], op0=ALU.mult, op1=ALU.add)
                for tap in range(1, 4):
                    P.stt(accf[:], X[:, tap:tap + TQ], K['cw'][:, c, tap:tap + 1], accf[:], ALU.mult, ALU.add)
                P.act(convo[:, c, :], accf[:], AF.Silu)
                P.copy(hist[:, c, :], X[:, TQ:TQ + 3], eng='pool')
            P.tag = 'pC'
            for s in range(NSUB):
                kv = kvst[s % 2]
                tok0 = base + q0 + s * 128
                ps = nextM()
                for k in range(8):
                    P.mm(ps[:, :], hT[:, k, s * 128:(s + 1) * 128], Win[:, k, 512:1024], start=(k == 0), stop=(k == 7))
                P.copy(kv[:, 0:512], ps[:, :], eng='act')
                ps = nextM()
                for k in range(8):
                    P.mm(ps[:, 0:280], hT[:, k, s * 128:(s + 1) * 128], Win[:, k, 1024:1304], start=(k == 0),
                         stop=(k == 7))
                P.copy(kv[:, 512:768], ps[:, 0:256])
                P.act(gate[:, s, :], ps[:, 256:280], AF.Sigmoid)
                P.dma(o_cmp_p[tok0:tok0 + 128, :], kv[:, 0:256])
                P.dma(o_sel_p[tok0:tok0 + 128, :], kv[:, 256:512])
                tl = q0 + s * 128
                if tl >= S - 512:
                    P.dma(o_win_p[sq, tl - (S - 512):tl - (S - 512) + 128, :], kv[:, 512:768])
                P.copy(Vs[:, tl // 128, :, 0:64], kv[:, 384:512].rearrange("p (g d) -> p g d", g=2), eng='pool')
                P.copy(Vw[:, (tl // 128) % 8, :, 0:64], kv[:, 640:768].rearrange("p (g d) -> p g d", g=2), eng='pool')
                ps = nextM()
                for k in range(8):
                    P.mm(ps[:, :], hT[:, k, s * 128:(s + 1) * 128], Win[:, k, Z0:Z0 + 512], start=(k == 0), stop=(k == 7))
                P.act(zs[:, s, :], ps[:, :], AF.Silu)
                ps = nextM()
                for k in range(8):
                    P.mm(ps[:, 0:8], hT[:, k, s * 128:(s + 1) * 128], Win[:, k, DTO:DTO + 8], start=(k == 0),
                         stop=(k == 7))
                P.tt(dtv[:, s, :], ps[:, 0:8], K['dtb'][:], ALU.add)
                P.act(dtv[:, s, :], dtv[:, s, :], AF.Exp)
                P.act(dtv[:, s, :], dtv[:, s, :], AF.Ln, bias=1.0)
                P.tt(av[:, s, :], dtv[:, s, :], K['Aneg'][:], ALU.mult)
            P.tag = 'pD'
            jlo = max(0, q0 // 16 - 1)
            jhi = (q0 + TQ) // 16 - 1
            nb = jhi - jlo
            cst = 16 * jlo - q0 + 16
            for i in range(2):
                for g in range(2):
                    ps = nextM()
                    for r in range(32):
                        P.mm(ps[:, 0:nb], K['W1'][i][g * 64:(g + 1) * 64, r, :],
                             raw[i][g * 64:(g + 1) * 64, cst + r:cst + r + 16 * (nb - 1) + 1:16],
                             start=(r == 0), stop=(r == 31))
                    P.act(Hh[i][:, g, jlo:jhi], ps[:, 0:nb], AF.Silu, bias=K['b1'][i][:, 0:1])
                P.copy(raw[i][:, 0:16], raw[i][:, TQ:TQ + 16], eng='pool')
            for g in range(2):
                ps = nextM()
                P.mm(ps[0:64, 0:nb], K['W2'][0][:, :], Hh[0][:, g, jlo:jhi], start=True, stop=True)
                P.copy(kcT[0:64, g, jlo:jhi], ps[0:64, 0:nb])
                for c in range(jlo // 128, (jhi - 1) // 128 + 1):
                    ps = nextM()
                    P.mm(ps[:, 0:64], Hh[1][:, g, c * 128:(c + 1) * 128], K['W2'][1][:, :], start=True, stop=True)
                    P.copy(vca[:, c, g, 0:64], ps[:, 0:64], eng='act')
            P.tag = 'pE'
            P.dma(QTa[96:128, 0, :, :], C['alq'][:, :, q0:q0 + TQ])
            if useG1:
                P.dma(QTa[96:128, 1, :, :], C['alq'][:, :, q0:q0 + TQ])
            P.dma(va[:], C['va'][q0:q0 + TQ, :].rearrange("(s p) c -> p s c", p=128))

            def attend(hp, branch, tiles, first_branch):
                po = psO[hp % 2]
                n_t = len(tiles)
                for idx in range(n_t + 1):
                    if idx < n_t:
                        kt, G, mk, vv, ov = tiles[idx]
                        sb_ = psS[idx % 2]
                        P.mm(sb_[:, 0:2 * TQ], kt, QTa[:, G, 2 * hp:2 * hp + 2, :], start=True, stop=(mk is None))
                        if mk is not None:
                            P.mm(sb_[:, 0:2 * TQ], identb[:], mk.unsqueeze(1).broadcast_to([128, 2, TQ]),
                                 start=False, stop=True)
                        P.act(PT[idx % 3][:], sb_[:, 0:2 * TQ], AF.Exp)
                    if idx >= 1:
                        j = idx - 1
                        kt, G, mk, vv, ov = tiles[j]
                        pt = PT[j % 3]
                        for hl in range(2):
                            for s in range(NSUB):
                                P.mm(po[:, hl * 65 * NSUB + s * 65:hl * 65 * NSUB + (s + 1) * 65],
                                     pt[:, hl * TQ + s * 128:hl * TQ + (s + 1) * 128], vv,
                                     start=(j == 0 and hl == 0 and s == 0), stop=(j == n_t - 1), skip_group_check=True)
                        if ov is not None:
                            for hl in range(2):
                                for s in range(NSUB):
                                    P.mm(psI[:, hl * 64 * NSUB + s * 64:hl * 64 * NSUB + (s + 1) * 64],
                                         pt[:, hl * TQ + s * 128:hl * TQ + (s + 1) * 128], ov,
                                         start=(j == 0 and hl == 0 and s == 0), stop=(j == n_t - 1),
                                         skip_group_check=True)
                for hl in range(2):
                    h = 2 * hp + hl
                    o0 = hl * 65 * NSUB
                    zc = po[:, o0 + 64:o0 + 64 + 65 * (NSUB - 1) + 1:65]
                    rzh = rz[:, hl * NSUB:(hl + 1) * NSUB]
                    cfh = coef[:, hl * NSUB:(hl + 1) * NSUB]
                    P.ts(rzh, zc, 1e-30, None, op0=ALU.max)
                    P.add('dve', lambda e, rzh=rzh: e.reciprocal(rzh, rzh), [rzh], [rzh])
                    P.tt(cfh, rzh, gate[:, :, branch * 8 + h], ALU.mult)
                    for s in range(NSUB):
                        dst = att[:, s, h * 64:(h + 1) * 64]
                        src = po[:, o0 + s * 65:o0 + s * 65 + 64]
                        if first_branch:
                            P.ts(dst, src, coef[:, hl * NSUB + s:hl * NSUB + s + 1], None, op0=ALU.mult)
                        else:
                            P.stt(dst, src, coef[:, hl * NSUB + s:hl * NSUB + s + 1], dst, ALU.mult, ALU.add)

            P.tag = 'pF'
            for hp in range(4):
                g = hp // 2
                tiles = []
                for c in range(2):
                    if c == 1 and q0 < 2048:
                        continue
                    rel = (q0 - 2048 * c) // TQ
                    mk = cmpm[:, rel, :] if rel < NCM else None
                    tiles.append((kcT[:, g, c * 128:(c + 1) * 128], 0, mk, vca[:, c, g, :], ovl[:, c, :]))
                attend(hp, 0, tiles, True)
                for hl in range(2):
                    for s in range(NSUB):
                        dst = imp[:, s, g, :]
                        src = psI[:, hl * 64 * NSUB + s * 64:hl * 64 * NSUB + (s + 1) * 64]
                        rzc = rz[:, hl * NSUB + s:hl * NSUB + s + 1]
                        if hp % 2 == 0 and hl == 0:
                            P.ts(dst, src, rzc, None, op0=ALU.mult)
                        else:
                            P.stt(dst, src, rzc, dst, ALU.mult, ALU.add)
            P.tag = 'pG1'
            for s in range(NSUB):
                for g in range(2):
                    P.tt(impw[:], imp[:, s, g, :], va[:, s, 0:64], ALU.mult)
                    P.tt(impf[:], impw[:], va[:, s, 64:128], ALU.add)
                    P.add('dve', lambda e: e.max(m8[:, 0:8], impf[:]), [impf[:]], [m8[:, 0:8]])
                    P.add('dve', lambda e: e.match_replace(impw[:], m8[:, 0:8], impf[:], -1e30),
                          [m8[:, 0:8], impf[:]], [impw[:]])
                    P.add('dve', lambda e: e.max(m8[:, 8:16], impw[:]), [impw[:]], [m8[:, 8:16]])
                    P.ts(msk[:], impf[:], m8[:, 15:16], None, op0=ALU.is_ge)
                    P.ts(SELB[:, s, g, :, 64:96], msk[:].rearrange("p (a b) -> p a b", a=2), -1.0, -NEGM,
                         op0=ALU.add, op1=ALU.mult)
            P.tag = 'pI'
            for hp in range(4):
                g = hp // 2
                tiles = []
                for i in range(4 + NSUB):
                    k0 = q0 - 512 + 128 * i
                    if k0 < 0:
                        continue
                    mk = far[:, (4 - i) - 1, :] if i < 4 else caus[:, i - 4, :]
                    if i < 4 and not far_nz[(4 - i) - 1]:
                        mk = None
                    col = k0 % 1024
                    tiles.append((KTw[:, g, col:col + 128], 0, mk, Vw[:, (k0 // 128) % 8, g, :], None))
                attend(hp, 2, tiles, False)
            P.tag = 'pG2'
            for s in range(NSUB):
                for g in range(2):
                    for G in range(2 if useG1 else 1):
                        pst = nextM()
                        P.tr(pst[0:96, 0:128], SELB[:, s, g, G, :], identf[:])
                        P.copy(QTa[64:96, G, g * 4:(g + 1) * 4, s * 128:(s + 1) * 128],
                               pst[64:96, 0:128].unsqueeze(1).broadcast_to([32, 4, 128]))
            P.tag = 'pH'
            for hp in range(4):
                g = hp // 2
                tiles = []
                for t in range(q0 // 128 + NSUB):
                    mk = caus[:, (t * 128 - q0) // 128, :] if t * 128 >= q0 else None
                    tiles.append((KTs[g][:, t * 128:(t + 1) * 128], t // 16, mk, Vs[:, t, g, :], None))
                attend(hp, 1, tiles, False)
            P.tag = 'pJ'
            for s in range(NSUB):
                for c in range(4):
                    P.tr(psT[:, c * 128:(c + 1) * 128], convo[:, c, s * 128:(s + 1) * 128], identb[:])
                for g in range(2):
                    P.tr(psT[:, 512 + g * 128:512 + (g + 1) * 128], convo[:, 4 + g, s * 128:(s + 1) * 128], identb[:])
                P.copy(xtm1[:], psT[:, 0:512], eng='act')
                P.copy(Btm1[:], psT[:, 512:768].rearrange("p (g n) -> p g n", g=2))
                ssd_chunk(K, T, 128,
                          [convo[:, 4 + g, s * 128:(s + 1) * 128] for g in range(2)],
                          [convo[:, 6 + g, s * 128:(s + 1) * 128] for g in range(2)],
                          xtm1[:], [Btm1[:, g, :] for g in range(2)], dtv[:, s, :], av[:, s, :], zs[:, s, :],
                          hTf, hTb, ymix[:, s, :])
            P.tag = 'pK'
            mixT = hT
            for s in range(NSUB):
                P.copy(attb[:], att[:, s, :], eng='pool')
                for c in range(4):
                    P.tr(psT[:, c * 128:(c + 1) * 128], attb[:, c * 128:(c + 1) * 128], identb[:])
                for c in range(4):
                    P.tr(psT[:, 512 + c * 128:512 + (c + 1) * 128], ymix[:, s, c * 128:(c + 1) * 128], identb[:])
                for c in range(4):
                    P.copy(mixT[:, c, s * 128:(s + 1) * 128], psT[:, c * 128:(c + 1) * 128], eng='act')
                for c in range(4):
                    P.ts(mixT[:, 4 + c, s * 128:(s + 1) * 128], psT[:, 512 + c * 128:512 + (c + 1) * 128],
                         K['snw'][:, c:c + 1], None, op0=ALU.mult)
            for s in range(NSUB):
                r0 = base + q0 + s * 128
                P.dma(xres[:], x1_d[r0:r0 + 128, :])
                for h2 in range(2):
                    ps = nextM()
                    for k in range(8):
                        P.mm(ps[:, :], mixT[:, k, s * 128:(s + 1) * 128], Wout[:, k, h2 * 512:(h2 + 1) * 512],
                             start=(k == 0), stop=(k == 7))
                    P.tt(xres[:, h2 * 512:(h2 + 1) * 512], ps[:, :], xres[:, h2 * 512:(h2 + 1) * 512], ALU.add)
                P.dma(x2_d[r0:r0 + 128, :], xres[:])
                if cfg.dbg:
                    P.dma(dbg['att'][r0:r0 + 128, :], att[:, s, :])
                    P.dma(dbg['x2'][r0:r0 + 128, :], xres[:])
        pst = PSF[0]
        for c in range(8):
            P.tr(pst[0:3, c * 128:(c + 1) * 128] if False else PSF[c % 2][0:3, 0:128], hist[:, c, :], identf[:])
            P.copy(ostg[0:3, c * 128:(c + 1) * 128], PSF[c % 2][0:3, 0:128])
        P.dma(o_conv_p[sq, :, :], ostg[0:3, :])
        for c in range(4):
            ps = PSF[c % 2]
            P.tr(ps[:, 0:128], hTf[:, c * 128:(c + 1) * 128], identf[:])
            P.copy(ostg[:, c * 128:(c + 1) * 128], ps[:, 0:128])
            P.dma(o_ssm_p[sq, c * 128:(c + 1) * 128, :], ostg[:, c * 128:(c + 1) * 128])
        sc.close()

    def sample_group(K):
        P.tag = 's_group'
        sc = Scope()
        identf, identb, Win = K['identf'], K['identb'], K['Win']
        stat, junk = K['stat'], K['junk']
        NB = cfg.nseq_s
        hTs = sc.sb("g_hTs", [128, 8, NS], BF16)
        hb = sc.sb("g_hb", [128, D], BF16)
        xin = sc.sb("g_xin", [128, D], F32)
        qTs = sc.sb("g_qTs", [128, 8, NS], BF16)
        ksTs = sc.sb("g_ksTs", [128, 2, NS], BF16)
        kwTs = sc.sb("g_kwTs", [128, 2, NS], BF16)
        xbcT = sc.sb("g_xbcT", [128, 8, NS], F32)
        histT = sc.sb("g_histT", [128, 8, NB * 3], F32)
        ust = [sc.sb("g_ust%d" % i, [128, 512], F32) for i in range(2)]
        psM = PSF[0:2]
        psT = PSB
        mcnt = [0]

        def nextM():
            mcnt[0] += 1
            return psM[mcnt[0] % 2]

        for t0 in range(0, NS, 128):
            L = min(128, NS - t0)
            P.dma(xin[0:L, :], x1_d[NTP + t0:NTP + t0 + L, :])
            r = rms_rstd(stat, xin[0:L, :], L, 0, junk)
            P.ts(hb[0:L, :], xin[0:L, :], r, None, op0=ALU.mult)
            for k in range(8):
                P.tr(psT[:, k * 128:k * 128 + L], hb[0:L, k * 128:(k + 1) * 128], identb[0:L, 0:L])
            for k in range(8):
                P.ts(hTs[:, k, t0:t0 + L], psT[:, k * 128:k * 128 + L], K['nwT'][:, k:k + 1], None, op0=ALU.mult)

        def proj_fm(col0, ncol):
            ps = nextM()
            for k in range(8):
                P.mm(ps[0:ncol, 0:NS], Win[:, k, col0:col0 + ncol], hTs[:, k, :], start=(k == 0), stop=(k == 7))
            return ps

        for h in range(8):
            ps = proj_fm(Q0 + h * 64, 64)
            P.act(qTs[0:64, h, :], ps[0:64, 0:NS], AF.Copy, scale=0.125)
        for g in range(2):
            ps = proj_fm(KS + g * 64, 64)
            P.copy(ksTs[0:64, g, :], ps[0:64, 0:NS])
            ps = proj_fm(KW + g * 64, 64)
            P.copy(kwTs[0:64, g, :], ps[0:64, 0:NS])
        for c in range(8):
            ps = proj_fm(XB + c * 128, 128)
            P.copy(xbcT[:, c, :], ps[:, 0:NS], eng='act')
        for r0 in range(0, NB * 3, 96):
            L = min(96, NB * 3 - r0)
            P.dma(xin[0:L, :], state_conv[r0:r0 + L, :])
            for c in range(8):
                ps = nextM()
                P.tr(ps[:, 0:L], xin[0:L, c * 128:(c + 1) * 128], identf[0:L, 0:L])
                P.copy(histT[:, c, r0:r0 + L], ps[:, 0:L])

        P.dma(q_scr, qTs[0:64, :, :])
        P.dma(ks_scr, ksTs[0:64, :, :])
        P.dma(kw_scr, kwTs[0:64, :, :])
        P.dma(xb_scr, xbcT[:])
        P.dma(hs_scr, histT[:])
        gi = 0
        for t0 in range(0, NS, 128):
            L = min(128, NS - t0)
            for (c0, cwid, d0) in ((512, 512, 0), (1024, 280, 512), (Z0, 512, 792), (DTO, 8, 1304),
                                   (XB, 512, 1312), (XB + 512, 512, 1824)):
                ps = nextM()
                for k in range(8):
                    P.mm(ps[0:L, 0:cwid], hTs[:, k, t0:t0 + L], Win[:, k, c0:c0 + cwid], start=(k == 0), stop=(k == 7))
                st_ = ust[gi % 2]
                P.copy(st_[0:L, 0:cwid], ps[0:L, 0:cwid], eng=('act' if gi % 2 else 'dve'))
                gi += 1
                P.dma(u_scr[t0:t0 + L, d0:d0 + cwid], st_[0:L, 0:cwid])
        sc.close()

    def mixer_sample(K):
        P.tag = 's_init'
        sc = Scope()
        identf, identb, Win, Wout = K['identf'], K['identb'], K['Win'], K['Wout']
        stat, junk = K['stat'], K['junk']
        NB = cfg.nseq_s
        NPG = cfg.npages
        NKT = cfg.nkt_s
        NCH = (cfg.ncmp_s + 127) // 128
        NCB = cfg.ncmp_s
        NSL = cfg.nsel_s
        NG = (NSL + 31) // 32
        caus8 = sc.sb("caus8", [128, 32], BF16)
        far8 = sc.sb("far8", [128, 32], BF16)
        ovls = sc.sb("ovls", [128, NCH, NSL], BF16)
        vas = sc.sb("vas", [NQS, 2, NSL], F32)
        iot = sc.sb("iot", [128, NPG], F32)
        pidxf = sc.sb("pidxf", [128, NPG], F32)
        P.dma(caus8[:], C['caus8'])
        P.dma(far8[:], C['far8'])
        cmpms = sc.sb("cmpms", [128, 32], BF16)
        P.dma(cmpms[:], C['cmpm_s'])
        P.dma(ovls[:], C['ovl_s'])
        P.dma(vas[:], C['va_s'])
        P.dma(iot[:], iota_p)
        xin = sc.sb("xin_s", [128, 512], F32)
        qTs = sc.sb("qTs", [128, 8, NS], BF16)
        ksTs = sc.sb("ksTs", [128, 2, NS], BF16)
        kwTs = sc.sb("kwTs", [128, 2, NS], BF16)
        xbcT = sc.sb("xbcT", [128, 8, NS], F32)
        histT = sc.sb("histT", [128, 8, NB * 3], F32)
        xps = sc.sb("xps", [128, 8, 11], F32)
        accs = sc.sb("accs", [128, NQS], F32)
        convs = sc.sb("convs", [128, 8, NQS], BF16)
        ub = sc.sb("ub", [NQS, 1312], F32)
        pidx = sc.sb("pidx", [128, NPG], I32)
        pidxu = [sc.sb("pidxu%d" % i, [128, NPG], U32) for i in range(2)]
        raws = sc.sb("raws", [128, 2, NPG * 128], BF16)
        NPGB = 8
        PG = [sc.sb("PG%d" % i, [128, 256], F32) for i in range(NPGB)]
        Hs = [sc.sb("Hs%d" % i, [128, 2, NCH * 128], BF16) for i in range(2)]
        kcTs = sc.sb("kcTs", [128, 2, NCH * 128], BF16)
        vcs = sc.sb("vcs", [128, NCH, 2, 65], BF16)
        KTs = sc.sb("KTss", [128, 2, NKT * 128], BF16)
        Vs = sc.sb("Vss", [128, NKT, 2, 65], BF16)
        KTw = sc.sb("KTws", [128, 2, 640], BF16)
        Vw = sc.sb("Vws", [128, 5, 2, 65], BF16)
        QTa = sc.sb("QTas", [128, NG, 2, 32], BF16)
        PT = [sc.sb("PTs%d" % i, [128, 32], BF16) for i in range(3)]
        gate = sc.sb("gate_s", [NQS, 24], F32)
        zs = sc.sb("zs_s", [NQS, 512], BF16)
        dtv = sc.sb("dtv_s", [NQS, 8], F32)
        av = sc.sb("av_s", [NQS, 8], F32)
        att = sc.sb("att_s", [NQS, 512], F32)
        attb = sc.sb("attb_s", [NQS, 512], BF16)
        ymix = sc.sb("ymix_s", [NQS, 512], BF16)
        rz = sc.sb("rz_s", [NQS, 8], F32)
        coef = sc.sb("coef_s", [NQS, 8], F32)
        imp = sc.sb("imp_s", [NQS, 2, NSL], F32)
        NSLP = NG * 32
        impf = sc.sb("impf_s", [NQS, NSL], F32)
        impw = sc.sb("impw_s", [NQS, NSL], F32)
        m8 = sc.sb("m8_s", [NQS, 16], F32)
        msk = sc.sb("msk_s", [NQS, NSLP], F32)
        SELB = sc.sb("SELB_s", [NQS, NG, 96], F32)
        xtm = sc.sb("xtm_s", [NQS, 512], BF16)
        Btm = sc.sb("Btm_s", [NQS, 2, 128], BF16)
        hTf = sc.sb("hTf_s", [128, 512], F32)
        hTb = sc.sb("hTb_s", [128, 512], BF16)
        mixT = sc.sb("mixT_s", [128, 8, NQS], BF16)
        xres = sc.sb("xres_s", [NQS, D], F32)
        ostg = xin
        T = ssd_tmp(sc)
        psM = PSF[0:2]
        psS = PSF[2:4]
        psO = PSF[4]
        psI = PSF[5:7]
        psT = PSB
        mcnt = [0]

        def nextM():
            mcnt[0] += 1
            return psM[mcnt[0] % 2]

        P.dma(qTs[0:64, :, :], q_scr)
        P.dma(ksTs[0:64, :, :], ks_scr)
        P.dma(kwTs[0:64, :, :], kw_scr)
        P.dma(xbcT[:], xb_scr)
        P.dma(histT[:], hs_scr)

        P.memset(KTs[0:64, :, :], 0.0, eng='pool')
        P.memset(KTw[0:64, :, :], 0.0, eng='pool')
        for g in range(2):
            P.dma(KTs[64:128, g, :], C['kc_sel_s'])
            P.dma(KTw[64:128, g, :], C['kc_win_s'])
            P.dma(kcTs[64:128, g, :], C['kc_cmp_s'])
        P.memset(kcTs[0:64, :, :], 0.0)
        P.memset(Vs[:], 0.0, eng='pool')
        P.memset(Vs[:, 0:NPG, :, 64:65], 1.0, eng='pool')
        P.memset(Vs[0:NQS, NPG, :, 64:65], 1.0, eng='pool')
        P.memset(Vw[:], 0.0, eng='pool')
        P.memset(Vw[:, 0:4, :, 64:65], 1.0, eng='pool')
        P.memset(Vw[0:NQS, 4, :, 64:65], 1.0, eng='pool')
        P.memset(vcs[:], 0.0, eng='pool')
        P.memset(vcs[:, :, :, 64:65], 1.0, eng='pool')
        for i in range(2):
            P.memset(Hs[i][:], 0.0)
        P.memset(QTa[:], 0.0)
        for G in range(NG):
            P.dma(QTa[96:128, G, :, :], C['alq_s'])
        P.memset(SELB[:], 0.0)
        P.memset(msk[:], 0.0)

        def gather_page(dst, cache, idx):
            P.add('pool', lambda e: e.indirect_dma_start(out=dst, out_offset=None, in_=cache,
                                                          in_offset=bass.IndirectOffsetOnAxis(ap=idx, axis=0)),
                  [cache, idx], [dst], is_dma=True)

        def attend_s(g, tiles, branch, first_branch, with_imp, bg=None):
            n_t = len(tiles)
            for idx in range(n_t + 1):
                adv(bg, 1)
                if idx < n_t:
                    kt, rq, mk, vv, ov = tiles[idx]
                    sb_ = psS[idx % 2]
                    P.mm(sb_[:, 0:32], kt, rq, start=True, stop=(mk is None))
                    if mk is not None:
                        P.mm(sb_[:, 0:32], identb[:], mk, start=False, stop=True)
                    P.act(PT[idx % 3][:], sb_[:, 0:32], AF.Exp)
                if idx >= 1:
                    j = idx - 1
                    kt, rq, mk, vv, ov = tiles[j]
                    pt = PT[j % 3]
                    for hh in range(4):
                        P.mm(psO[0:NQS, hh * 65:(hh + 1) * 65], pt[:, hh * NQS:(hh + 1) * NQS], vv,
                             start=(j == 0 and hh == 0), stop=(j == n_t - 1), skip_group_check=True)
                    if with_imp:
                        for hh in range(4):
                            P.mm(psI[hh // 2][0:NQS, (hh % 2) * NSL:(hh % 2 + 1) * NSL], pt[:, hh * NQS:(hh + 1) * NQS], ov,
                                 start=(j == 0 and hh % 2 == 0), stop=(j == n_t - 1), skip_group_check=True)
            zc = psO[0:NQS, 64:64 + 65 * 3 + 1:65]
            P.ts(rz[:, 0:4], zc, 1e-30, None, op0=ALU.max)
            P.add('dve', lambda e: e.reciprocal(rz[:, 0:4], rz[:, 0:4]), [rz[:, 0:4]], [rz[:, 0:4]])
            P.tt(coef[:, 0:4], rz[:, 0:4], gate[:, branch * 8 + g * 4:branch * 8 + g * 4 + 4], ALU.mult)
            for hh in range(4):
                h = g * 4 + hh
                dst = att[:, h * 64:(h + 1) * 64]
                if first_branch:
                    P.ts(dst, psO[0:NQS, hh * 65:hh * 65 + 64], coef[:, hh:hh + 1], None, op0=ALU.mult)
                else:
                    P.stt(dst, psO[0:NQS, hh * 65:hh * 65 + 64], coef[:, hh:hh + 1], dst, ALU.mult, ALU.add)
            if with_imp:
                for hh in range(4):
                    src = psI[hh // 2][0:NQS, (hh % 2) * NSL:(hh % 2 + 1) * NSL]
                    if hh == 0:
                        P.ts(imp[:, g, :], src, rz[:, hh:hh + 1], None, op0=ALU.mult)
                    else:
                        P.stt(imp[:, g, :], src, rz[:, hh:hh + 1], imp[:, g, :], ALU.mult, ALU.add)

        pgc = [0]

        def adv(gen, n):
            if gen is None:
                return
            for _ in range(n):
                if next(gen, 'end') == 'end':
                    return

        def prep_idx(bb):
            P.dma(pidx[:], page_table[bb, :].partition_broadcast(128))
            P.ts(pidxf[:], pidx[:], 128.0, None, op0=ALU.mult)
            P.tt(pidxu[bb % 2][:], pidxf[:], iot[:], ALU.add)

        def gen_cmp(bb):
            for p in range(NPG):
                pg = PG[pgc[0] % NPGB]
                pgc[0] += 1
                gather_page(pg[:], cache_cmp, pidxu[bb % 2][:, p:p + 1])
                for i in range(2):
                    ps = nextM()
                    P.tr(ps[:, 0:128], pg[:, i * 128:(i + 1) * 128], identf[:])
                    P.copy(raws[:, i, :].rearrange("p (r c) -> p r c", r=16)[:, :, 8 * p:8 * p + 8],
                           ps[:, 0:128].rearrange("p (c r) -> p r c", r=16), eng=('act' if i == 0 else 'dve'))
                yield 1

        def gen_sel(bb):
            for p in range(NPG):
                pg = PG[pgc[0] % NPGB]
                pgc[0] += 1
                gather_page(pg[:], cache_sel, pidxu[bb % 2][:, p:p + 1])
                for g in range(2):
                    ps = nextM()
                    P.tr(ps[0:64, 0:128], pg[:, g * 64:(g + 1) * 64], identf[:])
                    P.copy(KTs[0:64, g, p * 128:(p + 1) * 128], ps[0:64, 0:128], eng=('act' if g == 0 else 'dve'))
                P.copy(Vs[:, p, :, 0:64], pg[:, 128:256].rearrange("p (g d) -> p g d", g=2), eng='act')
                yield 1

        prep_idx(0)
        gcur = gen_cmp(0)
        gsel = gen_sel(0)
        for b in range(NB):
            tb = b * NQS
            P.tag = 's_proj'
            P.dma(ub[:], u_scr[tb:tb + NQS, 0:1312])
            P.dma(xres[:], u_scr[tb:tb + NQS, 1312:2336])
            P.dma(o_cmp_s[tb:tb + NQS, :], ub[:, 0:256])
            P.dma(o_sel_s[tb:tb + NQS, :], ub[:, 256:512])
            P.dma(o_win_s[b, 512 - NQS:512, :], ub[:, 512:768])
            P.dma(o_win_s[b, 0:512 - NQS, :], cache_win[b, NQS:512, :])
            P.dma(o_conv_s[b, :, :], xres[NQS - 3:NQS, :])
            P.act(gate[:], ub[:, 768:792], AF.Sigmoid)
            P.act(zs[:], ub[:, 792:1304], AF.Silu)
            P.tt(dtv[:], ub[:, 1304:1312], K['dtb'][0:NQS, :], ALU.add)
            P.copy(xps[:, :, 0:3], histT[:, :, b * 3:(b + 1) * 3], eng='pool')
            P.copy(xps[:, :, 3:11], xbcT[:, :, tb:tb + NQS], eng='pool')
            for c in range(8):
                P.ts(accs[:], xps[:, c, 0:NQS], K['cw'][:, c, 0:1], K['cb'][:, c:c + 1], op0=ALU.mult, op1=ALU.add)
                for tap in range(1, 4):
                    P.stt(accs[:], xps[:, c, tap:tap + NQS], K['cw'][:, c, tap:tap + 1], accs[:], ALU.mult, ALU.add)
                P.act(convs[:, c, :], accs[:], AF.Silu)
            P.act(dtv[:], dtv[:], AF.Exp)
            P.act(dtv[:], dtv[:], AF.Ln, bias=1.0)
            P.tt(av[:], dtv[:], K['Aneg'][0:NQS, :], ALU.mult)
            if b + 1 < NB:
                prep_idx(b + 1)
            for g in range(2):
                P.copy(QTa[0:64, :, g, :].rearrange("p a (h t) -> p a h t", h=4),
                       qTs[0:64, g * 4:(g + 1) * 4, tb:tb + NQS].unsqueeze(1).broadcast_to([64, NG, 4, NQS]))
            P.tag = 's_cmp'
            adv(gcur, NPG)
            for i in range(2):
                for g in range(2):
                    for j0 in range(0, NCB, 512):
                        nb = min(512, NCB - j0)
                        ps = nextM()
                        for r in range(32):
                            st_ = (r % 16) * (NPG * 8) + j0 + r // 16
                            P.mm(ps[:, 0:nb], K['W1'][i][g * 64:(g + 1) * 64, r, :],
                                 raws[g * 64:(g + 1) * 64, i, st_:st_ + nb],
                                 start=(r == 0), stop=(r == 31))
                        P.act(Hs[i][:, g, j0:j0 + nb], ps[:, 0:nb], AF.Silu, bias=K['b1'][i][:, 0:1])
                        adv(gsel, 6)
            for g in range(2):
                for j0 in range(0, NCB, 512):
                    nb = min(512, NCB - j0)
                    ps = nextM()
                    P.mm(ps[0:64, 0:nb], K['W2'][0][:, :], Hs[0][:, g, j0:j0 + nb], start=True, stop=True)
                    P.copy(kcTs[0:64, g, j0:j0 + nb], ps[0:64, 0:nb])
                for c in range(NCH):
                    ps = nextM()
                    P.mm(ps[:, 0:64], Hs[1][:, g, c * 128:(c + 1) * 128], K['W2'][1][:, :], start=True, stop=True)
                    P.copy(vcs[:, c, g, 0:64], ps[:, 0:64], eng='act')
            for g in range(2):
                tiles = [(kcTs[:, g, c * 128:(c + 1) * 128], QTa[:, 0, g, :], (cmpms[:] if c == NCH - 1 else None),
                          vcs[:, c, g, :], ovls[:, c, :]) for c in range(NCH)]
                attend_s(g, tiles, 0, True, True)
            P.tag = 's_topk'
            for g in range(2):
                P.tt(impw[:], imp[:, g, :], vas[:, 0, :], ALU.mult)
                P.tt(impf[:], impw[:], vas[:, 1, :], ALU.add)
                P.add('dve', lambda e: e.max(m8[:, 0:8], impf[:]), [impf[:]], [m8[:, 0:8]])
                P.add('dve', lambda e: e.match_replace(impw[:], m8[:, 0:8], impf[:], -1e30),
                      [m8[:, 0:8], impf[:]], [impw[:]])
                P.add('dve', lambda e: e.max(m8[:, 8:16], impw[:]), [impw[:]], [m8[:, 8:16]])
                P.ts(msk[:, 0:NSL], impf[:], m8[:, 15:16], None, op0=ALU.is_ge)
                P.ts(SELB[:, :, 64:96], msk[:].rearrange("p (a b) -> p a b", a=NG), -1.0, -NEGM,
                     op0=ALU.add, op1=ALU.mult)
                for G in range(NG):
                    pst = nextM()
                    P.tr(pst[0:96, 0:NQS], SELB[:, G, :], identf[0:NQS, 0:NQS])
                    P.copy(QTa[64:96, G, g, :].rearrange("p (h t) -> p h t", h=4),
                           pst[64:96, 0:NQS].unsqueeze(1).broadcast_to([32, 4, NQS]))
            P.tag = 's_sel'
            adv(gsel, NPG)
            gcur = gen_cmp(b + 1) if b + 1 < NB else None
            for g in range(2):
                P.copy(KTs[0:64, g, NPG * 128:NPG * 128 + NQS], ksTs[0:64, g, tb:tb + NQS])
            P.copy(Vs[0:NQS, NPG, :, 0:64], ub[:, 384:512].rearrange("p (g d) -> p g d", g=2))
            for g in range(2):
                tiles = []
                for t in range(NKT):
                    mk = caus8[:] if t == NPG else None
                    tiles.append((KTs[:, g, t * 128:(t + 1) * 128], QTa[:, t // 16, g, :], mk, Vs[:, t, g, :], None))
                attend_s(g, tiles, 1, False, False, bg=gcur)
            P.tag = 's_win'
            adv(gcur, NPG)
            gsel = gen_sel(b + 1) if b + 1 < NB else None
            for t in range(4):
                pg = PG[pgc[0] % NPGB]
                pgc[0] += 1
                P.dma(pg[:], cache_win[b, t * 128:(t + 1) * 128, :])
                for g in range(2):
                    ps = nextM()
                    P.tr(ps[0:64, 0:128], pg[:, g * 64:(g + 1) * 64], identf[:])
                    P.copy(KTw[0:64, g, t * 128:(t + 1) * 128], ps[0:64, 0:128], eng=('act' if g == 0 else 'dve'))
                P.copy(Vw[:, t, :, 0:64], pg[:, 128:256].rearrange("p (g d) -> p g d", g=2), eng='act')
            for g in range(2):
                P.copy(KTw[0:64, g, 512:512 + NQS], kwTs[0:64, g, tb:tb + NQS])
            P.copy(Vw[0:NQS, 4, :, 0:64], ub[:, 640:768].rearrange("p (g d) -> p g d", g=2))
            for g in range(2):
                tiles = []
                for t in range(5):
                    mk = far8[:] if t == 0 else (caus8[:] if t == 4 else None)
                    tiles.append((KTw[:, g, t * 128:(t + 1) * 128], QTa[:, 0, g, :], mk, Vw[:, t, g, :], None))
                attend_s(g, tiles, 2, False, False, bg=gsel)
            P.tag = 's_ssd'
            for c in range(4):
                ps = nextM()
                P.dma(ostg[:, 0:128], state_ssm[b, c * 128:(c + 1) * 128, :])
                P.tr(ps[:, 0:128], ostg[:, 0:128], identf[:])
                P.copy(hTf[:, c * 128:(c + 1) * 128], ps[:, 0:128])
            P.copy(hTb[:], hTf[:], eng='pool')
            adv(gsel, 8)
            for c in range(4):
                P.tr(psT[0:NQS, c * 128:(c + 1) * 128], convs[:, c, :], identb[:])
            for g in range(2):
                P.tr(psT[0:NQS, 512 + g * 128:512 + (g + 1) * 128], convs[:, 4 + g, :], identb[:])
            P.copy(xtm[:], psT[0:NQS, 0:512], eng='act')
            P.copy(Btm[:], psT[0:NQS, 512:768].rearrange("p (g n) -> p g n", g=2))
            ssd_chunk(K, T, NQS,
                      [convs[:, 4 + g, :] for g in range(2)],
                      [convs[:, 6 + g, :] for g in range(2)],
                      xtm[:], [Btm[:, g, :] for g in range(2)], dtv[:], av[:], zs[:], hTf, hTb, ymix[:])
            for c in range(4):
                ps = nextM()
                P.tr(ps[:, 0:128], hTf[:, c * 128:(c + 1) * 128], identf[:])
                P.copy(ostg[:, c * 128:(c + 1) * 128], ps[:, 0:128])
                P.dma(o_ssm_s[b, c * 128:(c + 1) * 128, :], ostg[:, c * 128:(c + 1) * 128])
            P.tag = 's_mix'
            adv(gsel, 8)
            P.copy(attb[:], att[:], eng='pool')
            for c in range(4):
                P.tr(psT[:, c * 128:c * 128 + NQS], attb[:, c * 128:(c + 1) * 128], identb[0:NQS, 0:NQS])
            for c in range(4):
                P.tr(psT[:, 512 + c * 128:512 + c * 128 + NQS], ymix[:, c * 128:(c + 1) * 128], identb[0:NQS, 0:NQS])
            for c in range(4):
                P.copy(mixT[:, c, :], psT[:, c * 128:c * 128 + NQS], eng='act')
                P.ts(mixT[:, 4 + c, :], psT[:, 512 + c * 128:512 + c * 128 + NQS], K['snw'][:, c:c + 1], None,
                     op0=ALU.mult)
            P.dma(xres[:], x1_d[NTP + tb:NTP + tb + NQS, :])
            for h2 in range(2):
                ps = nextM()
                for k in range(8):
                    P.mm(ps[0:NQS, :], mixT[:, k, :], Wout[:, k, h2 * 512:(h2 + 1) * 512], start=(k == 0), stop=(k == 7))
                P.tt(xres[:, h2 * 512:(h2 + 1) * 512], ps[0:NQS, :], xres[:, h2 * 512:(h2 + 1) * 512], ALU.add)
            P.dma(x2_d[NTP + tb:NTP + tb + NQS, :], xres[:])
            if cfg.dbg:
                P.dma(dbg['att'][NTP + tb:NTP + tb + NQS, :], att[:])
                P.dma(dbg['x2'][NTP + tb:NTP + tb + NQS, :], xres[:])
        sc.close()

    cur = x_in
    if 'ffn1' in cfg.stages:
        ffn_phase("ffn1", x_in, x1_d, False)
    if 'mixp' in cfg.stages or 'mixs' in cfg.stages:
        mixer_phase()
    if 'mixs' not in cfg.stages and NS > 0:
        P.dma(x2_d[NTP:NT, :], x1_d[NTP:NT, :])
    if 'mixp' not in cfg.stages and NTP > 0:
        P.dma(x2_d[0:NTP, :], x1_d[0:NTP, :])
    if 'ffn2' in cfg.stages:
        ffn_phase("ffn2", x2_d, y_out, True)
    P.finalize(top)
    top.close()
    return nc, P, hc


WNAMES = ("ffn1_wg", "ffn1_wu", "ffn1_wd", "ffn2_wg", "ffn2_wu", "ffn2_wd", "norm_ffn1", "norm_ffn2", "norm_mix",
          "w_in", "w_out", "cmp_pe_k", "cmp_w1_k", "cmp_w2_k", "cmp_pe_v", "cmp_w1_v", "cmp_w2_v", "conv_w",
          "conv_b", "dt_bias", "a_log", "d_skip", "ssm_norm")


def make_in_maps(cfg, hc, inp, ncores):
    nsp, nss = cfg.nseq_p, cfg.nseq_s
    base = {}
    for n in WNAMES:
        base[n] = np.ascontiguousarray(inp[n][0], dtype=np.float32)
    base['norm_final'] = np.ascontiguousarray(inp['norm_final'], dtype=np.float32)
    for k, v in hc.items():
        base['c_' + k] = v
    base['cache_cmp'] = np.ascontiguousarray(inp['cache_cmp'][0]).reshape(-1, 256)
    base['cache_sel'] = np.ascontiguousarray(inp['cache_sel'][0]).reshape(-1, 256)
    base['iota_p'] = np.ascontiguousarray(np.tile(np.arange(128, dtype=np.float32)[:, None], (1, cfg.npages)))
    maps = []
    for c in range(ncores):
        m = dict(base)
        xp = inp['x_prompt'][c * nsp:(c + 1) * nsp].reshape(-1, D)
        xs = inp['x_sample'][c * nss:(c + 1) * nss].reshape(-1, D)
        m['x_in'] = np.ascontiguousarray(np.concatenate([xp, xs], axis=0))
        m['cache_win'] = np.ascontiguousarray(inp['cache_win'][0, c * nss:(c + 1) * nss]).reshape(nss, 512, 256)
        m['state_conv'] = np.ascontiguousarray(inp['state_conv'][0, c * nss:(c + 1) * nss]).reshape(nss * 3, 1024)
        m['state_ssm'] = np.ascontiguousarray(inp['state_ssm'][0, c * nss:(c + 1) * nss]).reshape(nss, 512, 128)
        m['page_table'] = np.ascontiguousarray(inp['page_table'][c * nss:(c + 1) * nss]).astype(np.int32)
        maps.append(m)
    return maps


def assemble(cfg, results, ncores):
    nsp, nss, S = cfg.nseq_p, cfg.nseq_s, cfg.seq
    cat = lambda name: [np.asarray(r[name]) for r in results]
    y = cat('y_out')
    y_p = np.concatenate([a[:cfg.ntok_p].reshape(nsp, S, D) for a in y], 0)
    y_s = np.concatenate([a[cfg.ntok_p:].reshape(nss, NQS, D) for a in y], 0)
    ncp = np.concatenate([a.reshape(nsp, S, 2, 2, 64) for a in cat('o_cmp_p')], 0)[None]
    nsl = np.concatenate([a.reshape(nsp, S, 2, 2, 64) for a in cat('o_sel_p')], 0)[None]
    nwp = np.concatenate([a.reshape(nsp, 512, 2, 2, 64) for a in cat('o_win_p')], 0)[None]
    ncv = np.concatenate([a.reshape(nsp, 3, 1024) for a in cat('o_conv_p')], 0)[None]
    nsm = np.concatenate([a.reshape(nsp, 8, 64, 128) for a in cat('o_ssm_p')], 0)[None]
    scp = np.concatenate([a.reshape(nss, NQS, 2, 2, 64) for a in cat('o_cmp_s')], 0)[None]
    ssl = np.concatenate([a.reshape(nss, NQS, 2, 2, 64) for a in cat('o_sel_s')], 0)[None]
    swp = np.concatenate([a.reshape(nss, 512, 2, 2, 64) for a in cat('o_win_s')], 0)[None]
    scv = np.concatenate([a.reshape(nss, 3, 1024) for a in cat('o_conv_s')], 0)[None]
    ssm = np.concatenate([a.reshape(nss, 8, 64, 128) for a in cat('o_ssm_s')], 0)[None]
    return (y_p, y_s, ncp, nsl, nwp, ncv, nsm, scp, ssl, swp, scv, ssm)


def kernel(**inputs):
    inputs = {k: np.asarray(v) for k, v in inputs.items()}
    B, S = inputs['x_prompt'].shape[0], inputs['x_prompt'].shape[1]
    NBS = inputs['x_sample'].shape[0]
    npages = inputs['page_table'].shape[1]
    ncores = 8
    cfg = Cfg(nseq_p=B // ncores, seq=S, nseq_s=NBS // ncores, past=npages * 128,
              n_phys=inputs['cache_cmp'].shape[1])
    nc, P, hc = build(cfg)
    maps = make_in_maps(cfg, hc, inputs, ncores)
    res = run_bass_kernel_spmd(nc, maps, core_ids=list(range(ncores)))
    outs = assemble(cfg, res.results, ncores)
    return tuple(np.ascontiguousarray(o, dtype=np.float32) for o in outs)
```
